# Optimizing a Trainium2 kernel written in Bass

```python
import jax, jax.numpy as jnp
from jax import lax
import numpy as np

D_MODEL = 1024
BATCH = 2
SEQ = 8192
DEPTH = 1
DEC_BATCH = 8
DEC_SEQ = 64
PAST_LEN = 1024

CHUNK = 64
LEFT_CHUNKS = 8
A_WINDOW = LEFT_CHUNKS * CHUNK
H_A = 8
DH_A = 64
MAX_REL = 256
H_B = 8
DH_NOPE = 64
DH_ROPE = 32
DV_B = 64
D_C = 256
D_FF = 2816
CONV_W = 3
ROPE_BASE = 10000.0
EPS = 1e-6
Q_BLOCK = 128
A_SCALE = DH_A ** -0.5
MLA_SCALE = (DH_NOPE + DH_ROPE) ** -0.5
IN_SIZES = (H_A * DH_A, H_A * DH_A, H_A * DH_A, H_B * DH_NOPE, H_B * DH_ROPE, D_C, DH_ROPE, D_MODEL, D_MODEL)
IN_WIDTH = 3 * H_A * DH_A + H_B * (DH_NOPE + DH_ROPE) + D_C + DH_ROPE + 2 * D_MODEL

kernel_name = "hybrid_chunk_band_mla_convffn_step"


def rmsnorm(x, w):
    xf = x.astype(jnp.float32)
    y = xf * lax.rsqrt(jnp.mean(xf * xf, axis=-1, keepdims=True) + EPS)
    return (y * w.astype(jnp.float32)).astype(x.dtype)


def rope(x, pos):
    half = DH_ROPE // 2
    inv = ROPE_BASE ** (-jnp.arange(half, dtype=jnp.float32) / half)
    ang = pos.astype(jnp.float32)[:, None] * inv[None, :]
    shape = (ang.shape[0],) + (1,) * (x.ndim - 3) + (half,)
    c = jnp.cos(ang).reshape(shape)
    s = jnp.sin(ang).reshape(shape)
    xf = x.astype(jnp.float32)
    x1, x2 = xf[..., :half], xf[..., half:]
    return jnp.concatenate([x1 * c - x2 * s, x1 * s + x2 * c], axis=-1).astype(x.dtype)


def masked_softmax(s, mask):
    if mask is not None:
        s = jnp.where(mask, s, -1e30)
    return jax.nn.softmax(s, axis=-1)


def gather_rel_bias(table, d):
    return table[:, jnp.clip(d, -MAX_REL, MAX_REL) + MAX_REL].astype(jnp.float32)


def mixer_inputs(xn, pos, p):
    B, S = xn.shape[0], xn.shape[1]
    h = xn @ p["w_in"]
    offsets = np.cumsum(IN_SIZES)[:-1].tolist()
    qa, ka, va, qn, qr, ckv, kr, ga, gb = jnp.split(h, offsets, axis=-1)
    qa = qa.reshape(B, S, H_A, DH_A)
    ka = ka.reshape(B, S, H_A, DH_A)
    va = va.reshape(B, S, H_A, DH_A)
    qn = qn.reshape(B, S, H_B, DH_NOPE)
    qr = rope(qr.reshape(B, S, H_B, DH_ROPE), pos)
    ckv = rmsnorm(ckv, p["kv_norm"])
    kr = rope(kr, pos)
    return qa, ka, va, qn, qr, ckv, kr, ga, gb


def band_attention_prompt(qa, ka, va, table):
    B, S = qa.shape[0], qa.shape[1]
    nc = S // CHUNK
    qc = qa.reshape(B, nc, CHUNK, H_A, DH_A)
    pad = ((0, 0), (LEFT_CHUNKS, 0), (0, 0), (0, 0), (0, 0))
    kp = jnp.pad(ka.reshape(B, nc, CHUNK, H_A, DH_A), pad)
    vp = jnp.pad(va.reshape(B, nc, CHUNK, H_A, DH_A), pad)
    k_band = jnp.concatenate([kp[:, j:j + nc] for j in range(LEFT_CHUNKS + 1)], axis=2)
    v_band = jnp.concatenate([vp[:, j:j + nc] for j in range(LEFT_CHUNKS + 1)], axis=2)
    valid = (jnp.arange(nc)[:, None] - LEFT_CHUNKS + jnp.arange(LEFT_CHUNKS + 1)[None, :]) >= 0
    valid = jnp.repeat(valid, CHUNK, axis=1)
    d = A_WINDOW + jnp.arange(CHUNK)[:, None] - jnp.arange((LEFT_CHUNKS + 1) * CHUNK)[None, :]
    bias = gather_rel_bias(table, d)
    s = (jnp.einsum('bnqhd,bnkhd->bnhqk', qc, k_band) * A_SCALE).astype(jnp.float32) + bias[None, None]
    pr = masked_softmax(s, valid[None, :, None, None, :])
    o = jnp.einsum('bnhqk,bnkhd->bnqhd', pr.astype(va.dtype), v_band)
    return o.reshape(B, S, H_A * DH_A)


def band_attention_sample(qa, ka, va, cache_k, cache_v, table, past_len):
    B, T = qa.shape[0], qa.shape[1]
    W = cache_k.shape[1]
    k_all = jnp.concatenate([cache_k.astype(ka.dtype), ka], axis=1)
    v_all = jnp.concatenate([cache_v.astype(va.dtype), va], axis=1)
    q_pos = past_len + jnp.arange(T)
    k_pos = jnp.concatenate([past_len - W + jnp.arange(W), past_len + jnp.arange(T)])
    bias = gather_rel_bias(table, q_pos[:, None] - k_pos[None, :])
    s = (jnp.einsum('bqhd,bkhd->bhqk', qa, k_all) * A_SCALE).astype(jnp.float32) + bias[None]
    pr = masked_softmax(s, None)
    o = jnp.einsum('bhqk,bkhd->bqhd', pr.astype(va.dtype), v_all).reshape(B, T, H_A * DH_A)
    return o, k_all[:, -W:], v_all[:, -W:]


def mla_kv(ckv, p):
    k_nope = jnp.einsum('bsc,chd->bshd', ckv, p["w_uk"])
    v = jnp.einsum('bsc,chd->bshd', ckv, p["w_uv"])
    return k_nope, v


def mla_attend(qn, qr, k_nope, kr, v, mask):
    s = jnp.einsum('bqhd,bkhd->bhqk', qn, k_nope) + jnp.einsum('bqhd,bkd->bhqk', qr, kr)
    pr = masked_softmax((s * MLA_SCALE).astype(jnp.float32), mask)
    return jnp.einsum('bhqk,bkhd->bqhd', pr.astype(v.dtype), v)


def mla_prompt(qn, qr, ckv, kr, p):
    B, S = qn.shape[0], qn.shape[1]
    nb = S // Q_BLOCK
    k_nope, v = mla_kv(ckv, p)
    k_chunk = jnp.arange(S) // CHUNK
    qn_b = qn.reshape(B, nb, Q_BLOCK, H_B, DH_NOPE).transpose(1, 0, 2, 3, 4)
    qr_b = qr.reshape(B, nb, Q_BLOCK, H_B, DH_ROPE).transpose(1, 0, 2, 3, 4)

    def block(args):
        qn_i, qr_i, i = args
        q_chunk = (i * Q_BLOCK + jnp.arange(Q_BLOCK)) // CHUNK
        mask = k_chunk[None, :] <= q_chunk[:, None]
        return mla_attend(qn_i, qr_i, k_nope, kr, v, mask)

    o = lax.map(block, (qn_b, qr_b, jnp.arange(nb)))
    return o.transpose(1, 0, 2, 3, 4).reshape(B, S, H_B * DV_B)


def mla_sample(qn, qr, ckv, kr, cache_ckv, cache_kr, p):
    B, T = qn.shape[0], qn.shape[1]
    c_all = jnp.concatenate([cache_ckv.astype(ckv.dtype), ckv], axis=1)
    kr_all = jnp.concatenate([cache_kr.astype(kr.dtype), kr], axis=1)
    k_nope, v = mla_kv(c_all, p)
    o = mla_attend(qn, qr, k_nope, kr_all, v, None)
    return o.reshape(B, T, H_B * DV_B)


def merge_out(ya, yb, ga, gb, p):
    za = ya @ p["w_branch_a"]
    zb = yb @ p["w_branch_b"]
    return (jax.nn.sigmoid(ga) * za + jax.nn.sigmoid(gb) * zb) @ p["w_out"]


def conv_ffn(xn, left, p):
    T = xn.shape[1]
    g = xn @ p["w_ffn_gate"]
    u = xn @ p["w_ffn_up"]
    gp = jnp.concatenate([left.astype(g.dtype), g], axis=1)
    c = p["conv_b"] + p["conv_w"][0] * gp[:, 0:T]
    for j in range(1, CONV_W):
        c = c + p["conv_w"][j] * gp[:, j:j + T]
    h = jax.nn.gelu(c, approximate=True) * u
    return h @ p["w_ffn_down"], gp[:, -(CONV_W - 1):]


def prompt_layer(x, p):
    B, S = x.shape[0], x.shape[1]
    pos = jnp.arange(S, dtype=jnp.int32)
    xn = rmsnorm(x, p["norm_mix_pre"])
    qa, ka, va, qn, qr, ckv, kr, ga, gb = mixer_inputs(xn, pos, p)
    ya = band_attention_prompt(qa, ka, va, p["rel_bias_table"])
    yb = mla_prompt(qn, qr, ckv, kr, p)
    x = x + rmsnorm(merge_out(ya, yb, ga, gb, p), p["norm_mix_post"])
    left = jnp.zeros((B, CONV_W - 1, D_FF), x.dtype)
    f, conv_state = conv_ffn(rmsnorm(x, p["norm_ffn_pre"]), left, p)
    x = x + rmsnorm(f, p["norm_ffn_post"])
    keep = min(A_WINDOW, S)
    return x, ka[:, -keep:], va[:, -keep:], ckv, kr, conv_state


def sample_layer(x, cache_k, cache_v, cache_ckv, cache_kr, conv_state, p):
    T = x.shape[1]
    past_len = cache_ckv.shape[1]
    pos = past_len + jnp.arange(T, dtype=jnp.int32)
    xn = rmsnorm(x, p["norm_mix_pre"])
    qa, ka, va, qn, qr, ckv, kr, ga, gb = mixer_inputs(xn, pos, p)
    ya, new_k, new_v = band_attention_sample(qa, ka, va, cache_k, cache_v, p["rel_bias_table"], past_len)
    yb = mla_sample(qn, qr, ckv, kr, cache_ckv, cache_kr, p)
    x = x + rmsnorm(merge_out(ya, yb, ga, gb, p), p["norm_mix_post"])
    f, new_conv = conv_ffn(rmsnorm(x, p["norm_ffn_pre"]), conv_state, p)
    x = x + rmsnorm(f, p["norm_ffn_post"])
    return x, new_k, new_v, ckv, kr, new_conv


def setup_inputs(seed: int = 0) -> dict:
    key = jax.random.key(seed)
    ks = jax.random.split(key, 32)
    f32 = jnp.float32

    def nrm(k, shape, scale):
        return jax.random.normal(k, shape, f32) * scale

    a_cache = min(A_WINDOW, PAST_LEN)
    return {
        "x_prompt": nrm(ks[0], (BATCH, SEQ, D_MODEL), 1.0),
        "x_sample": nrm(ks[1], (DEC_BATCH, DEC_SEQ, D_MODEL), 1.0),
        "cache_a_k": nrm(ks[2], (DEPTH, DEC_BATCH, a_cache, H_A, DH_A), 1.0),
        "cache_a_v": nrm(ks[3], (DEPTH, DEC_BATCH, a_cache, H_A, DH_A), 1.0),
        "cache_mla_ckv": nrm(ks[4], (DEPTH, DEC_BATCH, PAST_LEN, D_C), 1.0),
        "cache_mla_krope": nrm(ks[5], (DEPTH, DEC_BATCH, PAST_LEN, DH_ROPE), 1.0),
        "state_ffn_conv": nrm(ks[6], (DEPTH, DEC_BATCH, CONV_W - 1, D_FF), 1.0),
        "norm_mix_pre": 1.0 + nrm(ks[7], (DEPTH, D_MODEL), 0.1),
        "norm_mix_post": 1.0 + nrm(ks[8], (DEPTH, D_MODEL), 0.1),
        "w_in": nrm(ks[9], (DEPTH, D_MODEL, IN_WIDTH), D_MODEL ** -0.5),
        "rel_bias_table": nrm(ks[10], (DEPTH, H_A, 2 * MAX_REL + 1), 0.5),
        "kv_norm": 1.0 + nrm(ks[11], (DEPTH, D_C), 0.1),
        "w_uk": nrm(ks[12], (DEPTH, D_C, H_B, DH_NOPE), D_C ** -0.5),
        "w_uv": nrm(ks[13], (DEPTH, D_C, H_B, DV_B), D_C ** -0.5),
        "w_branch_a": nrm(ks[14], (DEPTH, H_A * DH_A, D_MODEL), (H_A * DH_A) ** -0.5),
        "w_branch_b": nrm(ks[15], (DEPTH, H_B * DV_B, D_MODEL), (H_B * DV_B) ** -0.5),
        "w_out": nrm(ks[16], (DEPTH, D_MODEL, D_MODEL), D_MODEL ** -0.5),
        "norm_ffn_pre": 1.0 + nrm(ks[17], (DEPTH, D_MODEL), 0.1),
        "norm_ffn_post": 1.0 + nrm(ks[18], (DEPTH, D_MODEL), 0.1),
        "w_ffn_gate": nrm(ks[19], (DEPTH, D_MODEL, D_FF), D_MODEL ** -0.5),
        "w_ffn_up": nrm(ks[20], (DEPTH, D_MODEL, D_FF), D_MODEL ** -0.5),
        "conv_w": nrm(ks[21], (DEPTH, CONV_W, D_FF), CONV_W ** -0.5),
        "conv_b": nrm(ks[22], (DEPTH, D_FF), 0.01),
        "w_ffn_down": nrm(ks[23], (DEPTH, D_FF, D_MODEL), D_FF ** -0.5),
    }


def reference(x_prompt, x_sample, cache_a_k, cache_a_v, cache_mla_ckv, cache_mla_krope, state_ffn_conv,
              norm_mix_pre, norm_mix_post, w_in, rel_bias_table, kv_norm, w_uk, w_uv, w_branch_a, w_branch_b,
              w_out, norm_ffn_pre, norm_ffn_post, w_ffn_gate, w_ffn_up, conv_w, conv_b, w_ffn_down):
    xp = x_prompt
    xs = x_sample
    pk, pv, pc, pr, pcv = [], [], [], [], []
    sk, sv, sc, sr, scv = [], [], [], [], []
    for l in range(DEPTH):
        p = {
            "norm_mix_pre": norm_mix_pre[l], "norm_mix_post": norm_mix_post[l], "w_in": w_in[l],
            "rel_bias_table": rel_bias_table[l], "kv_norm": kv_norm[l], "w_uk": w_uk[l], "w_uv": w_uv[l],
            "w_branch_a": w_branch_a[l], "w_branch_b": w_branch_b[l], "w_out": w_out[l],
            "norm_ffn_pre": norm_ffn_pre[l], "norm_ffn_post": norm_ffn_post[l],
            "w_ffn_gate": w_ffn_gate[l], "w_ffn_up": w_ffn_up[l], "conv_w": conv_w[l], "conv_b": conv_b[l],
            "w_ffn_down": w_ffn_down[l],
        }
        xp, k1, v1, c1, r1, cv1 = prompt_layer(xp, p)
        pk.append(k1); pv.append(v1); pc.append(c1); pr.append(r1); pcv.append(cv1)
        xs, k2, v2, c2, r2, cv2 = sample_layer(xs, cache_a_k[l], cache_a_v[l], cache_mla_ckv[l],
                                               cache_mla_krope[l], state_ffn_conv[l], p)
        sk.append(k2); sv.append(v2); sc.append(c2); sr.append(r2); scv.append(cv2)
    return (xp, xs,
            jnp.stack(pk), jnp.stack(pv), jnp.stack(pc), jnp.stack(pr), jnp.stack(pcv),
            jnp.stack(sk), jnp.stack(sv), jnp.stack(sc), jnp.stack(sr), jnp.stack(scv))
```

```python
import numpy as np
import concourse.bass as bass
import concourse.mybir as mybir
from concourse.bass_utils import run_bass_kernel_spmd
from contextlib import ExitStack

F32 = mybir.dt.float32
BF16 = mybir.dt.bfloat16
AF = mybir.ActivationFunctionType
ALU = mybir.AluOpType

EPS = 1e-6
A_SCALE = 64 ** -0.5
MLA_SCALE = 96 ** -0.5
NEG = -1.0e30
NALL = 2752
NQ = 2240
NKV = 6144
DFF = 2816
NFC = 22


class _Res:
    __slots__ = ("lw", "rd")

    def __init__(self):
        self.lw = None
        self.rd = []


class KB:
    ENG = ("pe", "act", "dve", "pool", "sp")

    def __init__(self, nc, n_dma_sems=20):
        self.nc = nc
        self.ops = []
        self.res = {}
        self.n_dma_sems = n_dma_sems
        self.out_dmas = []
        self.bar_deps = set()
        self.bar_pending = set()
        self.since_bar = {}

    def _r(self, k):
        r = self.res.get(k)
        if r is None:
            r = self.res[k] = _Res()
        return r

    def barrier(self):
        d = set(self.bar_deps)
        for e, i in self.since_bar.items():
            d.add(i)
        for i, op in enumerate(self.ops):
            if op["is_dma"] and i >= getattr(self, "_bar_pos", 0):
                d.add(i)
        self.bar_deps = set()
        last = {}
        for i in d:
            op = self.ops[i]
            if op["is_dma"]:
                if i >= getattr(self, "_bar_pos", 0):
                    self.bar_deps.add(i)
            else:
                last[op["eng"]] = max(last.get(op["eng"], -1), i)
        self.bar_deps.update(last.values())
        for i in self.bar_deps:
            self.ops[i]["sig"] = True
        self._bar_pos = len(self.ops)
        self.since_bar = {}
        self.res = {}
        self.bar_pending = set(self.ENG)

    def _add(self, eng, fn, reads, writes, is_dma):
        idx = len(self.ops)
        deps = set()
        if eng in self.bar_pending:
            deps.update(self.bar_deps)
            self.bar_pending.discard(eng)
        for k in reads:
            r = self._r(k)
            if r.lw is not None:
                deps.add(r.lw)
        for k in writes:
            r = self._r(k)
            if r.lw is not None:
                deps.add(r.lw)
            deps.update(r.rd)
        if eng == "pe" and not is_dma:
            deps = {d for d in deps if not (self.ops[d]["eng"] == "pe" and not self.ops[d]["is_dma"])}
        self.ops.append(dict(eng=eng, fn=fn, deps=deps, is_dma=is_dma, sig=is_dma))
        for d in deps:
            self.ops[d]["sig"] = True
        for k in reads:
            self._r(k).rd.append(idx)
        for k in writes:
            r = self._r(k)
            r.lw = idx
            r.rd = []
        if not is_dma:
            self.since_bar[eng] = idx
        return idx

    def emit(self, eng, fn, reads=(), writes=()):
        return self._add(eng, fn, list(reads), list(writes), False)

    def dma(self, eng, out, in_, reads=(), writes=(), final=False, **kw):
        def fn(e, out=out, in_=in_, kw=kw):
            return e.dma_start(out=out, in_=in_, **kw)
        i = self._add(eng, fn, list(reads), list(writes), True)
        if final:
            self.out_dmas.append(i)
        return i

    def finalize(self):
        nc = self.nc
        ops = self.ops
        esem = {e: nc.alloc_semaphore(name="s_" + e) for e in self.ENG}
        dsem = {e: [nc.alloc_semaphore(name="d_%s_%d" % (e, i)) for i in range(self.n_dma_sems)]
                for e in ("sp", "act", "pool")}
        ecnt = {e: 0 for e in self.ENG}
        dnext = {e: 0 for e in dsem}
        dval = {e: [0] * self.n_dma_sems for e in dsem}
        tok = {}
        prevtok = {}
        for i, op in enumerate(ops):
            if not op["sig"]:
                continue
            e = op["eng"]
            if op["is_dma"]:
                s = dnext[e] % self.n_dma_sems
                dnext[e] += 1
                if dval[e][s] > 0:
                    prevtok[i] = (("d", e, s), dval[e][s])
                dval[e][s] += 16
                tok[i] = (("d", e, s), dval[e][s])
            else:
                ecnt[e] += 1
                tok[i] = (("e", e), ecnt[e])
        seen = {e: {} for e in self.ENG}

        def semof(key):
            return esem[key[1]] if key[0] == "e" else dsem[key[1]][key[2]]

        streams = {e: [] for e in self.ENG}
        for i, op in enumerate(ops):
            streams[op["eng"]].append(i)

        def run(ename, engine):
            sn = seen[ename]
            for i in streams[ename]:
                op = ops[i]
                need = {}
                for d in op["deps"]:
                    k, v = tok[d]
                    if need.get(k, 0) < v:
                        need[k] = v
                if i in prevtok:
                    k, v = prevtok[i]
                    if need.get(k, 0) < v:
                        need[k] = v
                for k, v in need.items():
                    if sn.get(k, 0) < v:
                        engine.wait_ge(semof(k), v)
                        sn[k] = v
                ins = op["fn"](engine)
                if op["sig"]:
                    k, v = tok[i]
                    ins.then_inc(semof(k), 16 if op["is_dma"] else 1)
            if ename == "sp":
                for i in self.out_dmas:
                    k, v = tok[i]
                    if sn.get(k, 0) < v:
                        engine.wait_ge(semof(k), v)
                        sn[k] = v

        with nc.Block() as block:
            @block.tensor
            def _(e):
                run("pe", e)

            @block.scalar
            def _(e):
                run("act", e)

            @block.vector
            def _(e):
                run("dve", e)

            @block.gpsimd
            def _(e):
                run("pool", e)

            @block.sync
            def _(e):
                run("sp", e)


class Ring:
    def __init__(self, aps, name):
        self.aps = aps
        self.name = name
        self.i = 0

    def next(self):
        j = self.i % len(self.aps)
        self.i += 1
        return self.aps[j], (self.name, j)


def build(stop=None):
    nc = bass.Bass("TRN2", target_bir_lowering=False)
    kb = KB(nc)

    def din(name, shape):
        return nc.dram_tensor(name, shape, F32, kind="ExternalInput").ap()

    def dout(name, shape):
        return nc.dram_tensor(name, shape, F32, kind="ExternalOutput").ap()

    x_all = din("x_all", [NALL, 1024])
    x_kv = din("x_kv", [NKV, 1024])
    cs_all = din("cs_all", [NALL, 32])
    cs_kv = din("cs_kv", [NKV, 32])
    qtab_d = din("qtab", [2, 32, NQ])
    kvbias_d = din("kvbias", [128, 4])
    bandbias_d = din("bandbias", [128, 2])
    ident_d = din("ident", [128, 128])
    ones_d = din("ones", [128, 64])
    cak = din("cak", [512, 512])
    cav = din("cav", [512, 512])
    cckv = din("cckv", [1024, 256])
    ckr = din("ckr", [1024, 32])
    sconv = din("sconv", [128, 2, NFC])
    w_c = din("w_c", [1024, 288])
    w_qq = din("w_qq", [8, 128, 8, 2, 96])
    w_qk = din("w_qk", [8, 128, 8, 2, 64])
    w_kv = din("w_kv", [1024, 1024])
    w_g = din("w_g", [1024, 2048])
    relb = din("relb", [8, 128, 5, 128])
    w_uk = din("w_uk", [256, 512])
    w_uv = din("w_uv", [256, 512])
    w_ba = din("w_ba", [512, 1024])
    w_bb = din("w_bb", [512, 1024])
    w_out = din("w_out", [1024, 1024])
    w_fgu = din("w_fgu", [NFC, 128, 8, 2, 128])
    w_fd = din("w_fd", [DFF, 1024])
    convw = din("convw", [128, 4, NFC])
    nrm = din("nrm", [128, 5, 1024])

    y_own = dout("y_own", [2048, 1024])
    y_smp = dout("y_smp", [64, 1024])
    o_kav = dout("o_kav", [512, 1024])
    o_ckv = dout("o_ckv", [2048, 256])
    o_kr = dout("o_kr", [2048, 32])
    o_conv = dout("o_conv", [128, 2, NFC])
    o_sk = dout("o_sk", [512, 512])
    o_sv = dout("o_sv", [512, 512])
    o_sckv = dout("o_sckv", [64, 256])
    o_skr = dout("o_skr", [64, 32])
    o_sconv = dout("o_sconv", [128, 2, NFC])
    x1_d = nc.dram_tensor("x1_scr", [NQ, 1024], F32).ap()
    wgu_bf = nc.dram_tensor("wgu_bf", [NFC, 128, 2048], BF16).ap()
    wd_bf = nc.dram_tensor("wd_bf", [NFC, 128, 1024], BF16).ap()

    ps = nc.alloc_psum_tensor("ps", [128, 4096], F32).ap()

    def bank(b, n=512, p0=0, p1=128, off=0):
        return ps[p0:p1, 512 * b + off:512 * b + off + n]

    def PK(*bs):
        return [("ps", b) for b in bs]

    es = ExitStack()

    def sb(name, shape, dt, stack=None):
        return (stack or es).enter_context(nc.sbuf_tensor("sb_" + name, shape, dt))[:]

    ident = sb("ident", [128, 128], F32)
    nrm_rep = sb("nrm_rep", [128, 2, 1024], F32)
    NSLOT = {}

    def load_nrm(slot, row):
        NSLOT[row] = slot
        kb.dma("sp", nrm_rep[:, slot, :], nrm[:, row, :], writes=["nrm_rep"])
    kvbias = sb("kvbias", [128, 4], F32)
    z1 = sb("z1", [1, 128], BF16)
    kb.emit("pool", lambda e: e.memset(z1, 0.0), writes=["z1"])
    bandbias = sb("bandbias", [128, 2], F32)
    xs = ExitStack()
    xnT = sb("xnT", [128, 8, NALL], BF16, xs)
    ybT = sb("ybT", [128, 4, NQ], BF16, xs)

    kb.dma("sp", ident, ident_d, writes=["ident"])
    kb.dma("sp", kvbias, kvbias_d, writes=["kvbias"])
    kb.dma("sp", bandbias, bandbias_d, writes=["bandbias"])
    load_nrm(0, 0)
    load_nrm(1, 4)

    def rstd_from_ss(st, n, inv_d, keys):
        kb.emit("dve", lambda e: e.tensor_scalar(st[0:n, 1:2], st[0:n, 0:1], inv_d, EPS, ALU.mult, ALU.add),
                reads=keys, writes=keys)
        kb.emit("act", lambda e: e.activation(st[0:n, 3:4], st[0:n, 1:2], AF.Sqrt), reads=keys, writes=keys)
        kb.emit("dve", lambda e: e.reciprocal(st[0:n, 2:3], st[0:n, 3:4]), reads=keys, writes=keys)

    def rstd_gen(st, n, inv_d, keys):
        kb.emit("dve", lambda e: e.tensor_scalar(st[0:n, 1:2], st[0:n, 0:1], inv_d, EPS, ALU.mult, ALU.add),
                reads=keys, writes=keys)
        yield
        kb.emit("act", lambda e: e.activation(st[0:n, 3:4], st[0:n, 1:2], AF.Sqrt), reads=keys, writes=keys)
        yield
        kb.emit("dve", lambda e: e.reciprocal(st[0:n, 2:3], st[0:n, 3:4]), reads=keys, writes=keys)
        yield

    def norm_transpose(xt, kx, n, widx, dst, dkeys, junk, kj, st, kst, b0=0):
        kb.emit("act", lambda e: e.activation(junk[0:n], xt[0:n], AF.Square, accum_out=st[0:n, 0:1]),
                reads=[kx], writes=[kj, kst])
        yield
        yield from rstd_gen(st, n, 1.0 / 1024, [kst])
        kb.emit("dve", lambda e: e.scalar_tensor_tensor(xt[0:n], xt[0:n], st[0:n, 2:3], nrm_rep[0:n, NSLOT[widx], :],
                                                        ALU.mult, ALU.mult),
                reads=[kx, kst, "nrm_rep"], writes=[kx])
        yield
        for kc in range(8):
            kb.emit("pe", lambda e, kc=kc: e.transpose(bank(b0, n, off=kc * 128) if kc < 4 else bank(b0 + 1, n, off=(kc - 4) * 128),
                                                      xt[0:n, kc * 128:(kc + 1) * 128], ident[0:n, 0:n]),
                    reads=[kx, "ident"], writes=PK(b0, b0 + 1))
        yield
        src = ps[:, 512 * b0:512 * b0 + 1024].rearrange("p (k t) -> p k t", k=8)[:, :, 0:n]
        kb.emit("act", lambda e: e.activation(dst, src, AF.Copy), reads=PK(b0, b0 + 1), writes=dkeys)
        yield

    def run_pipeline(gens, depth):
        active = []
        it = iter(gens)
        done = False
        while True:
            if not done and len(active) < depth:
                try:
                    active.append(next(it))
                except StopIteration:
                    done = True
            if not active:
                if done:
                    break
                continue
            for g in list(active):
                try:
                    next(g)
                except StopIteration:
                    active.remove(g)

    lat = ExitStack()
    ckvT_kv = sb("ckvT_kv", [128, 2, NKV], BF16, lat)
    ckvT_own = sb("ckvT_own", [128, 2, NQ], BF16, lat)
    ckvT_c = sb("ckvT_c", [128, 2, 1024], BF16, lat)
    Kp = sb("Kp", [96, NKV + 2048], BF16, lat)
    Ks = sb("Ks", [96, 1088], BF16, lat)
    pa = ExitStack()
    xring = Ring([sb("xr%d" % i, [128, 1024], F32, pa) for i in range(6)], "xr")
    jring = Ring([sb("jr%d" % i, [128, 1024], BF16, pa) for i in range(4)], "jr")
    sring = Ring([sb("sr%d" % i, [128, 4], F32, pa) for i in range(12)], "sr")
    ltring = Ring([sb("lt%d" % i, [128, 320], F32, pa) for i in range(5)], "lt")
    csring = Ring([sb("cs%d" % i, [128, 32], F32, pa) for i in range(5)], "cs")
    tring = Ring([sb("tt%d" % i, [128, 64], F32, pa) for i in range(5)], "tt")
    xtmpT = Ring([sb("xtT%d" % i, [128, 8, 128], BF16, pa) for i in range(5)], "xtT")
    wc = sb("wc", [128, 8, 288], BF16, pa)
    kb.dma("pool", wc, w_c.rearrange("(kc p) n -> p kc n", p=128), writes=["wc"])
    tiles_all = [(t * 128, 128) for t in range(21)] + [(2688, 64)]

    def latent(xT_fn, xkeys, n, cs_rows, dst_ckvT, dst_krT, dkeys, out_ckv=None, out_kr=None, par=0):
        mmb, trb = (2, 3) if par == 0 else (6, 7)
        for kc in range(8):
            kb.emit("pe", lambda e, kc=kc: e.matmul(bank(mmb, 288, 0, n), xT_fn(kc), wc[:, kc, :],
                                                    start=(kc == 0), stop=(kc == 7)),
                    reads=xkeys + ["wc"], writes=PK(mmb))
        lt, kl = ltring.next()
        st, kst = sring.next()
        junk, kj = jring.next()
        cs, kcs = csring.next()
        tt, ktt = tring.next()
        kb.dma("sp", cs[0:n], cs_rows, writes=[kcs])
        yield
        kb.emit("act", lambda e: e.activation(lt[0:n, 0:288], bank(mmb, 288, 0, n), AF.Copy), reads=PK(mmb), writes=[kl])
        yield
        kb.emit("act", lambda e: e.activation(junk[0:n, 0:256], lt[0:n, 0:256], AF.Square, accum_out=st[0:n, 0:1]),
                reads=[kl], writes=[kj, kst])
        x1, x2 = lt[0:n, 256:272], lt[0:n, 272:288]
        c, s = cs[0:n, 0:16], cs[0:n, 16:32]
        kb.emit("dve", lambda e: e.tensor_tensor(tt[0:n, 0:16], x1, c, ALU.mult), reads=[kl, kcs], writes=[ktt])
        kb.emit("dve", lambda e: e.tensor_tensor(tt[0:n, 16:32], x2, s, ALU.mult), reads=[kl, kcs], writes=[ktt])
        kb.emit("dve", lambda e: e.tensor_tensor(tt[0:n, 32:48], x1, s, ALU.mult), reads=[kl, kcs], writes=[ktt])
        kb.emit("dve", lambda e: e.tensor_tensor(tt[0:n, 48:64], x2, c, ALU.mult), reads=[kl, kcs], writes=[ktt])
        yield
        kb.emit("dve", lambda e: e.tensor_tensor(lt[0:n, 288:304], tt[0:n, 0:16], tt[0:n, 16:32], ALU.subtract),
                reads=[ktt], writes=[(kl, "kr")])
        kb.emit("dve", lambda e: e.tensor_tensor(lt[0:n, 304:320], tt[0:n, 32:48], tt[0:n, 48:64], ALU.add),
                reads=[ktt], writes=[(kl, "kr")])
        yield from rstd_gen(st, n, 1.0 / 256, [kst])
        kb.emit("dve", lambda e: e.scalar_tensor_tensor(lt[0:n, 0:256], lt[0:n, 0:256], st[0:n, 2:3],
                                                        nrm_rep[0:n, NSLOT[4], 0:256], ALU.mult, ALU.mult),
                reads=[kl, kst, "nrm_rep"], writes=[kl])
        yield
        if out_ckv is not None:
            kb.dma("sp", out_ckv, lt[0:n, 0:256], reads=[kl], final=True)
            kb.dma("sp", out_kr, lt[0:n, 288:320], reads=[kl, (kl, "kr")], final=True)
        for c2 in range(2):
            kb.emit("pe", lambda e, c2=c2: e.transpose(bank(trb, n, off=c2 * 128), lt[0:n, c2 * 128:(c2 + 1) * 128],
                                                      ident[0:n, 0:n]),
                    reads=[kl, "ident"], writes=PK(trb))
        kb.emit("pe", lambda e: e.transpose(bank(trb, n, 0, 96, off=256), lt[0:n, 224:320], ident[0:n, 0:n]),
                reads=[kl, (kl, "kr"), "ident"], writes=PK(trb))
        yield
        src = bank(trb, 256).rearrange("p (k t) -> p k t", k=2)[:, :, 0:n]
        kb.emit("act", lambda e: e.activation(dst_ckvT, src, AF.Copy), reads=PK(trb), writes=dkeys)
        kb.emit("dve", lambda e: e.tensor_copy(dst_krT, bank(trb, n, 64, 96, off=256)), reads=PK(trb), writes=dkeys)
        yield

    def own_tile(ti, t0, n):
        xt, kx = xring.next()
        kb.dma("sp", xt[0:n], x_all[t0:t0 + n, :], writes=[kx])
        junk, kj = jring.next()
        st, kst = sring.next()
        yield
        yield from norm_transpose(xt, kx, n, 0, xnT[:, :, t0:t0 + n], [("xnT", t0 // 128)], junk, kj, st, kst, b0=4 * (ti % 2))
        if t0 < 512:
            return
        q0 = t0 - 512
        is_own = 640 <= t0 < 2688
        is_smp = t0 == 2688
        oc = ok = None
        if is_own:
            oc, ok = o_ckv[t0 - 640:t0 - 640 + n, :], o_kr[t0 - 640:t0 - 640 + n, :]
        if is_smp:
            oc, ok = o_sckv[:, :], o_skr[:, :]
        if is_own:
            krdst = Kp[64:96, NKV + t0 - 640:NKV + t0 - 640 + n]
        elif is_smp:
            krdst = Ks[64:96, 1024:1088]
        else:
            krdst = Ks[64:96, 0:n]
        yield from latent(lambda kc: xnT[:, kc, t0:t0 + n], [("xnT", t0 // 128)], n, cs_all[t0:t0 + n, :],
                          ckvT_own[:, :, q0:q0 + n], krdst, [("lat_own", t0 // 128), "Ks_dump"], oc, ok, par=ti % 2)

    def kv_tile(t):
        xt, kx = xring.next()
        kb.dma("sp", xt, x_kv[t * 128:(t + 1) * 128, :], writes=[kx])
        junk, kj = jring.next()
        st, kst = sring.next()
        xT, kxT = xtmpT.next()
        yield
        yield from norm_transpose(xt, kx, 128, 0, xT, [kxT], junk, kj, st, kst, b0=4 * (t % 2))
        yield from latent(lambda kc: xT[:, kc, :], [kxT], 128, cs_kv[t * 128:(t + 1) * 128, :],
                          ckvT_kv[:, :, t * 128:(t + 1) * 128], Kp[64:96, t * 128:(t + 1) * 128], [("lat_kv", t)], par=t % 2)

    gens = [own_tile(ti, t0, n) for ti, (t0, n) in enumerate(tiles_all)] + [kv_tile(t) for t in range(NKV // 128)]
    run_pipeline(gens, 4)
    for t in range(8):
        xt, kx = xring.next()
        kb.dma("sp", xt[:, 0:256], cckv[t * 128:(t + 1) * 128, :], writes=[kx])
        kb.emit("pool", lambda e, xt=xt: e.memset(xt[:, 256:320], 0.0), writes=[kx])
        kb.dma("sp", xt[:, 320:352], ckr[t * 128:(t + 1) * 128, :], writes=[kx])
        for c2 in range(2):
            kb.emit("pe", lambda e, c2=c2, xt=xt: e.transpose(bank(3, 128, off=c2 * 128), xt[:, c2 * 128:(c2 + 1) * 128], ident),
                    reads=[kx, "ident"], writes=PK(3))
        kb.emit("pe", lambda e, xt=xt: e.transpose(bank(3, 128, 0, 96, off=256), xt[:, 256:352], ident),
                reads=[kx, "ident"], writes=PK(3))
        src = bank(3, 256).rearrange("p (k t) -> p k t", k=2)
        kb.emit("act", lambda e, t=t, src=src: e.activation(ckvT_c[:, :, t * 128:(t + 1) * 128], src, AF.Copy),
                reads=PK(3), writes=[("lat_c", t)])
        kb.emit("dve", lambda e, t=t: e.tensor_copy(Ks[64:96, t * 128:(t + 1) * 128], bank(3, 128, 64, 96, off=256)),
                reads=PK(3), writes=[("lat_c", t), "Ks_dump"])
    kb.barrier()
    pa.close()
    if stop == "B":
        kb.finalize()
        return nc

    ACCC = [0]

    def attend(pc, q_ap, qrows, ncol, ktiles, scale, out_dst, par, PTring, recring, okeys, qkeys, relb_t=None, hook=None, defer=False, relb_key="relb"):
        nt = len(ktiles)
        if ncol <= 512:
            nbuf, bstep = 4, 1
            ACCC[0] += 1
            accb = 6 + ACCC[0] % 2
            akeys = PK(accb)
        else:
            nbuf, bstep = 2, 2
            accb = 6
            akeys = PK(6, 7)
        LA = nbuf - 1
        G = max(1, 512 // ncol) if ncol <= 128 else 1
        batches = []
        for ti, kt in enumerate(ktiles):
            ok = False
            if batches and len(batches[-1]) < G and G > 1:
                p = ktiles[batches[-1][-1]]
                ok = (p["nk"] == 128 and kt["nk"] == 128 and p.get("c0", 0) == 0 and kt.get("c0", 0) == 0
                      and (p.get("bias") is kt.get("bias"))
                      and (relb_t is None or kt["rb"] == p["rb"] + 1))
            if ok:
                batches[-1].append(ti)
            else:
                batches.append([ti])
        nbt = len(batches)
        st8 = {}

        def segs_of(c0):
            segs = []
            a = c0
            while a < ncol:
                b = min(ncol, (a // 512 + 1) * 512)
                segs.append((a, b))
                a = b
            return segs

        def qk(bi):
            sb_ = 2 + bstep * (bi % nbuf)
            bks = PK(*range(sb_, sb_ + bstep))
            for s_i, ti in enumerate(batches[bi]):
                kt = ktiles[ti]
                nk = kt["nk"]
                c0 = kt.get("c0", 0)
                o_ = 512 * sb_ + s_i * ncol
                for (a, b) in segs_of(c0):
                    kb.emit("pe", lambda e, a=a, b=b, kt=kt, nk=nk, o_=o_: e.matmul(
                        ps[0:nk, o_ + a:o_ + b], kt["K"], q_ap[:, a:b], start=True, stop=True),
                        reads=kt["kkeys"] + qkeys, writes=bks)

        def ex(bi):
            bt = batches[bi]
            nb = len(bt)
            kt = ktiles[bt[0]]
            sb_ = 2 + bstep * (bi % nbuf)
            nk = kt["nk"]
            c0 = kt.get("c0", 0)
            PT, kpt = PTring.next()
            st8[bi] = (PT, kpt)
            w = (nb - 1) * ncol + ncol
            src = ps[0:nk, 512 * sb_ + c0:512 * sb_ + w]
            rkeys = PK(*range(sb_, sb_ + bstep))
            if relb_t is not None:
                tmp, ktmp = pc["tmpring"].next()
                rb = kt["rb"]
                if nb == 1:
                    i0, i1, o0_ = src, relb_t[0:nk, rb, c0:ncol], tmp[0:nk, c0:ncol]
                else:
                    i0 = src.rearrange("p (b c) -> p b c", b=nb)
                    i1 = relb_t[0:nk, rb:rb + nb, 0:ncol]
                    o0_ = tmp[0:nk, 0:w].rearrange("p (b c) -> p b c", b=nb)
                kb.emit("dve", lambda e, i0=i0, i1=i1, o0_=o0_: e.scalar_tensor_tensor(o0_, i0, scale, i1, ALU.mult, ALU.add),
                        reads=rkeys + [relb_key], writes=[ktmp])
                src2, rk2, sc = tmp[0:nk, c0:w], [ktmp], 1.0
            else:
                src2, rk2, sc = src, rkeys, scale
            bias = kt.get("bias")
            if bias is not None:
                kb.emit("act", lambda e, PT=PT, src2=src2, bias=bias, nk=nk, c0=c0, sc=sc, w=w: e.activation(
                    PT[0:nk, c0:w], src2, AF.Exp, bias=bias, scale=sc),
                    reads=rk2 + ["kvbias", "bandbias"], writes=[kpt])
            else:
                kb.emit("act", lambda e, PT=PT, src2=src2, nk=nk, c0=c0, sc=sc, w=w: e.activation(
                    PT[0:nk, c0:w], src2, AF.Exp, scale=sc), reads=rk2, writes=[kpt])

        def pv(bi):
            PT, kpt = st8.pop(bi)
            for s_i, ti in enumerate(batches[bi]):
                kt = ktiles[ti]
                nk = kt["nk"]
                c0 = kt.get("c0", 0)
                po = s_i * ncol
                segs = segs_of(c0)
                pieces = []
                half = kt.get("half")
                if half is not None:
                    lo, hi = (0, 64) if half == "lo" else (64, 128)
                    pieces.append((c0, c0 + 64, lo, hi))
                    for (a, b) in segs_of(c0 + 64):
                        pieces.append((a, b, 0, nk))
                else:
                    for (a, b) in segs:
                        pieces.append((a, b, 0, nk))
                half2 = kt.get("half2")
                if half2 is not None:
                    newp = []
                    h2a, h2b, lo2, hi2 = half2
                    for (a, b, lo, hi) in pieces:
                        if b <= h2a or a >= h2b:
                            newp.append((a, b, lo, hi))
                        else:
                            if a < h2a:
                                newp.append((a, h2a, lo, hi))
                            newp.append((max(a, h2a), min(b, h2b), lo2, hi2))
                            if b > h2b:
                                newp.append((h2b, b, lo, hi))
                    pieces = newp
                st_flag = (ti == 0)
                if ti == 0:
                    assert half is None and half2 is None and c0 == 0
                for pi, (a, b, lo, hi) in enumerate(pieces):
                    last_in_bank = False
                    kb.emit("pe", lambda e, a=a, b=b, lo=lo, hi=hi, kt=kt, PT=PT, st_flag=st_flag, lb=last_in_bank, po=po: e.matmul(
                        ps[:, 512 * accb + a:512 * accb + b], kt["V"][lo:hi, :], PT[lo:hi, po + a:po + b], start=st_flag, stop=lb),
                        reads=[kpt] + kt["vkeys"], writes=akeys)

        for bi in range(min(LA, nbt)):
            qk(bi)
        for bi in range(nbt):
            if bi + LA < nbt:
                qk(bi + LA)
            ex(bi)
            pv(bi)
            if hook is not None:
                hook(batches[bi][-1])
        for (a, b) in segs_of(0):
            kb.emit("pe", lambda e, a=a, b=b: e.matmul(ps[:, 512 * accb + a:512 * accb + b], z1[0:1, 0:128], q_ap[0:1, a:b],
                                                       start=False, stop=True),
                    reads=qkeys + ["z1"], writes=akeys)

        def fin():
            rec, krec = recring.next()
            (o0, o1), (d0, d1) = ((0, 64), (64, 128)) if par == 0 else ((64, 128), (0, 64))
            A0 = 512 * accb
            kb.emit("dve", lambda e: e.tensor_scalar(rec[d0:d1, 0:ncol], ps[d0:d1, A0:A0 + ncol], 1e-30, None, ALU.max),
                    reads=akeys, writes=[krec])
            kb.emit("dve", lambda e: e.reciprocal(rec[d0:d1, 0:ncol], rec[d0:d1, 0:ncol]), reads=[krec], writes=[krec])
            kb.emit("dve", lambda e: e.tensor_tensor(out_dst, ps[o0:o1, A0:A0 + ncol], rec[d0:d1, 0:ncol], ALU.mult),
                    reads=akeys + [krec], writes=okeys)
        if defer:
            return fin
        fin()

    pc_ = ExitStack()
    wuk = sb("wuk", [128, 2, 512], BF16, pc_)
    wuv = sb("wuv", [128, 2, 512], BF16, pc_)
    qtr = Ring([sb("qtab%d" % i, [96, 2, 512], F32, pc_) for i in range(2)], "qtab")
    Vp = sb("Vp", [128, 64, 128], BF16, pc_)
    Vs = sb("Vs", [128, 9, 128], BF16, pc_)
    qTs = [sb("qT%d" % i, [96, NQ], BF16, pc_) for i in range(2)]
    wqr = Ring([sb("wq%d" % i, [128, 8, 2, 96], BF16, pc_) for i in range(2)], "wq")
    PTr = Ring([sb("PT%d" % i, [128, 1024], BF16, pc_) for i in range(4)], "PT")
    recr = Ring([sb("rec%d" % i, [128, 1024], F32, pc_) for i in range(1)], "rec")
    rt = Ring([sb("rt%d" % i, [96, 512], F32, pc_) for i in range(4)], "rt")
    ones_sb = sb("ones_sb", [128, 64], BF16, pc_)
    kb.dma("pool", wuk, w_uk.rearrange("(c p) n -> p c n", p=128), writes=["wuk"])
    kb.dma("pool", wuv, w_uv.rearrange("(c p) n -> p c n", p=128), writes=["wuv"])
    kb.dma("pool", ones_sb, ones_d, writes=["ones"])
    pc = {}

    qgroups = [(640 - 512, 1024), (640 - 512 + 1024, 1024), (0, 128)]
    kvb = [kvbias[:, v:v + 1] for v in range(3)]

    def mla_prep(h):
        par = h % 2
        vo = 0 if par == 0 else 64
        on = 64 if par == 0 else 0
        qT = qTs[h % 2]
        wq, kwq = wqr.next()
        th = [(-1, lambda: kb.dma("pool", wq, w_qq[h], writes=[kwq]))]

        def kexp(dst, srcT, ncols, dkey):
            for c in range(2):
                kb.emit("pe", lambda e, c=c: e.matmul(bank(0, ncols, 0, 64), wuk[:, c, h * 64:(h + 1) * 64], srcT(c),
                                                      start=(c == 0), stop=(c == 1)),
                        reads=["wuk"], writes=PK(0))
            kb.emit("dve", lambda e: e.tensor_copy(dst, bank(0, ncols, 0, 64)), reads=PK(0), writes=[dkey])

        def vexp(dst3, srcT, nt_, nk, dkey, ones_dst):
            for j in range(nt_):
                for c in range(2):
                    kb.emit("pe", lambda e, j=j, c=c: e.matmul(bank(1, 64, 0, nk, off=j * 64), srcT(c, j),
                                                              wuv[:, c, h * 64:(h + 1) * 64], start=(c == 0), stop=(c == 1)),
                            reads=["wuv"], writes=PK(1))
            src = bank(1, nt_ * 64, 0, nk).rearrange("p (t d) -> p t d", d=64)
            kb.emit("dve", lambda e: e.tensor_copy(dst3, src), reads=PK(1), writes=[dkey])

        def ones_fill(V3, t0_, nt_, dkey):
            for t in range(t0_, t0_ + nt_):
                kb.emit("pool", lambda e, t=t: e.tensor_copy(V3[:, t, on:on + 64], ones_sb), reads=["ones"], writes=[dkey])

        for g in range(12):
            th.append((4 * g + 3, lambda g=g: kexp(Kp[0:64, g * 512:(g + 1) * 512],
                                                   lambda c: ckvT_kv[:, c, g * 512:(g + 1) * 512], 512, ("Kp", g))))
        for g in range(4):
            th.append((48 + 4 * g + 3, lambda g=g: kexp(Kp[0:64, NKV + g * 512:NKV + (g + 1) * 512],
                                                        lambda c: ckvT_own[:, c, 128 + g * 512:128 + (g + 1) * 512], 512, ("Kp", 12 + g))))
        for g in range(2):
            th.append((-1, lambda g=g: kexp(Ks[0:64, g * 512:(g + 1) * 512], lambda c: ckvT_c[:, c, g * 512:(g + 1) * 512], 512, ("Ks", g))))
        th.append((-1, lambda: kexp(Ks[0:64, 1024:1088], lambda c: ckvT_own[:, c, 2176:2240], 64, ("Ks", 2))))
        for g in range(6):
            def vth(g=g):
                ones_fill(Vp, g * 8, 8, ("Vp", g))
                vexp(Vp[:, g * 8:(g + 1) * 8, vo:vo + 64], lambda c, j: ckvT_kv[:, c, (g * 8 + j) * 128:(g * 8 + j + 1) * 128], 8, 128, ("Vp", g), None)
            th.append((8 * g + 7, vth))
        for g in range(2):
            def vth2(g=g):
                ones_fill(Vp, 48 + g * 8, 8, ("Vp", 6 + g))
                vexp(Vp[:, 48 + g * 8:48 + (g + 1) * 8, vo:vo + 64],
                     lambda c, j: ckvT_own[:, c, 128 + (g * 8 + j) * 128:128 + (g * 8 + j + 1) * 128], 8, 128, ("Vp", 6 + g), None)
            th.append((48 + 8 * g + 7, vth2))

        def vs_th():
            ones_fill(Vs, 0, 9, "Vs")
            vexp(Vs[:, 0:8, vo:vo + 64], lambda c, j: ckvT_c[:, c, j * 128:(j + 1) * 128], 8, 128, "Vs", None)
            vexp(Vs[0:64, 8:9, vo:vo + 64], lambda c, j: ckvT_own[:, c, 2176:2240], 1, 64, "Vs", None)
        th.append((-1, vs_th))

        def qproj(a, n):
            for v in range(2):
                for kc in range(8):
                    kb.emit("pe", lambda e, v=v, kc=kc: e.matmul(bank(v, n, 0, 96), wq[:, kc, v, :],
                                                                 xnT[:, kc, 512 + a:512 + a + n],
                                                                 start=(kc == 0), stop=(kc == 7)),
                            reads=[kwq] + [("xnT", tt_) for tt_ in range((512 + a) // 128, (512 + a + n + 127) // 128)],
                            writes=PK(v))
            kb.emit("dve", lambda e: e.tensor_copy(qT[0:64, a:a + n], bank(0, n, 0, 64)),
                    reads=PK(0), writes=[("qT", h % 2, a // 512)])
            r1, k1 = rt.next()
            r2, k2 = rt.next()
            qtab, kqt = qtr.next()
            kb.dma("sp", qtab[64:96, 0, 0:n], qtab_d[0][:, a:a + n], writes=[kqt])
            kb.dma("sp", qtab[64:96, 1, 0:n], qtab_d[1][:, a:a + n], writes=[kqt])
            kb.emit("dve", lambda e: e.tensor_tensor(r1[64:96, 0:n], bank(0, n, 64, 96), qtab[64:96, 0, 0:n], ALU.mult),
                    reads=PK(0) + [kqt], writes=[k1])
            kb.emit("dve", lambda e: e.tensor_tensor(r2[64:96, 0:n], bank(1, n, 64, 96), qtab[64:96, 1, 0:n], ALU.mult),
                    reads=PK(1) + [kqt], writes=[k2])
            kb.emit("dve", lambda e: e.tensor_tensor(qT[64:96, a:a + n], r1[64:96, 0:n], r2[64:96, 0:n], ALU.add),
                    reads=[k1, k2], writes=[("qT", h % 2, a // 512)])
        for (a, n) in [(0, 512), (512, 512), (1024, 512), (1536, 512), (2048, 192)]:
            th.append((-1, lambda a=a, n=n: qproj(a, n)))
        return th

    def mla_attn(h, nxt):
        par = h % 2
        vo = 0 if par == 0 else 64
        qT = qTs[h % 2]
        qk = [("qT", h % 2, i) for i in range(5)]
        kvt = [dict(K=Kp[0:96, t * 128:(t + 1) * 128], V=Vp[:, t, :], nk=128, bias=kvb[t // 16],
                    kkeys=[("Kp", t // 4)], vkeys=[("Vp", t // 8)]) for t in range(48)]

        def own_t(t, c0, half):
            return dict(K=Kp[0:96, NKV + t * 128:NKV + (t + 1) * 128], V=Vp[:, 48 + t, :], nk=128, bias=None, c0=c0, half=half,
                        kkeys=[("Kp", 12 + t // 4)], vkeys=[("Vp", 6 + t // 8)])
        attend(pc, qT[0:96, 0:128], 96, 128, kvt, MLA_SCALE, ybT[vo:vo + 64, h // 2, 0:128], par, PTr, recr,
               [("ybT", h, 2)], qk)
        kts = [dict(K=Ks[0:96, t * 128:(t + 1) * 128], V=Vs[:, t, :], nk=128, bias=None,
                    kkeys=[("Ks", t // 4)], vkeys=["Vs"]) for t in range(8)]
        kts.append(dict(K=Ks[0:96, 1024:1088], V=Vs[0:64, 8, :], nk=64, bias=None, kkeys=[("Ks", 2)], vkeys=["Vs"]))
        attend(pc, qT[0:96, 2176:2240], 96, 64, kts, MLA_SCALE, ybT[vo:vo + 64, h // 2, 2176:2240], par, PTr, recr,
               [("ybT", h, 3)], qk)
        kt0 = kvt + [own_t(t, 128 * t, "lo") for t in range(8)]
        attend(pc, qT[0:96, 128:1152], 96, 1024, kt0, MLA_SCALE, ybT[vo:vo + 64, h // 2, 128:1152], par, PTr, recr,
               [("ybT", h, 0)], qk)
        kt1 = kvt + [own_t(t, 0, None) for t in range(8)] + [own_t(8 + t, 128 * t, "lo") for t in range(8)]
        pend = sorted(nxt, key=lambda x: x[0])

        def hook(ti):
            k = 0
            while pend and pend[0][0] <= ti and k < 2:
                pend.pop(0)[1]()
                k += 1
        attend(pc, qT[0:96, 1152:2176], 96, 1024, kt1, MLA_SCALE, ybT[vo:vo + 64, h // 2, 1152:2176], par, PTr, recr,
               [("ybT", h, 1)], qk, hook=hook)
        while pend:
            pend.pop(0)[1]()

    for (_, fn_) in mla_prep(0):
        fn_()
    for h_ in range(8):
        mla_attn(h_, mla_prep(h_ + 1) if h_ < 7 else [])
    kb.barrier()
    pc_.close()
    lat.close()
    if stop == "C":
        kb.finalize()
        return nc
    yas = ExitStack()
    yaT = sb("yaT", [128, 4, NQ], BF16, yas)

    pd = ExitStack()
    wab = Ring([sb("wab%d" % i, [128, 8, 2, 64], BF16, pd) for i in range(2)], "wab")
    wkv = sb("wkv", [128, 8, 1024], BF16, pd)
    va_all = sb("va_all", [128, 22, 512], BF16, pd)
    cav_sb = sb("cav_sb", [128, 4, 512], BF16, pd)
    kaTc = sb("kaTc", [64, 8, 512], BF16, pd)
    qaTs = [sb("qaT%d" % i, [64, NALL], BF16, pd) for i in range(2)]
    kaTs = [sb("kaT%d" % i, [64, NALL], BF16, pd) for i in range(2)]
    Vbs = [sb("Vb%d" % i, [128, 22, 128], BF16, pd) for i in range(2)]
    Vbcs = [sb("Vbc%d" % i, [128, 4, 128], BF16, pd) for i in range(2)]
    relbs = [sb("relb_t%d" % i, [128, 5, 128], F32, pd) for i in range(2)]
    PTb = Ring([sb("PTb%d" % i, [128, 512], BF16, pd) for i in range(4)], "PTb")
    recb = Ring([sb("recb%d" % i, [128, 128], F32, pd) for i in range(3)], "recb")
    tmpr = Ring([sb("tmpb%d" % i, [128, 512], F32, pd) for i in range(4)], "tmpb")
    stg = Ring([sb("stg%d" % i, [128, 1024], F32, pd) for i in range(1)], "stg")
    ones_b = sb("ones_b", [128, 64], BF16, pd)
    kb.dma("pool", ones_b, ones_d, writes=["ones"])
    cvr = Ring([sb("cv%d" % i, [128, 2048], BF16, pd) for i in range(2)], "cv")

    def convert_ffn(fcs):
        for fc in fcs:
            cv, kcv = cvr.next()
            kb.dma("pool", cv, w_fgu[fc].rearrange("p a b c -> p (a b c)"), writes=[kcv])
            kb.dma("pool", wgu_bf[fc], cv, reads=[kcv])
            cv, kcv = cvr.next()
            kb.dma("pool", cv[:, 0:1024], w_fd[fc * 128:(fc + 1) * 128, :], writes=[kcv])
            kb.dma("pool", wd_bf[fc], cv[:, 0:1024], reads=[kcv])

    kb.dma("pool", wkv, w_kv.rearrange("(kc p) n -> p kc n", p=128), writes=["wkv"])
    kb.dma("pool", cav_sb, cav.rearrange("(t p) n -> p t n", p=128), writes=["cav_sb"])
    pcb = {"tmpring": tmpr}
    kb.emit("pool", lambda e: e.memset(va_all[:, 21, :], 0.0), writes=[("va_all", 21)])
    for ti, (t0, n) in enumerate(tiles_all):
        for hf in range(2):
            for kc in range(8):
                kb.emit("pe", lambda e, hf=hf, kc=kc, t0=t0, n=n: e.matmul(bank(hf, 512, 0, n), xnT[:, kc, t0:t0 + n],
                                                                          wkv[:, kc, hf * 512:(hf + 1) * 512],
                                                                          start=(kc == 0), stop=(kc == 7)),
                        reads=["wkv", ("xnT", ti)], writes=PK(hf))
        kb.emit("act", lambda e, ti=ti, n=n: e.activation(va_all[0:n, ti, :], bank(1, 512, 0, n), AF.Copy),
                reads=PK(1), writes=[("va_all", ti)])
        if 2176 <= t0 < 2688 or t0 == 2688:
            s_, ks_ = stg.next()
            kb.emit("dve", lambda e, s_=s_, n=n: e.tensor_copy(s_[0:n, :], ps[0:n, 0:1024]), reads=PK(0, 1), writes=[ks_])
            if t0 < 2688:
                kb.dma("sp", o_kav[t0 - 2176:t0 - 2176 + n, :], s_[0:n, :], reads=[ks_], final=True)
            else:
                kb.dma("sp", o_sk[448:512, :], s_[0:64, 0:512], reads=[ks_], final=True)
                kb.dma("sp", o_sv[448:512, :], s_[0:64, 512:1024], reads=[ks_], final=True)
    kb.dma("sp", o_sk[0:448, :], cak[64:512, :], final=True)
    kb.dma("sp", o_sv[0:448, :], cav[64:512, :], final=True)
    for t in range(4):
        s_, ks_ = stg.next()
        kb.dma("sp", s_[:, 0:512], cak[t * 128:(t + 1) * 128, :], writes=[ks_])
        for hh in range(8):
            kb.emit("pe", lambda e, hh=hh, s_=s_: e.transpose(ps[0:64, hh * 128:(hh + 1) * 128], s_[:, hh * 64:(hh + 1) * 64], ident),
                    reads=[ks_, "ident"], writes=PK(0, 1))
        src = ps[0:64, 0:1024].rearrange("p (k t) -> p k t", k=8)
        kb.emit("act", lambda e, t=t, src=src: e.activation(kaTc[:, :, t * 128:(t + 1) * 128], src, AF.Copy),
                reads=PK(0, 1), writes=["kaTc"])

    bb0 = bandbias[:, 0:1]

    def band_prep(h):
        par = h % 2
        vo = 0 if par == 0 else 64
        on = 64 if par == 0 else 0
        qaT, kaT, Vb, Vbc, relb_t = qaTs[par], kaTs[par], Vbs[par], Vbcs[par], relbs[par]
        w2, kw2 = wab.next()
        th = []

        def loads():
            kb.dma("pool", w2, w_qk[h], writes=[kw2])
            kb.dma("sp", relb_t, relb[h], writes=[("relb", par)])
        th.append(loads)

        def proj(a, n):
            for v in range(2):
                for kc in range(8):
                    kb.emit("pe", lambda e, v=v, kc=kc: e.matmul(bank(v, n, 0, 64), w2[:, kc, v, :], xnT[:, kc, a:a + n],
                                                                 start=(kc == 0), stop=(kc == 7)),
                            reads=[kw2] + [("xnT", tt_) for tt_ in range(a // 128, (a + n + 127) // 128)], writes=PK(v))
            kb.emit("act", lambda e: e.activation(qaT[:, a:a + n], bank(0, n, 0, 64), AF.Copy), reads=PK(0), writes=[("qaT", par)])
            kb.emit("dve", lambda e: e.tensor_copy(kaT[:, a:a + n], bank(1, n, 0, 64)), reads=PK(1), writes=[("kaT", par)])
        for (a, n) in [(0, 512), (512, 512), (1024, 512), (1536, 512), (2048, 512), (2560, 192)]:
            th.append(lambda a=a, n=n: proj(a, n))

        def vbuild():
            kb.emit("pool", lambda e: e.tensor_copy(Vb[:, :, vo:vo + 64], va_all[:, :, h * 64:(h + 1) * 64]),
                    reads=[("va_all", i) for i in range(22)], writes=[("Vb", par)])
            for t in range(22):
                kb.emit("pool", lambda e, t=t: e.tensor_copy(Vb[:, t, on:on + 64], ones_b), reads=["ones"], writes=[("Vb", par)])
            kb.emit("pool", lambda e: e.tensor_copy(Vbc[:, :, vo:vo + 64], cav_sb[:, :, h * 64:(h + 1) * 64]), reads=["cav_sb"], writes=[("Vbc", par)])
            for t in range(4):
                kb.emit("pool", lambda e, t=t: e.tensor_copy(Vbc[:, t, on:on + 64], ones_b), reads=["ones"], writes=[("Vbc", par)])
        th.append(vbuild)
        return th

    def band_attn(h, nxt):
        par = h % 2
        vo = 0 if par == 0 else 64
        qaT, kaT, Vb, Vbc, relb_t = qaTs[par], kaTs[par], Vbs[par], Vbcs[par], relbs[par]
        pend = list(nxt)
        for m in range(17):
            kts = []
            for r in (1, 2, 3, 4, 0):
                t = m + r
                d = dict(K=kaT[:, t * 128:(t + 1) * 128], V=Vb[:, t, :], nk=128, rb=r,
                         bias=(bb0 if t < 5 else None), kkeys=[("kaT", par)], vkeys=[("Vb", par)])
                if r == 0:
                    d["half2"] = (64, 128, 64, 128)
                if r == 4:
                    d["half2"] = (0, 64, 0, 64)
                kts.append(d)
            fin = attend(pcb, qaT[:, 512 + 128 * m:640 + 128 * m], 64, 128, kts, A_SCALE,
                         yaT[vo:vo + 64, h // 2, 128 * m:128 * m + 128], par, PTb, recb, [("yaT", h, m)], [("qaT", par)],
                         relb_t=relb_t, relb_key=("relb", par), defer=True)
            if prev[0] is not None:
                prev[0]()
            prev[0] = fin
            if pend and m >= 2 and m % 2 == 0:
                pend.pop(0)()
        kts = [dict(K=kaTc[:, h, t * 128:(t + 1) * 128], V=Vbc[:, t, :], nk=128, rb=t, bias=None, kkeys=["kaTc"], vkeys=[("Vbc", par)])
               for t in range(4)]
        kts.append(dict(K=kaT[:, 2688:2752], V=Vb[0:64, 21, :], nk=64, rb=4, bias=None, kkeys=[("kaT", par)], vkeys=[("Vb", par)]))
        fin = attend(pcb, qaT[:, 2688:2752], 64, 64, kts, A_SCALE, yaT[vo:vo + 64, h // 2, 2176:2240], par, PTb, recb,
                     [("yaT", h, 17)], [("qaT", par)], relb_t=relb_t, relb_key=("relb", par), defer=True)
        prev[0]()
        prev[0] = fin
        while pend:
            pend.pop(0)()
        convert_ffn(range(3 * h, min(NFC, 3 * h + 3)))

    prev = [None]
    for fn_ in band_prep(0):
        fn_()
    for h_ in range(8):
        band_attn(h_, band_prep(h_ + 1) if h_ < 7 else [])
    prev[0]()
    kb.barrier()
    pd.close()
    if stop == "D":
        kb.finalize()
        return nc
    load_nrm(0, 1)

    pe_ = ExitStack()
    wa = sb("wa", [128, 4, 1024], BF16, pe_)
    wb = sb("wb", [128, 4, 1024], BF16, pe_)
    wo = sb("wo", [128, 8, 1024], BF16, pe_)
    wg = sb("wg", [128, 8, 2048], BF16, pe_)
    mT = sb("mT", [128, 8, 512], BF16, pe_)
    sgr = Ring([sb("sg%d" % i, [128, 512], F32, pe_) for i in range(4)], "sg")
    xr2 = Ring([sb("x2r%d" % i, [128, 1024], F32, pe_) for i in range(2)], "x2r")
    mr = Ring([sb("mr%d" % i, [128, 1024], F32, pe_) for i in range(2)], "mr")
    jr2 = Ring([sb("j2r%d" % i, [128, 1024], BF16, pe_) for i in range(2)], "j2r")
    sr2 = Ring([sb("s2r%d" % i, [128, 4], F32, pe_) for i in range(4)], "s2r")
    kb.dma("pool", wa, w_ba.rearrange("(c p) n -> p c n", p=128), writes=["wa"])
    kb.dma("pool", wb, w_bb.rearrange("(c p) n -> p c n", p=128), writes=["wb"])
    kb.dma("pool", wo, w_out.rearrange("(c p) n -> p c n", p=128), writes=["wo"])
    kb.dma("pool", wg, w_g.rearrange("(c p) n -> p c n", p=128), writes=["wg"])
    qblocks = [(0, 512), (512, 512), (1024, 512), (1536, 512), (2048, 192)]
    for (q0, n) in qblocks:
        for fc in range(8):
            for p in range(4):
                kb.emit("pe", lambda e, p=p, fc=fc, q0=q0, n=n: e.matmul(bank(0, n), wa[:, p, fc * 128:(fc + 1) * 128], yaT[:, p, q0:q0 + n],
                                                                         start=(p == 0), stop=(p == 3)), reads=["wa"], writes=PK(0))
            for p in range(4):
                kb.emit("pe", lambda e, p=p, fc=fc, q0=q0, n=n: e.matmul(bank(1, n), wb[:, p, fc * 128:(fc + 1) * 128], ybT[:, p, q0:q0 + n],
                                                                         start=(p == 0), stop=(p == 3)), reads=["wb"], writes=PK(1))
            for g in range(2):
                for kc in range(8):
                    kb.emit("pe", lambda e, g=g, kc=kc, fc=fc, q0=q0, n=n: e.matmul(
                        bank(2 + g, n), wg[:, kc, g * 1024 + fc * 128:g * 1024 + (fc + 1) * 128], xnT[:, kc, 512 + q0:512 + q0 + n],
                        start=(kc == 0), stop=(kc == 7)), reads=["wg"], writes=PK(2 + g))
            sa, ksa = sgr.next()
            sbb, ksb = sgr.next()
            kb.emit("act", lambda e, sa=sa, n=n: e.activation(sa[:, 0:n], bank(2, n), AF.Sigmoid), reads=PK(2), writes=[ksa])
            kb.emit("act", lambda e, sbb=sbb, n=n: e.activation(sbb[:, 0:n], bank(3, n), AF.Sigmoid), reads=PK(3), writes=[ksb])
            kb.emit("dve", lambda e, sa=sa, n=n: e.tensor_tensor(sa[:, 0:n], sa[:, 0:n], bank(0, n), ALU.mult), reads=PK(0) + [ksa], writes=[ksa])
            kb.emit("dve", lambda e, sbb=sbb, n=n: e.tensor_tensor(sbb[:, 0:n], sbb[:, 0:n], bank(1, n), ALU.mult), reads=PK(1) + [ksb], writes=[ksb])
            kb.emit("dve", lambda e, sa=sa, sbb=sbb, fc=fc, n=n: e.tensor_tensor(mT[:, fc, 0:n], sa[:, 0:n], sbb[:, 0:n], ALU.add),
                    reads=[ksa, ksb], writes=[("mT", fc)])
        for tt_ in range((n + 127) // 128):
            nn = min(128, n - tt_ * 128)
            for hf in range(2):
                for fc in range(8):
                    kb.emit("pe", lambda e, hf=hf, fc=fc, tt_=tt_, nn=nn: e.matmul(bank(4 + hf, 512, 0, nn), mT[:, fc, tt_ * 128:tt_ * 128 + nn],
                                                                                 wo[:, fc, hf * 512:(hf + 1) * 512], start=(fc == 0), stop=(fc == 7)),
                            reads=["wo"] + [("mT", f_) for f_ in range(8)], writes=PK(4 + hf))
            xt, kx = xr2.next()
            mx, kmx = mr.next()
            junk, kj = jr2.next()
            st, kst = sr2.next()
            r0 = 512 + q0 + tt_ * 128
            kb.dma("sp", xt[0:nn], x_all[r0:r0 + nn, :], writes=[kx])
            kb.emit("act", lambda e, mx=mx, nn=nn: e.activation(mx[0:nn], ps[0:nn, 2048:3072], AF.Copy), reads=PK(4, 5), writes=[kmx])
            kb.emit("act", lambda e, mx=mx, junk=junk, st=st, nn=nn: e.activation(junk[0:nn], mx[0:nn], AF.Square, accum_out=st[0:nn, 0:1]),
                    reads=[kmx], writes=[kj, kst])
            rstd_from_ss(st, nn, 1.0 / 1024, [kst])
            kb.emit("dve", lambda e, mx=mx, st=st, nn=nn: e.scalar_tensor_tensor(mx[0:nn], mx[0:nn], st[0:nn, 2:3], nrm_rep[0:nn, NSLOT[1], :], ALU.mult, ALU.mult),
                    reads=[kmx, kst, "nrm_rep"], writes=[kmx])
            kb.emit("dve", lambda e, mx=mx, xt=xt, nn=nn: e.tensor_tensor(mx[0:nn], mx[0:nn], xt[0:nn], ALU.add), reads=[kmx, kx], writes=[kmx])
            kb.dma("sp", x1_d[q0 + tt_ * 128:q0 + tt_ * 128 + nn, :], mx[0:nn], reads=[kmx], writes=[("x1d", (q0 + tt_ * 128) // 64)])
    kb.barrier()
    pe_.close()
    yas.close()
    xs.close()
    if stop == "E":
        kb.finalize()
        return nc
    load_nrm(0, 2)
    load_nrm(1, 3)

    pf = ExitStack()
    NB = len(qblocks)
    x1ra = Ring([sb("x1ra%d" % i, [128, 1024], F32, pf) for i in range(3)], "x1ra")
    x1rb = Ring([sb("x1rb%d" % i, [128, 1024], F32, pf) for i in range(2)], "x1rb")
    xn2Ts = [sb("xn2T%d" % k, [128, 8, 512], BF16, pf) for k in range(2)]
    hTs = [sb("hT%d" % k, [128, NFC, 512], BF16, pf) for k in range(2)]
    mxs_ = [[sb("mx%d_%d" % (k, i), [128, 1024], F32, pf) for i in range(4)] for k in range(2)]
    gtail = sb("gtail", [128, 2, NFC], F32, pf)
    stail = sb("stail", [128, 2, NFC], F32, pf)
    cw = sb("cw", [128, 4, NFC], F32, pf)
    wgr = Ring([sb("wfg%d" % i, [128, 8, 2, 128], BF16, pf) for i in range(4)], "wfg")
    wdr = Ring([sb("wfd%d" % i, [128, 512], BF16, pf) for i in range(8)], "wfd")
    gbr = Ring([sb("gb%d" % i, [128, 516], F32, pf) for i in range(6)], "gb")
    cbr = Ring([sb("cb%d" % i, [128, 512], F32, pf) for i in range(4)], "cb")
    jr3 = Ring([sb("j3r%d" % i, [128, 1024], BF16, pf) for i in range(2)], "j3r")
    sr3 = Ring([sb("s3r%d" % i, [128, 4], F32, pf) for i in range(4)], "s3r")
    sr3b = Ring([sb("s3rb%d" % i, [128, 4], F32, pf) for i in range(4)], "s3rb")
    prr = Ring([sb("prr%d" % i, [128, 1024], F32, pf) for i in range(2)], "prr")
    kb.dma("sp", cw, convw, writes=["cw"])
    kb.emit("pool", lambda e: e.memset(gtail, 0.0), writes=["gtail"])
    kb.dma("sp", stail, sconv, writes=["stail"])

    def tiles_of(b):
        q0, n = qblocks[b]
        return [(tt_, q0 + tt_ * 128, min(128, n - tt_ * 128)) for tt_ in range((n + 127) // 128)]

    def pipe_rounds(gens, depth):
        active = []
        it = iter(gens)
        done = False
        while True:
            if not done and len(active) < depth:
                try:
                    active.append(next(it))
                except StopIteration:
                    done = True
            if not active:
                if done:
                    return
                continue
            for g in list(active):
                try:
                    next(g)
                except StopIteration:
                    active.remove(g)
            yield

    def pre_tile(b, tt_, r0, nn):
        k = b % 2
        xt, kx = x1ra.next()
        kb.dma("sp", xt[0:nn], x1_d[r0:r0 + nn, :], writes=[kx])
        junk, kj = jr3.next()
        st, kst = sr3.next()
        mx, kmx = prr.next()
        kb.emit("act", lambda e: e.activation(junk[0:nn], xt[0:nn], AF.Square, accum_out=st[0:nn, 0:1]),
                reads=[kx], writes=[kj, kst])
        yield
        yield from rstd_gen(st, nn, 1.0 / 1024, [kst])
        kb.emit("dve", lambda e: e.scalar_tensor_tensor(mx[0:nn], xt[0:nn], st[0:nn, 2:3], nrm_rep[0:nn, NSLOT[2], :], ALU.mult, ALU.mult),
                reads=[kx, kst, "nrm_rep"], writes=[kmx])
        yield
        for kc in range(8):
            kb.emit("pe", lambda e, kc=kc: e.transpose(bank(0, nn, off=kc * 128) if kc < 4 else bank(1, nn, off=(kc - 4) * 128),
                                                      mx[0:nn, kc * 128:(kc + 1) * 128], ident[0:nn, 0:nn]),
                    reads=[kmx, "ident"], writes=PK(0, 1))
        yield
        src = ps[:, 0:1024].rearrange("p (k t) -> p k t", k=8)[:, :, 0:nn]
        kb.emit("act", lambda e: e.activation(xn2Ts[k][:, :, tt_ * 128:tt_ * 128 + nn], src, AF.Copy),
                reads=PK(0, 1), writes=[("xn2T", k, tt_)])
        yield

    def ffn_pre(b):
        return pipe_rounds([pre_tile(b, tt_, r0, nn) for (tt_, r0, nn) in tiles_of(b)], 2)

    cbs = {}

    def stage_a(b, fc):
        q0, n = qblocks[b]
        k = b % 2
        xn2T = xn2Ts[k]
        xk = [("xn2T", k, t_[0]) for t_ in tiles_of(b)]
        segs = [(0, n, gtail, "gtail")] if b < NB - 1 else [(0, 128, gtail, "gtail"), (128, 64, stail, "stail")]
        wf, kwf = wgr.next()
        kb.dma("sp", wf.rearrange("p a b c -> p (a b c)"), wgu_bf[fc], writes=[kwf])
        pb = 2 + 2 * (fc % 2)
        for v in range(2):
            for kc in range(8):
                kb.emit("pe", lambda e, v=v, kc=kc: e.matmul(bank(pb + v, n), wf[:, kc, v, :], xn2T[:, kc, 0:n],
                                                             start=(kc == 0), stop=(kc == 7)),
                        reads=[kwf] + xk, writes=PK(pb + v))
        cb_, kcb = cbr.next()
        cbs[(b, fc)] = (cb_, kcb, pb)
        for (c0, ns, tl, tlk) in segs:
            gb_, kgb = gbr.next()
            kb.emit("act", lambda e, gb_=gb_, c0=c0, ns=ns: e.activation(gb_[:, 2:2 + ns], bank(pb, ns, off=c0), AF.Copy), reads=PK(pb), writes=[kgb])
            kb.emit("dve", lambda e, gb_=gb_, tl=tl: e.tensor_copy(gb_[:, 0:2], tl[:, :, fc]), reads=[tlk], writes=[(kgb, "t")])
            kb.emit("pool", lambda e, gb_=gb_, c0=c0, ns=ns: e.tensor_scalar(cb_[:, c0:c0 + ns], gb_[:, 2:2 + ns], cw[:, 2, fc:fc + 1], cw[:, 3, fc:fc + 1], ALU.mult, ALU.add),
                    reads=[kgb, "cw"], writes=[kcb])
            kb.emit("dve", lambda e, gb_=gb_, c0=c0, ns=ns: e.scalar_tensor_tensor(cb_[:, c0:c0 + ns], gb_[:, 1:1 + ns], cw[:, 1, fc:fc + 1], cb_[:, c0:c0 + ns], ALU.mult, ALU.add),
                    reads=[kgb, (kgb, "t"), "cw", kcb], writes=[kcb])
            kb.emit("dve", lambda e, gb_=gb_, c0=c0, ns=ns: e.scalar_tensor_tensor(cb_[:, c0:c0 + ns], gb_[:, 0:ns], cw[:, 0, fc:fc + 1], cb_[:, c0:c0 + ns], ALU.mult, ALU.add),
                    reads=[kgb, (kgb, "t"), "cw", kcb], writes=[kcb])
            kb.emit("dve", lambda e, gb_=gb_, ns=ns, tl=tl: e.tensor_copy(tl[:, :, fc], gb_[:, ns:ns + 2]), reads=[kgb, (kgb, "t")], writes=[tlk])

    def stage_b(b, fc):
        q0, n = qblocks[b]
        hT = hTs[b % 2]
        cb_, kcb, pb = cbs.pop((b, fc))
        kb.emit("act", lambda e: e.activation(cb_[:, 0:n], cb_[:, 0:n], AF.Gelu_apprx_tanh), reads=[kcb], writes=[kcb])
        kb.emit("dve", lambda e: e.tensor_tensor(hT[:, fc, 0:n], cb_[:, 0:n], bank(pb + 1, n), ALU.mult),
                reads=[kcb] + PK(pb + 1), writes=[("hT", b % 2, fc)])

    def down_pieces(b):
        hT = hTs[b % 2]
        tl = tiles_of(b)
        for ps_ in range(4):
            hf, pair = ps_ // 2, ps_ % 2
            tls = [t_ for t_ in tl if t_[0] // 2 == pair]
            if not tls:
                continue
            for fc in range(NFC):
                wd_, kwd = wdr.next()
                kb.dma("sp", wd_, wd_bf[fc][:, hf * 512:(hf + 1) * 512], writes=[kwd])
                for (tt_, r0, nn) in tls:
                    kb.emit("pe", lambda e, fc=fc, tt_=tt_, nn=nn, wd_=wd_: e.matmul(bank(6 + tt_ % 2, 512, 0, nn), hT[:, fc, tt_ * 128:tt_ * 128 + nn],
                                                                                 wd_, start=(fc == 0), stop=(fc == NFC - 1)),
                            reads=[kwd, ("hT", b % 2, fc)], writes=PK(6 + tt_ % 2))
                if fc == NFC - 1:
                    for (tt_, r0, nn) in tls:
                        mx = mxs_[b % 2][tt_]
                        kb.emit("act", lambda e, mx=mx, nn=nn, tt_=tt_, hf=hf: e.activation(mx[0:nn, hf * 512:(hf + 1) * 512], bank(6 + tt_ % 2, 512, 0, nn), AF.Copy),
                                reads=PK(6 + tt_ % 2), writes=[("mx", b % 2, tt_)])
                yield

    def post_tile(b, tt_, r0, nn):
        k = b % 2
        mx, kmx = mxs_[k][tt_], ("mx", k, tt_)
        junk, kj = jr3.next()
        st, kst = sr3b.next()
        xt, kx = x1rb.next()
        kb.dma("sp", xt[0:nn], x1_d[r0:r0 + nn, :], writes=[kx])
        kb.emit("act", lambda e: e.activation(junk[0:nn], mx[0:nn], AF.Square, accum_out=st[0:nn, 0:1]),
                reads=[kmx], writes=[kj, kst])
        yield
        yield from rstd_gen(st, nn, 1.0 / 1024, [kst])
        kb.emit("dve", lambda e: e.scalar_tensor_tensor(mx[0:nn], mx[0:nn], st[0:nn, 2:3], nrm_rep[0:nn, NSLOT[3], :], ALU.mult, ALU.mult),
                reads=[kmx, kst, "nrm_rep"], writes=[kmx])
        yield
        kb.emit("dve", lambda e: e.tensor_tensor(mx[0:nn], mx[0:nn], xt[0:nn], ALU.add), reads=[kmx, kx], writes=[kmx])
        yield
        if 128 <= r0 < 2176:
            kb.dma("sp", y_own[r0 - 128:r0 - 128 + nn, :], mx[0:nn], reads=[kmx], final=True)
        elif r0 >= 2176:
            kb.dma("sp", y_smp[0:nn, :], mx[0:nn], reads=[kmx], final=True)
        yield

    def ffn_post(b):
        return pipe_rounds([post_tile(b, tt_, r0, nn) for (tt_, r0, nn) in tiles_of(b)], 2)

    def drain(g, k):
        for _ in range(k):
            if next(g, "end") == "end":
                return True
        return False

    def flush(g):
        if g is not None:
            for _ in g:
                pass

    flush(ffn_pre(0))
    dgen = None
    pregen = None
    postgen = None
    for b in range(NB):
        stage_a(b, 0)
        for fc in range(NFC):
            if fc + 1 < NFC:
                stage_a(b, fc + 1)
            stage_b(b, fc)
            if fc == 2 and b + 1 < NB:
                pregen = ffn_pre(b + 1)
            if pregen is not None and drain(pregen, 2):
                pregen = None
            if postgen is not None and drain(postgen, 1):
                postgen = None
            if dgen is not None and drain(dgen, 4):
                dgen = None
        flush(pregen)
        pregen = None
        if dgen is not None:
            flush(dgen)
        flush(postgen)
        postgen = ffn_post(b - 1) if b >= 1 else None
        if b == NB - 1:
            kb.dma("sp", o_conv, gtail, reads=["gtail"], final=True)
            kb.dma("sp", o_sconv, stail, reads=["stail"], final=True)
        dgen = down_pieces(b)
    flush(postgen)
    flush(dgen)
    flush(ffn_post(NB - 1))
    pf.close()
    kb.finalize()
    es.close()
    return nc


_NC = None


def kernel(x_prompt, x_sample, cache_a_k, cache_a_v, cache_mla_ckv, cache_mla_krope, state_ffn_conv,
           norm_mix_pre, norm_mix_post, w_in, rel_bias_table, kv_norm, w_uk, w_uv, w_branch_a, w_branch_b,
           w_out, norm_ffn_pre, norm_ffn_post, w_ffn_gate, w_ffn_up, conv_w, conv_b, w_ffn_down):
    global _NC
    f = lambda a: np.ascontiguousarray(np.asarray(a, dtype=np.float32))
    x_prompt, x_sample = f(x_prompt), f(x_sample)
    W = f(w_in)[0]
    qa, ka, va, qn, qr, ck, kr_, ga, gb = 0, 512, 1024, 1536, 2048, 2304, 2560, 2592, 3616
    wq = np.zeros((1024, 8, 96), np.float32)
    wq2 = np.zeros((1024, 8, 96), np.float32)
    for h in range(8):
        wq[:, h, 0:64] = W[:, qn + 64 * h:qn + 64 * (h + 1)]
        wq[:, h, 64:96] = W[:, qr + 32 * h:qr + 32 * (h + 1)]
        wq2[:, h, 0:64] = W[:, qn + 64 * h:qn + 64 * (h + 1)]
        wq2[:, h, 64:80] = W[:, qr + 32 * h + 16:qr + 32 * h + 32]
        wq2[:, h, 80:96] = W[:, qr + 32 * h:qr + 32 * h + 16]
    half = 16
    inv = (10000.0 ** (-np.arange(half, dtype=np.float32) / half)).astype(np.float32)

    def cs_tab(pos):
        ang = pos.astype(np.float32)[:, None] * inv[None, :]
        return np.cos(ang).astype(np.float32), np.sin(ang).astype(np.float32)

    tbl = f(rel_bias_table)[0]
    kt_, ki_, qi_ = np.meshgrid(np.arange(5), np.arange(128), np.arange(128), indexing="ij")
    didx = np.clip(512 + qi_ - 128 * kt_ - ki_, -256, 256) + 256
    relb = np.ascontiguousarray(tbl[:, didx].transpose(0, 2, 1, 3))
    nrm = np.zeros((5, 1024), np.float32)
    nrm[0], nrm[1], nrm[2], nrm[3] = f(norm_mix_pre)[0], f(norm_mix_post)[0], f(norm_ffn_pre)[0], f(norm_ffn_post)[0]
    nrm[4, :256] = f(kv_norm)[0]
    nrm_b = np.ascontiguousarray(np.broadcast_to(nrm[None], (128, 5, 1024)))
    cwl = np.ascontiguousarray(np.concatenate([f(conv_w)[0], f(conv_b)], axis=0).reshape(4, NFC, 128).transpose(2, 0, 1))
    common = dict(
        ident=np.eye(128, dtype=np.float32), ones=np.ones((128, 64), np.float32),
        w_c=np.ascontiguousarray(W[:, ck:ga]),
        w_qq=np.ascontiguousarray(np.stack([wq, wq2], axis=2).reshape(8, 128, 8, 2, 96).transpose(2, 1, 0, 3, 4)),
        w_qk=np.ascontiguousarray(np.stack([W[:, qa:ka].reshape(1024, 8, 64), W[:, ka:va].reshape(1024, 8, 64)], axis=2)
                                  .reshape(8, 128, 8, 2, 64).transpose(2, 1, 0, 3, 4)),
        w_kv=np.ascontiguousarray(W[:, ka:qn]), w_g=np.ascontiguousarray(W[:, ga:]),
        relb=relb, w_uk=f(w_uk)[0].reshape(256, 512), w_uv=f(w_uv)[0].reshape(256, 512),
        w_ba=f(w_branch_a)[0], w_bb=f(w_branch_b)[0], w_out=f(w_out)[0],
        w_fgu=np.ascontiguousarray(np.stack([f(w_ffn_gate)[0].reshape(8, 128, NFC, 128), f(w_ffn_up)[0].reshape(8, 128, NFC, 128)], axis=3)
                                   .transpose(2, 1, 0, 3, 4)),
        w_fd=f(w_ffn_down)[0],
        convw=cwl, nrm=nrm_b,
    )
    in_maps = []
    for c in range(8):
        b, j = c // 4, c % 4
        s0 = 2048 * j
        x_all = np.zeros((NALL, 1024), np.float32)
        lo = s0 - 640
        if lo >= 0:
            x_all[0:640] = x_prompt[b, lo:s0]
        x_all[640:2688] = x_prompt[b, s0:s0 + 2048]
        x_all[2688:] = x_sample[c]
        pos_all = np.concatenate([np.arange(lo, s0 + 2048), 1024 + np.arange(64)])
        cc, ss = cs_tab(pos_all)
        cs_all = np.concatenate([cc, ss], axis=1)
        blocks = [v if v < j else 0 for v in range(3)]
        x_kv = np.concatenate([x_prompt[b, 2048 * v:2048 * (v + 1)] for v in blocks], axis=0)
        pos_kv = np.concatenate([np.arange(2048 * v, 2048 * (v + 1)) for v in blocks])
        ck_, sk_ = cs_tab(pos_kv)
        cs_kv = np.concatenate([ck_, sk_], axis=1)
        posq = pos_all[512:]
        cq, sq = cs_tab(posq)
        qtab = np.zeros((2, 32, NQ), np.float32)
        qtab[0, 0:16], qtab[0, 16:32] = cq.T, cq.T
        qtab[1, 0:16], qtab[1, 16:32] = -sq.T, sq.T
        kvbias = np.zeros((128, 4), np.float32)
        for v in range(3):
            if v >= j:
                kvbias[:, v] = NEG
        bandbias = np.zeros((128, 2), np.float32)
        if j == 0:
            bandbias[:, 0] = NEG
        m = dict(common)
        m.update(x_all=x_all, x_kv=np.ascontiguousarray(x_kv), cs_all=cs_all, cs_kv=cs_kv, qtab=qtab,
                 kvbias=kvbias, bandbias=bandbias,
                 cak=f(cache_a_k)[0, c].reshape(512, 512), cav=f(cache_a_v)[0, c].reshape(512, 512),
                 cckv=f(cache_mla_ckv)[0, c], ckr=f(cache_mla_krope)[0, c], sconv=np.ascontiguousarray(f(state_ffn_conv)[0, c].reshape(2, NFC, 128).transpose(2, 0, 1)))
        in_maps.append(m)
    if _NC is None:
        _NC = build()
    res = run_bass_kernel_spmd(_NC, in_maps, core_ids=list(range(8))).results
    y_p = np.zeros((2, 8192, 1024), np.float32)
    ckv_p = np.zeros((1, 2, 8192, 256), np.float32)
    kr_p = np.zeros((1, 2, 8192, 32), np.float32)
    for c in range(8):
        b, j = c // 4, c % 4
        y_p[b, 2048 * j:2048 * (j + 1)] = res[c]["y_own"]
        ckv_p[0, b, 2048 * j:2048 * (j + 1)] = res[c]["o_ckv"]
        kr_p[0, b, 2048 * j:2048 * (j + 1)] = res[c]["o_kr"]
    y_s = np.stack([res[c]["y_smp"] for c in range(8)])
    ak_p = np.stack([res[4 * b + 3]["o_kav"][:, 0:512].reshape(512, 8, 64) for b in range(2)])[None]
    av_p = np.stack([res[4 * b + 3]["o_kav"][:, 512:1024].reshape(512, 8, 64) for b in range(2)])[None]
    conv_p = np.stack([res[4 * b + 3]["o_conv"].transpose(1, 2, 0).reshape(2, DFF) for b in range(2)])[None]
    sk = np.stack([res[c]["o_sk"].reshape(512, 8, 64) for c in range(8)])[None]
    sv = np.stack([res[c]["o_sv"].reshape(512, 8, 64) for c in range(8)])[None]
    sckv = np.stack([res[c]["o_sckv"] for c in range(8)])[None]
    skr = np.stack([res[c]["o_skr"] for c in range(8)])[None]
    sconv = np.stack([res[c]["o_sconv"].transpose(1, 2, 0).reshape(2, DFF) for c in range(8)])[None]
    return (y_p, y_s, ak_p, av_p, ckv_p, kr_p, conv_p, sk, sv, sckv, skr, sconv)
```

```python
import numpy as np
import concourse.bass as bass
import concourse.mybir as mybir
from concourse.bass_utils import run_bass_kernel_spmd
from contextlib import ExitStack

F32 = mybir.dt.float32
BF16 = mybir.dt.bfloat16
AF = mybir.ActivationFunctionType
ALU = mybir.AluOpType

EPS = 1e-6
A_SCALE = 64 ** -0.5
MLA_SCALE = 96 ** -0.5
NEG = -1.0e30
NALL = 2752
NQ = 2240
NKV = 6144
DFF = 2816
NFC = 22


class _Res:
    __slots__ = ("lw", "rd")

    def __init__(self):
        self.lw = None
        self.rd = []


class KB:
    ENG = ("pe", "act", "dve", "pool", "sp")

    def __init__(self, nc, n_dma_sems=20):
        self.nc = nc
        self.ops = []
        self.res = {}
        self.n_dma_sems = n_dma_sems
        self.out_dmas = []
        self.bar_deps = set()
        self.bar_pending = set()
        self.since_bar = {}

    def _r(self, k):
        r = self.res.get(k)
        if r is None:
            r = self.res[k] = _Res()
        return r

    def barrier(self):
        d = set(self.bar_deps)
        for e, i in self.since_bar.items():
            d.add(i)
        for i, op in enumerate(self.ops):
            if op["is_dma"] and i >= getattr(self, "_bar_pos", 0):
                d.add(i)
        self.bar_deps = set()
        last = {}
        for i in d:
            op = self.ops[i]
            if op["is_dma"]:
                if i >= getattr(self, "_bar_pos", 0):
                    self.bar_deps.add(i)
            else:
                last[op["eng"]] = max(last.get(op["eng"], -1), i)
        self.bar_deps.update(last.values())
        for i in self.bar_deps:
            self.ops[i]["sig"] = True
        self._bar_pos = len(self.ops)
        self.since_bar = {}
        self.res = {}
        self.bar_pending = set(self.ENG)

    def _add(self, eng, fn, reads, writes, is_dma):
        idx = len(self.ops)
        deps = set()
        if eng in self.bar_pending:
            deps.update(self.bar_deps)
            self.bar_pending.discard(eng)
        for k in reads:
            r = self._r(k)
            if r.lw is not None:
                deps.add(r.lw)
        for k in writes:
            r = self._r(k)
            if r.lw is not None:
                deps.add(r.lw)
            deps.update(r.rd)
        if eng == "pe" and not is_dma:
            deps = {d for d in deps if not (self.ops[d]["eng"] == "pe" and not self.ops[d]["is_dma"])}
        self.ops.append(dict(eng=eng, fn=fn, deps=deps, is_dma=is_dma, sig=is_dma))
        for d in deps:
            self.ops[d]["sig"] = True
        for k in reads:
            self._r(k).rd.append(idx)
        for k in writes:
            r = self._r(k)
            r.lw = idx
            r.rd = []
        if not is_dma:
            self.since_bar[eng] = idx
        return idx

    def emit(self, eng, fn, reads=(), writes=()):
        return self._add(eng, fn, list(reads), list(writes), False)

    def dma(self, eng, out, in_, reads=(), writes=(), final=False, **kw):
        def fn(e, out=out, in_=in_, kw=kw):
            return e.dma_start(out=out, in_=in_, **kw)
        i = self._add(eng, fn, list(reads), list(writes), True)
        if final:
            self.out_dmas.append(i)
        return i

    def finalize(self):
        nc = self.nc
        ops = self.ops
        esem = {e: nc.alloc_semaphore(name="s_" + e) for e in self.ENG}
        dsem = {e: [nc.alloc_semaphore(name="d_%s_%d" % (e, i)) for i in range(self.n_dma_sems)]
                for e in ("sp", "act", "pool")}
        ecnt = {e: 0 for e in self.ENG}
        dnext = {e: 0 for e in dsem}
        dval = {e: [0] * self.n_dma_sems for e in dsem}
        tok = {}
        prevtok = {}
        for i, op in enumerate(ops):
            if not op["sig"]:
                continue
            e = op["eng"]
            if op["is_dma"]:
                s = dnext[e] % self.n_dma_sems
                dnext[e] += 1
                if dval[e][s] > 0:
                    prevtok[i] = (("d", e, s), dval[e][s])
                dval[e][s] += 16
                tok[i] = (("d", e, s), dval[e][s])
            else:
                ecnt[e] += 1
                tok[i] = (("e", e), ecnt[e])
        seen = {e: {} for e in self.ENG}

        def semof(key):
            return esem[key[1]] if key[0] == "e" else dsem[key[1]][key[2]]

        streams = {e: [] for e in self.ENG}
        for i, op in enumerate(ops):
            streams[op["eng"]].append(i)

        def run(ename, engine):
            sn = seen[ename]
            for i in streams[ename]:
                op = ops[i]
                need = {}
                for d in op["deps"]:
                    k, v = tok[d]
                    if need.get(k, 0) < v:
                        need[k] = v
                if i in prevtok:
                    k, v = prevtok[i]
                    if need.get(k, 0) < v:
                        need[k] = v
                for k, v in need.items():
                    if sn.get(k, 0) < v:
                        engine.wait_ge(semof(k), v)
                        sn[k] = v
                ins = op["fn"](engine)
                if op["sig"]:
                    k, v = tok[i]
                    ins.then_inc(semof(k), 16 if op["is_dma"] else 1)
            if ename == "sp":
                for i in self.out_dmas:
                    k, v = tok[i]
                    if sn.get(k, 0) < v:
                        engine.wait_ge(semof(k), v)
                        sn[k] = v

        with nc.Block() as block:
            @block.tensor
            def _(e):
                run("pe", e)

            @block.scalar
            def _(e):
                run("act", e)

            @block.vector
            def _(e):
                run("dve", e)

            @block.gpsimd
            def _(e):
                run("pool", e)

            @block.sync
            def _(e):
                run("sp", e)


class Ring:
    def __init__(self, aps, name):
        self.aps = aps
        self.name = name
        self.i = 0

    def next(self):
        j = self.i % len(self.aps)
        self.i += 1
        return self.aps[j], (self.name, j)


def build(stop=None):
    nc = bass.Bass("TRN2", target_bir_lowering=False)
    kb = KB(nc)

    def din(name, shape):
        return nc.dram_tensor(name, shape, F32, kind="ExternalInput").ap()

    def dout(name, shape):
        return nc.dram_tensor(name, shape, F32, kind="ExternalOutput").ap()

    x_all = din("x_all", [NALL, 1024])
    x_kv = din("x_kv", [NKV, 1024])
    cs_all = din("cs_all", [NALL, 32])
    cs_kv = din("cs_kv", [NKV, 32])
    qtab_d = din("qtab", [2, 32, NQ])
    kvbias_d = din("kvbias", [128, 4])
    bandbias_d = din("bandbias", [128, 2])
    ident_d = din("ident", [128, 128])
    ones_d = din("ones", [128, 64])
    cak = din("cak", [512, 512])
    cav = din("cav", [512, 512])
    cckv = din("cckv", [1024, 256])
    ckr = din("ckr", [1024, 32])
    sconv = din("sconv", [128, 2, NFC])
    w_c = din("w_c", [1024, 288])
    w_qq = din("w_qq", [8, 128, 8, 2, 96])
    w_qk = din("w_qk", [8, 128, 8, 2, 64])
    w_kv = din("w_kv", [1024, 1024])
    w_g = din("w_g", [1024, 2048])
    relb = din("relb", [8, 128, 5, 128])
    w_uk = din("w_uk", [256, 512])
    w_uv = din("w_uv", [256, 512])
    w_ba = din("w_ba", [512, 1024])
    w_bb = din("w_bb", [512, 1024])
    w_out = din("w_out", [1024, 1024])
    w_fgu = din("w_fgu", [NFC, 128, 8, 2, 128])
    w_fd = din("w_fd", [DFF, 1024])
    convw = din("convw", [128, 4, NFC])
    nrm = din("nrm", [128, 5, 1024])

    y_own = dout("y_own", [2048, 1024])
    y_smp = dout("y_smp", [64, 1024])
    o_kav = dout("o_kav", [512, 1024])
    o_ckv = dout("o_ckv", [2048, 256])
    o_kr = dout("o_kr", [2048, 32])
    o_conv = dout("o_conv", [128, 2, NFC])
    o_sk = dout("o_sk", [512, 512])
    o_sv = dout("o_sv", [512, 512])
    o_sckv = dout("o_sckv", [64, 256])
    o_skr = dout("o_skr", [64, 32])
    o_sconv = dout("o_sconv", [128, 2, NFC])
    x1_d = nc.dram_tensor("x1_scr", [NQ, 1024], F32).ap()
    wgu_bf = nc.dram_tensor("wgu_bf", [NFC, 128, 2048], BF16).ap()
    wd_bf = nc.dram_tensor("wd_bf", [NFC, 128, 1024], BF16).ap()

    ps = nc.alloc_psum_tensor("ps", [128, 4096], F32).ap()

    def bank(b, n=512, p0=0, p1=128, off=0):
        return ps[p0:p1, 512 * b + off:512 * b + off + n]

    def PK(*bs):
        return [("ps", b) for b in bs]

    es = ExitStack()

    def sb(name, shape, dt, stack=None):
        return (stack or es).enter_context(nc.sbuf_tensor("sb_" + name, shape, dt))[:]

    ident = sb("ident", [128, 128], F32)
    nrm_rep = sb("nrm_rep", [128, 2, 1024], F32)
    NSLOT = {}

    def load_nrm(slot, row):
        NSLOT[row] = slot
        kb.dma("sp", nrm_rep[:, slot, :], nrm[:, row, :], writes=["nrm_rep"])
    kvbias = sb("kvbias", [128, 4], F32)
    z1 = sb("z1", [1, 128], BF16)
    kb.emit("pool", lambda e: e.memset(z1, 0.0), writes=["z1"])
    bandbias = sb("bandbias", [128, 2], F32)
    xs = ExitStack()
    xnT = sb("xnT", [128, 8, NALL], BF16, xs)
    ybT = sb("ybT", [128, 4, NQ], BF16, xs)

    kb.dma("sp", ident, ident_d, writes=["ident"])
    kb.dma("sp", kvbias, kvbias_d, writes=["kvbias"])
    kb.dma("sp", bandbias, bandbias_d, writes=["bandbias"])
    load_nrm(0, 0)
    load_nrm(1, 4)

    def rstd_from_ss(st, n, inv_d, keys):
        kb.emit("dve", lambda e: e.tensor_scalar(st[0:n, 1:2], st[0:n, 0:1], inv_d, EPS, ALU.mult, ALU.add),
                reads=keys, writes=keys)
        kb.emit("act", lambda e: e.activation(st[0:n, 3:4], st[0:n, 1:2], AF.Sqrt), reads=keys, writes=keys)
        kb.emit("dve", lambda e: e.reciprocal(st[0:n, 2:3], st[0:n, 3:4]), reads=keys, writes=keys)

    def rstd_gen(st, n, inv_d, keys):
        kb.emit("dve", lambda e: e.tensor_scalar(st[0:n, 1:2], st[0:n, 0:1], inv_d, EPS, ALU.mult, ALU.add),
                reads=keys, writes=keys)
        yield
        kb.emit("act", lambda e: e.activation(st[0:n, 3:4], st[0:n, 1:2], AF.Sqrt), reads=keys, writes=keys)
        yield
        kb.emit("dve", lambda e: e.reciprocal(st[0:n, 2:3], st[0:n, 3:4]), reads=keys, writes=keys)
        yield

    def norm_transpose(xt, kx, n, widx, dst, dkeys, junk, kj, st, kst, b0=0):
        kb.emit("act", lambda e: e.activation(junk[0:n], xt[0:n], AF.Square, accum_out=st[0:n, 0:1]),
                reads=[kx], writes=[kj, kst])
        yield
        yield from rstd_gen(st, n, 1.0 / 1024, [kst])
        kb.emit("dve", lambda e: e.scalar_tensor_tensor(xt[0:n], xt[0:n], st[0:n, 2:3], nrm_rep[0:n, NSLOT[widx], :],
                                                        ALU.mult, ALU.mult),
                reads=[kx, kst, "nrm_rep"], writes=[kx])
        yield
        for kc in range(8):
            kb.emit("pe", lambda e, kc=kc: e.transpose(bank(b0, n, off=kc * 128) if kc < 4 else bank(b0 + 1, n, off=(kc - 4) * 128),
                                                      xt[0:n, kc * 128:(kc + 1) * 128], ident[0:n, 0:n]),
                    reads=[kx, "ident"], writes=PK(b0, b0 + 1))
        yield
        src = ps[:, 512 * b0:512 * b0 + 1024].rearrange("p (k t) -> p k t", k=8)[:, :, 0:n]
        kb.emit("act", lambda e: e.activation(dst, src, AF.Copy), reads=PK(b0, b0 + 1), writes=dkeys)
        yield

    def run_pipeline(gens, depth):
        active = []
        it = iter(gens)
        done = False
        while True:
            if not done and len(active) < depth:
                try:
                    active.append(next(it))
                except StopIteration:
                    done = True
            if not active:
                if done:
                    break
                continue
            for g in list(active):
                try:
                    next(g)
                except StopIteration:
                    active.remove(g)

    lat = ExitStack()
    ckvT_kv = sb("ckvT_kv", [128, 2, NKV], BF16, lat)
    ckvT_own = sb("ckvT_own", [128, 2, NQ], BF16, lat)
    ckvT_c = sb("ckvT_c", [128, 2, 1024], BF16, lat)
    Kp = sb("Kp", [96, NKV + 2048], BF16, lat)
    Ks = sb("Ks", [96, 1088], BF16, lat)
    pa = ExitStack()
    xring = Ring([sb("xr%d" % i, [128, 1024], F32, pa) for i in range(6)], "xr")
    jring = Ring([sb("jr%d" % i, [128, 1024], BF16, pa) for i in range(4)], "jr")
    sring = Ring([sb("sr%d" % i, [128, 4], F32, pa) for i in range(12)], "sr")
    ltring = Ring([sb("lt%d" % i, [128, 320], F32, pa) for i in range(5)], "lt")
    csring = Ring([sb("cs%d" % i, [128, 32], F32, pa) for i in range(5)], "cs")
    tring = Ring([sb("tt%d" % i, [128, 64], F32, pa) for i in range(5)], "tt")
    xtmpT = Ring([sb("xtT%d" % i, [128, 8, 128], BF16, pa) for i in range(5)], "xtT")
    wc = sb("wc", [128, 8, 288], BF16, pa)
    kb.dma("pool", wc, w_c.rearrange("(kc p) n -> p kc n", p=128), writes=["wc"])
    tiles_all = [(t * 128, 128) for t in range(21)] + [(2688, 64)]

    def latent(xT_fn, xkeys, n, cs_rows, dst_ckvT, dst_krT, dkeys, out_ckv=None, out_kr=None, par=0):
        mmb, trb = (2, 3) if par == 0 else (6, 7)
        for kc in range(8):
            kb.emit("pe", lambda e, kc=kc: e.matmul(bank(mmb, 288, 0, n), xT_fn(kc), wc[:, kc, :],
                                                    start=(kc == 0), stop=(kc == 7)),
                    reads=xkeys + ["wc"], writes=PK(mmb))
        lt, kl = ltring.next()
        st, kst = sring.next()
        junk, kj = jring.next()
        cs, kcs = csring.next()
        tt, ktt = tring.next()
        kb.dma("sp", cs[0:n], cs_rows, writes=[kcs])
        yield
        kb.emit("act", lambda e: e.activation(lt[0:n, 0:288], bank(mmb, 288, 0, n), AF.Copy), reads=PK(mmb), writes=[kl])
        yield
        kb.emit("act", lambda e: e.activation(junk[0:n, 0:256], lt[0:n, 0:256], AF.Square, accum_out=st[0:n, 0:1]),
                reads=[kl], writes=[kj, kst])
        x1, x2 = lt[0:n, 256:272], lt[0:n, 272:288]
        c, s = cs[0:n, 0:16], cs[0:n, 16:32]
        kb.emit("dve", lambda e: e.tensor_tensor(tt[0:n, 0:16], x1, c, ALU.mult), reads=[kl, kcs], writes=[ktt])
        kb.emit("dve", lambda e: e.tensor_tensor(tt[0:n, 16:32], x2, s, ALU.mult), reads=[kl, kcs], writes=[ktt])
        kb.emit("dve", lambda e: e.tensor_tensor(tt[0:n, 32:48], x1, s, ALU.mult), reads=[kl, kcs], writes=[ktt])
        kb.emit("dve", lambda e: e.tensor_tensor(tt[0:n, 48:64], x2, c, ALU.mult), reads=[kl, kcs], writes=[ktt])
        yield
        kb.emit("dve", lambda e: e.tensor_tensor(lt[0:n, 288:304], tt[0:n, 0:16], tt[0:n, 16:32], ALU.subtract),
                reads=[ktt], writes=[(kl, "kr")])
        kb.emit("dve", lambda e: e.tensor_tensor(lt[0:n, 304:320], tt[0:n, 32:48], tt[0:n, 48:64], ALU.add),
                reads=[ktt], writes=[(kl, "kr")])
        yield from rstd_gen(st, n, 1.0 / 256, [kst])
        kb.emit("dve", lambda e: e.scalar_tensor_tensor(lt[0:n, 0:256], lt[0:n, 0:256], st[0:n, 2:3],
                                                        nrm_rep[0:n, NSLOT[4], 0:256], ALU.mult, ALU.mult),
                reads=[kl, kst, "nrm_rep"], writes=[kl])
        yield
        if out_ckv is not None:
            kb.dma("sp", out_ckv, lt[0:n, 0:256], reads=[kl], final=True)
            kb.dma("sp", out_kr, lt[0:n, 288:320], reads=[kl, (kl, "kr")], final=True)
        for c2 in range(2):
            kb.emit("pe", lambda e, c2=c2: e.transpose(bank(trb, n, off=c2 * 128), lt[0:n, c2 * 128:(c2 + 1) * 128],
                                                      ident[0:n, 0:n]),
                    reads=[kl, "ident"], writes=PK(trb))
        kb.emit("pe", lambda e: e.transpose(bank(trb, n, 0, 96, off=256), lt[0:n, 224:320], ident[0:n, 0:n]),
                reads=[kl, (kl, "kr"), "ident"], writes=PK(trb))
        yield
        src = bank(trb, 256).rearrange("p (k t) -> p k t", k=2)[:, :, 0:n]
        kb.emit("act", lambda e: e.activation(dst_ckvT, src, AF.Copy), reads=PK(trb), writes=dkeys)
        kb.emit("dve", lambda e: e.tensor_copy(dst_krT, bank(trb, n, 64, 96, off=256)), reads=PK(trb), writes=dkeys)
        yield

    def own_tile(ti, t0, n):
        xt, kx = xring.next()
        kb.dma("sp", xt[0:n], x_all[t0:t0 + n, :], writes=[kx])
        junk, kj = jring.next()
        st, kst = sring.next()
        yield
        yield from norm_transpose(xt, kx, n, 0, xnT[:, :, t0:t0 + n], [("xnT", t0 // 128)], junk, kj, st, kst, b0=4 * (ti % 2))
        if t0 < 512:
            return
        q0 = t0 - 512
        is_own = 640 <= t0 < 2688
        is_smp = t0 == 2688
        oc = ok = None
        if is_own:
            oc, ok = o_ckv[t0 - 640:t0 - 640 + n, :], o_kr[t0 - 640:t0 - 640 + n, :]
        if is_smp:
            oc, ok = o_sckv[:, :], o_skr[:, :]
        if is_own:
            krdst = Kp[64:96, NKV + t0 - 640:NKV + t0 - 640 + n]
        elif is_smp:
            krdst = Ks[64:96, 1024:1088]
        else:
            krdst = Ks[64:96, 0:n]
        yield from latent(lambda kc: xnT[:, kc, t0:t0 + n], [("xnT", t0 // 128)], n, cs_all[t0:t0 + n, :],
                          ckvT_own[:, :, q0:q0 + n], krdst, [("lat_own", t0 // 128), "Ks_dump"], oc, ok, par=ti % 2)

    def kv_tile(t):
        xt, kx = xring.next()
        kb.dma("sp", xt, x_kv[t * 128:(t + 1) * 128, :], writes=[kx])
        junk, kj = jring.next()
        st, kst = sring.next()
        xT, kxT = xtmpT.next()
        yield
        yield from norm_transpose(xt, kx, 128, 0, xT, [kxT], junk, kj, st, kst, b0=4 * (t % 2))
        yield from latent(lambda kc: xT[:, kc, :], [kxT], 128, cs_kv[t * 128:(t + 1) * 128, :],
                          ckvT_kv[:, :, t * 128:(t + 1) * 128], Kp[64:96, t * 128:(t + 1) * 128], [("lat_kv", t)], par=t % 2)

    gens = [own_tile(ti, t0, n) for ti, (t0, n) in enumerate(tiles_all)] + [kv_tile(t) for t in range(NKV // 128)]
    run_pipeline(gens, 4)
    for t in range(8):
        xt, kx = xring.next()
        kb.dma("sp", xt[:, 0:256], cckv[t * 128:(t + 1) * 128, :], writes=[kx])
        kb.emit("pool", lambda e, xt=xt: e.memset(xt[:, 256:320], 0.0), writes=[kx])
        kb.dma("sp", xt[:, 320:352], ckr[t * 128:(t + 1) * 128, :], writes=[kx])
        for c2 in range(2):
            kb.emit("pe", lambda e, c2=c2, xt=xt: e.transpose(bank(3, 128, off=c2 * 128), xt[:, c2 * 128:(c2 + 1) * 128], ident),
                    reads=[kx, "ident"], writes=PK(3))
        kb.emit("pe", lambda e, xt=xt: e.transpose(bank(3, 128, 0, 96, off=256), xt[:, 256:352], ident),
                reads=[kx, "ident"], writes=PK(3))
        src = bank(3, 256).rearrange("p (k t) -> p k t", k=2)
        kb.emit("act", lambda e, t=t, src=src: e.activation(ckvT_c[:, :, t * 128:(t + 1) * 128], src, AF.Copy),
                reads=PK(3), writes=[("lat_c", t)])
        kb.emit("dve", lambda e, t=t: e.tensor_copy(Ks[64:96, t * 128:(t + 1) * 128], bank(3, 128, 64, 96, off=256)),
                reads=PK(3), writes=[("lat_c", t), "Ks_dump"])
    kb.barrier()
    pa.close()
    if stop == "B":
        kb.finalize()
        return nc

    ACCC = [0]

    def attend(pc, q_ap, qrows, ncol, ktiles, scale, out_dst, par, PTring, recring, okeys, qkeys, relb_t=None, hook=None, defer=False):
        nt = len(ktiles)
        if ncol <= 512:
            nbuf, bstep = 4, 1
            ACCC[0] += 1
            accb = 6 + ACCC[0] % 2
            akeys = PK(accb)
        else:
            nbuf, bstep = 2, 2
            accb = 6
            akeys = PK(6, 7)
        LA = nbuf - 1
        G = max(1, 512 // ncol) if ncol <= 128 else 1
        batches = []
        for ti, kt in enumerate(ktiles):
            ok = False
            if batches and len(batches[-1]) < G and G > 1:
                p = ktiles[batches[-1][-1]]
                ok = (p["nk"] == 128 and kt["nk"] == 128 and p.get("c0", 0) == 0 and kt.get("c0", 0) == 0
                      and (p.get("bias") is kt.get("bias"))
                      and (relb_t is None or kt["rb"] == p["rb"] + 1))
            if ok:
                batches[-1].append(ti)
            else:
                batches.append([ti])
        nbt = len(batches)
        st8 = {}

        def segs_of(c0):
            segs = []
            a = c0
            while a < ncol:
                b = min(ncol, (a // 512 + 1) * 512)
                segs.append((a, b))
                a = b
            return segs

        def qk(bi):
            sb_ = 2 + bstep * (bi % nbuf)
            bks = PK(*range(sb_, sb_ + bstep))
            for s_i, ti in enumerate(batches[bi]):
                kt = ktiles[ti]
                nk = kt["nk"]
                c0 = kt.get("c0", 0)
                o_ = 512 * sb_ + s_i * ncol
                for (a, b) in segs_of(c0):
                    kb.emit("pe", lambda e, a=a, b=b, kt=kt, nk=nk, o_=o_: e.matmul(
                        ps[0:nk, o_ + a:o_ + b], kt["K"], q_ap[:, a:b], start=True, stop=True),
                        reads=kt["kkeys"] + qkeys, writes=bks)

        def ex(bi):
            bt = batches[bi]
            nb = len(bt)
            kt = ktiles[bt[0]]
            sb_ = 2 + bstep * (bi % nbuf)
            nk = kt["nk"]
            c0 = kt.get("c0", 0)
            PT, kpt = PTring.next()
            st8[bi] = (PT, kpt)
            w = (nb - 1) * ncol + ncol
            src = ps[0:nk, 512 * sb_ + c0:512 * sb_ + w]
            rkeys = PK(*range(sb_, sb_ + bstep))
            if relb_t is not None:
                tmp, ktmp = pc["tmpring"].next()
                rb = kt["rb"]
                if nb == 1:
                    i0, i1, o0_ = src, relb_t[0:nk, rb, c0:ncol], tmp[0:nk, c0:ncol]
                else:
                    i0 = src.rearrange("p (b c) -> p b c", b=nb)
                    i1 = relb_t[0:nk, rb:rb + nb, 0:ncol]
                    o0_ = tmp[0:nk, 0:w].rearrange("p (b c) -> p b c", b=nb)
                kb.emit("dve", lambda e, i0=i0, i1=i1, o0_=o0_: e.scalar_tensor_tensor(o0_, i0, scale, i1, ALU.mult, ALU.add),
                        reads=rkeys + ["relb"], writes=[ktmp])
                src2, rk2, sc = tmp[0:nk, c0:w], [ktmp], 1.0
            else:
                src2, rk2, sc = src, rkeys, scale
            bias = kt.get("bias")
            if bias is not None:
                kb.emit("act", lambda e, PT=PT, src2=src2, bias=bias, nk=nk, c0=c0, sc=sc, w=w: e.activation(
                    PT[0:nk, c0:w], src2, AF.Exp, bias=bias, scale=sc),
                    reads=rk2 + ["kvbias", "bandbias"], writes=[kpt])
            else:
                kb.emit("act", lambda e, PT=PT, src2=src2, nk=nk, c0=c0, sc=sc, w=w: e.activation(
                    PT[0:nk, c0:w], src2, AF.Exp, scale=sc), reads=rk2, writes=[kpt])

        def pv(bi):
            PT, kpt = st8.pop(bi)
            for s_i, ti in enumerate(batches[bi]):
                kt = ktiles[ti]
                nk = kt["nk"]
                c0 = kt.get("c0", 0)
                po = s_i * ncol
                segs = segs_of(c0)
                pieces = []
                half = kt.get("half")
                if half is not None:
                    lo, hi = (0, 64) if half == "lo" else (64, 128)
                    pieces.append((c0, c0 + 64, lo, hi))
                    for (a, b) in segs_of(c0 + 64):
                        pieces.append((a, b, 0, nk))
                else:
                    for (a, b) in segs:
                        pieces.append((a, b, 0, nk))
                half2 = kt.get("half2")
                if half2 is not None:
                    newp = []
                    h2a, h2b, lo2, hi2 = half2
                    for (a, b, lo, hi) in pieces:
                        if b <= h2a or a >= h2b:
                            newp.append((a, b, lo, hi))
                        else:
                            if a < h2a:
                                newp.append((a, h2a, lo, hi))
                            newp.append((max(a, h2a), min(b, h2b), lo2, hi2))
                            if b > h2b:
                                newp.append((h2b, b, lo, hi))
                    pieces = newp
                st_flag = (ti == 0)
                if ti == 0:
                    assert half is None and half2 is None and c0 == 0
                for pi, (a, b, lo, hi) in enumerate(pieces):
                    last_in_bank = False
                    kb.emit("pe", lambda e, a=a, b=b, lo=lo, hi=hi, kt=kt, PT=PT, st_flag=st_flag, lb=last_in_bank, po=po: e.matmul(
                        ps[:, 512 * accb + a:512 * accb + b], kt["V"][lo:hi, :], PT[lo:hi, po + a:po + b], start=st_flag, stop=lb),
                        reads=[kpt] + kt["vkeys"], writes=akeys)

        for bi in range(min(LA, nbt)):
            qk(bi)
        for bi in range(nbt):
            if bi + LA < nbt:
                qk(bi + LA)
            ex(bi)
            pv(bi)
            if hook is not None:
                hook(batches[bi][-1])
        for (a, b) in segs_of(0):
            kb.emit("pe", lambda e, a=a, b=b: e.matmul(ps[:, 512 * accb + a:512 * accb + b], z1[0:1, 0:128], q_ap[0:1, a:b],
                                                       start=False, stop=True),
                    reads=qkeys + ["z1"], writes=akeys)

        def fin():
            rec, krec = recring.next()
            (o0, o1), (d0, d1) = ((0, 64), (64, 128)) if par == 0 else ((64, 128), (0, 64))
            A0 = 512 * accb
            kb.emit("dve", lambda e: e.tensor_scalar(rec[d0:d1, 0:ncol], ps[d0:d1, A0:A0 + ncol], 1e-30, None, ALU.max),
                    reads=akeys, writes=[krec])
            kb.emit("dve", lambda e: e.reciprocal(rec[d0:d1, 0:ncol], rec[d0:d1, 0:ncol]), reads=[krec], writes=[krec])
            kb.emit("dve", lambda e: e.tensor_tensor(out_dst, ps[o0:o1, A0:A0 + ncol], rec[d0:d1, 0:ncol], ALU.mult),
                    reads=akeys + [krec], writes=okeys)
        if defer:
            return fin
        fin()

    pc_ = ExitStack()
    wuk = sb("wuk", [128, 2, 512], BF16, pc_)
    wuv = sb("wuv", [128, 2, 512], BF16, pc_)
    qtr = Ring([sb("qtab%d" % i, [96, 2, 512], F32, pc_) for i in range(2)], "qtab")
    Vp = sb("Vp", [128, 64, 128], BF16, pc_)
    Vs = sb("Vs", [128, 9, 128], BF16, pc_)
    qTs = [sb("qT%d" % i, [96, NQ], BF16, pc_) for i in range(2)]
    wqr = Ring([sb("wq%d" % i, [128, 8, 2, 96], BF16, pc_) for i in range(2)], "wq")
    PTr = Ring([sb("PT%d" % i, [128, 1024], BF16, pc_) for i in range(4)], "PT")
    recr = Ring([sb("rec%d" % i, [128, 1024], F32, pc_) for i in range(1)], "rec")
    rt = Ring([sb("rt%d" % i, [96, 512], F32, pc_) for i in range(4)], "rt")
    ones_sb = sb("ones_sb", [128, 64], BF16, pc_)
    kb.dma("pool", wuk, w_uk.rearrange("(c p) n -> p c n", p=128), writes=["wuk"])
    kb.dma("pool", wuv, w_uv.rearrange("(c p) n -> p c n", p=128), writes=["wuv"])
    kb.dma("pool", ones_sb, ones_d, writes=["ones"])
    pc = {}

    qgroups = [(640 - 512, 1024), (640 - 512 + 1024, 1024), (0, 128)]
    kvb = [kvbias[:, v:v + 1] for v in range(3)]

    def mla_prep(h):
        par = h % 2
        vo = 0 if par == 0 else 64
        on = 64 if par == 0 else 0
        qT = qTs[h % 2]
        wq, kwq = wqr.next()
        th = [(-1, lambda: kb.dma("pool", wq, w_qq[h], writes=[kwq]))]

        def kexp(dst, srcT, ncols, dkey):
            for c in range(2):
                kb.emit("pe", lambda e, c=c: e.matmul(bank(0, ncols, 0, 64), wuk[:, c, h * 64:(h + 1) * 64], srcT(c),
                                                      start=(c == 0), stop=(c == 1)),
                        reads=["wuk"], writes=PK(0))
            kb.emit("dve", lambda e: e.tensor_copy(dst, bank(0, ncols, 0, 64)), reads=PK(0), writes=[dkey])

        def vexp(dst3, srcT, nt_, nk, dkey, ones_dst):
            for j in range(nt_):
                for c in range(2):
                    kb.emit("pe", lambda e, j=j, c=c: e.matmul(bank(1, 64, 0, nk, off=j * 64), srcT(c, j),
                                                              wuv[:, c, h * 64:(h + 1) * 64], start=(c == 0), stop=(c == 1)),
                            reads=["wuv"], writes=PK(1))
            src = bank(1, nt_ * 64, 0, nk).rearrange("p (t d) -> p t d", d=64)
            kb.emit("dve", lambda e: e.tensor_copy(dst3, src), reads=PK(1), writes=[dkey])

        def ones_fill(V3, t0_, nt_, dkey):
            for t in range(t0_, t0_ + nt_):
                kb.emit("pool", lambda e, t=t: e.tensor_copy(V3[:, t, on:on + 64], ones_sb), reads=["ones"], writes=[dkey])

        for g in range(12):
            th.append((4 * g + 3, lambda g=g: kexp(Kp[0:64, g * 512:(g + 1) * 512],
                                                   lambda c: ckvT_kv[:, c, g * 512:(g + 1) * 512], 512, ("Kp", g))))
        for g in range(4):
            th.append((48 + 4 * g + 3, lambda g=g: kexp(Kp[0:64, NKV + g * 512:NKV + (g + 1) * 512],
                                                        lambda c: ckvT_own[:, c, 128 + g * 512:128 + (g + 1) * 512], 512, ("Kp", 12 + g))))
        for g in range(2):
            th.append((-1, lambda g=g: kexp(Ks[0:64, g * 512:(g + 1) * 512], lambda c: ckvT_c[:, c, g * 512:(g + 1) * 512], 512, ("Ks", g))))
        th.append((-1, lambda: kexp(Ks[0:64, 1024:1088], lambda c: ckvT_own[:, c, 2176:2240], 64, ("Ks", 2))))
        for g in range(6):
            def vth(g=g):
                ones_fill(Vp, g * 8, 8, ("Vp", g))
                vexp(Vp[:, g * 8:(g + 1) * 8, vo:vo + 64], lambda c, j: ckvT_kv[:, c, (g * 8 + j) * 128:(g * 8 + j + 1) * 128], 8, 128, ("Vp", g), None)
            th.append((8 * g + 7, vth))
        for g in range(2):
            def vth2(g=g):
                ones_fill(Vp, 48 + g * 8, 8, ("Vp", 6 + g))
                vexp(Vp[:, 48 + g * 8:48 + (g + 1) * 8, vo:vo + 64],
                     lambda c, j: ckvT_own[:, c, 128 + (g * 8 + j) * 128:128 + (g * 8 + j + 1) * 128], 8, 128, ("Vp", 6 + g), None)
            th.append((48 + 8 * g + 7, vth2))

        def vs_th():
            ones_fill(Vs, 0, 9, "Vs")
            vexp(Vs[:, 0:8, vo:vo + 64], lambda c, j: ckvT_c[:, c, j * 128:(j + 1) * 128], 8, 128, "Vs", None)
            vexp(Vs[0:64, 8:9, vo:vo + 64], lambda c, j: ckvT_own[:, c, 2176:2240], 1, 64, "Vs", None)
        th.append((-1, vs_th))

        def qproj(a, n):
            for v in range(2):
                for kc in range(8):
                    kb.emit("pe", lambda e, v=v, kc=kc: e.matmul(bank(v, n, 0, 96), wq[:, kc, v, :],
                                                                 xnT[:, kc, 512 + a:512 + a + n],
                                                                 start=(kc == 0), stop=(kc == 7)),
                            reads=[kwq] + [("xnT", tt_) for tt_ in range((512 + a) // 128, (512 + a + n + 127) // 128)],
                            writes=PK(v))
            kb.emit("dve", lambda e: e.tensor_copy(qT[0:64, a:a + n], bank(0, n, 0, 64)),
                    reads=PK(0), writes=[("qT", h % 2, a // 512)])
            r1, k1 = rt.next()
            r2, k2 = rt.next()
            qtab, kqt = qtr.next()
            kb.dma("sp", qtab[64:96, 0, 0:n], qtab_d[0][:, a:a + n], writes=[kqt])
            kb.dma("sp", qtab[64:96, 1, 0:n], qtab_d[1][:, a:a + n], writes=[kqt])
            kb.emit("dve", lambda e: e.tensor_tensor(r1[64:96, 0:n], bank(0, n, 64, 96), qtab[64:96, 0, 0:n], ALU.mult),
                    reads=PK(0) + [kqt], writes=[k1])
            kb.emit("dve", lambda e: e.tensor_tensor(r2[64:96, 0:n], bank(1, n, 64, 96), qtab[64:96, 1, 0:n], ALU.mult),
                    reads=PK(1) + [kqt], writes=[k2])
            kb.emit("dve", lambda e: e.tensor_tensor(qT[64:96, a:a + n], r1[64:96, 0:n], r2[64:96, 0:n], ALU.add),
                    reads=[k1, k2], writes=[("qT", h % 2, a // 512)])
        for (a, n) in [(0, 512), (512, 512), (1024, 512), (1536, 512), (2048, 192)]:
            th.append((-1, lambda a=a, n=n: qproj(a, n)))
        return th

    def mla_attn(h, nxt):
        par = h % 2
        vo = 0 if par == 0 else 64
        qT = qTs[h % 2]
        qk = [("qT", h % 2, i) for i in range(5)]
        kvt = [dict(K=Kp[0:96, t * 128:(t + 1) * 128], V=Vp[:, t, :], nk=128, bias=kvb[t // 16],
                    kkeys=[("Kp", t // 4)], vkeys=[("Vp", t // 8)]) for t in range(48)]

        def own_t(t, c0, half):
            return dict(K=Kp[0:96, NKV + t * 128:NKV + (t + 1) * 128], V=Vp[:, 48 + t, :], nk=128, bias=None, c0=c0, half=half,
                        kkeys=[("Kp", 12 + t // 4)], vkeys=[("Vp", 6 + t // 8)])
        attend(pc, qT[0:96, 0:128], 96, 128, kvt, MLA_SCALE, ybT[vo:vo + 64, h // 2, 0:128], par, PTr, recr,
               [("ybT", h, 2)], qk)
        kts = [dict(K=Ks[0:96, t * 128:(t + 1) * 128], V=Vs[:, t, :], nk=128, bias=None,
                    kkeys=[("Ks", t // 4)], vkeys=["Vs"]) for t in range(8)]
        kts.append(dict(K=Ks[0:96, 1024:1088], V=Vs[0:64, 8, :], nk=64, bias=None, kkeys=[("Ks", 2)], vkeys=["Vs"]))
        attend(pc, qT[0:96, 2176:2240], 96, 64, kts, MLA_SCALE, ybT[vo:vo + 64, h // 2, 2176:2240], par, PTr, recr,
               [("ybT", h, 3)], qk)
        kt0 = kvt + [own_t(t, 128 * t, "lo") for t in range(8)]
        attend(pc, qT[0:96, 128:1152], 96, 1024, kt0, MLA_SCALE, ybT[vo:vo + 64, h // 2, 128:1152], par, PTr, recr,
               [("ybT", h, 0)], qk)
        kt1 = kvt + [own_t(t, 0, None) for t in range(8)] + [own_t(8 + t, 128 * t, "lo") for t in range(8)]
        pend = sorted(nxt, key=lambda x: x[0])

        def hook(ti):
            k = 0
            while pend and pend[0][0] <= ti and k < 2:
                pend.pop(0)[1]()
                k += 1
        attend(pc, qT[0:96, 1152:2176], 96, 1024, kt1, MLA_SCALE, ybT[vo:vo + 64, h // 2, 1152:2176], par, PTr, recr,
               [("ybT", h, 1)], qk, hook=hook)
        while pend:
            pend.pop(0)[1]()

    for (_, fn_) in mla_prep(0):
        fn_()
    for h_ in range(8):
        mla_attn(h_, mla_prep(h_ + 1) if h_ < 7 else [])
    kb.barrier()
    pc_.close()
    lat.close()
    if stop == "C":
        kb.finalize()
        return nc
    yas = ExitStack()
    yaT = sb("yaT", [128, 4, NQ], BF16, yas)

    pd = ExitStack()
    wab = Ring([sb("wab%d" % i, [128, 8, 2, 64], BF16, pd) for i in range(2)], "wab")
    wkv = sb("wkv", [128, 8, 1024], BF16, pd)
    va_all = sb("va_all", [128, 22, 512], BF16, pd)
    cav_sb = sb("cav_sb", [128, 4, 512], BF16, pd)
    kaTc = sb("kaTc", [64, 8, 512], BF16, pd)
    qaT = sb("qaT", [64, NALL], BF16, pd)
    kaT = sb("kaT", [64, NALL], BF16, pd)
    Vb = sb("Vb", [128, 22, 128], BF16, pd)
    Vbc = sb("Vbc", [128, 4, 128], BF16, pd)
    relb_t = sb("relb_t", [128, 5, 128], F32, pd)
    PTb = Ring([sb("PTb%d" % i, [128, 512], BF16, pd) for i in range(6)], "PTb")
    recb = Ring([sb("recb%d" % i, [128, 128], F32, pd) for i in range(3)], "recb")
    tmpr = Ring([sb("tmpb%d" % i, [128, 512], F32, pd) for i in range(6)], "tmpb")
    stg = Ring([sb("stg%d" % i, [128, 1024], F32, pd) for i in range(2)], "stg")
    ones_b = sb("ones_b", [128, 64], BF16, pd)
    kb.dma("pool", ones_b, ones_d, writes=["ones"])
    cvr = Ring([sb("cv%d" % i, [128, 2048], BF16, pd) for i in range(3)], "cv")

    def convert_ffn(fcs):
        for fc in fcs:
            cv, kcv = cvr.next()
            kb.dma("pool", cv, w_fgu[fc].rearrange("p a b c -> p (a b c)"), writes=[kcv])
            kb.dma("pool", wgu_bf[fc], cv, reads=[kcv])
            cv, kcv = cvr.next()
            kb.dma("pool", cv[:, 0:1024], w_fd[fc * 128:(fc + 1) * 128, :], writes=[kcv])
            kb.dma("pool", wd_bf[fc], cv[:, 0:1024], reads=[kcv])

    kb.dma("pool", wkv, w_kv.rearrange("(kc p) n -> p kc n", p=128), writes=["wkv"])
    kb.dma("pool", cav_sb, cav.rearrange("(t p) n -> p t n", p=128), writes=["cav_sb"])
    pcb = {"tmpring": tmpr}
    kb.emit("pool", lambda e: e.memset(va_all[:, 21, :], 0.0), writes=[("va_all", 21)])
    for ti, (t0, n) in enumerate(tiles_all):
        for hf in range(2):
            for kc in range(8):
                kb.emit("pe", lambda e, hf=hf, kc=kc, t0=t0, n=n: e.matmul(bank(hf, 512, 0, n), xnT[:, kc, t0:t0 + n],
                                                                          wkv[:, kc, hf * 512:(hf + 1) * 512],
                                                                          start=(kc == 0), stop=(kc == 7)),
                        reads=["wkv", ("xnT", ti)], writes=PK(hf))
        kb.emit("act", lambda e, ti=ti, n=n: e.activation(va_all[0:n, ti, :], bank(1, 512, 0, n), AF.Copy),
                reads=PK(1), writes=[("va_all", ti)])
        if 2176 <= t0 < 2688 or t0 == 2688:
            s_, ks_ = stg.next()
            kb.emit("dve", lambda e, s_=s_, n=n: e.tensor_copy(s_[0:n, :], ps[0:n, 0:1024]), reads=PK(0, 1), writes=[ks_])
            if t0 < 2688:
                kb.dma("sp", o_kav[t0 - 2176:t0 - 2176 + n, :], s_[0:n, :], reads=[ks_], final=True)
            else:
                kb.dma("sp", o_sk[448:512, :], s_[0:64, 0:512], reads=[ks_], final=True)
                kb.dma("sp", o_sv[448:512, :], s_[0:64, 512:1024], reads=[ks_], final=True)
    kb.dma("sp", o_sk[0:448, :], cak[64:512, :], final=True)
    kb.dma("sp", o_sv[0:448, :], cav[64:512, :], final=True)
    for t in range(4):
        s_, ks_ = stg.next()
        kb.dma("sp", s_[:, 0:512], cak[t * 128:(t + 1) * 128, :], writes=[ks_])
        for hh in range(8):
            kb.emit("pe", lambda e, hh=hh, s_=s_: e.transpose(ps[0:64, hh * 128:(hh + 1) * 128], s_[:, hh * 64:(hh + 1) * 64], ident),
                    reads=[ks_, "ident"], writes=PK(0, 1))
        src = ps[0:64, 0:1024].rearrange("p (k t) -> p k t", k=8)
        kb.emit("act", lambda e, t=t, src=src: e.activation(kaTc[:, :, t * 128:(t + 1) * 128], src, AF.Copy),
                reads=PK(0, 1), writes=["kaTc"])

    bb0 = bandbias[:, 0:1]

    def band_head(h):
        par = h % 2
        vo = 0 if par == 0 else 64
        on = 64 if par == 0 else 0
        w2, kw2 = wab.next()
        kb.dma("pool", w2, w_qk[h], writes=[kw2])
        kb.dma("sp", relb_t, relb[h], writes=["relb"])
        for (a, n) in [(0, 512), (512, 512), (1024, 512), (1536, 512), (2048, 512), (2560, 192)]:
            for v in range(2):
                for kc in range(8):
                    kb.emit("pe", lambda e, v=v, kc=kc, a=a, n=n: e.matmul(bank(v, n, 0, 64), w2[:, kc, v, :], xnT[:, kc, a:a + n],
                                                                           start=(kc == 0), stop=(kc == 7)),
                            reads=[kw2] + [("xnT", tt_) for tt_ in range(a // 128, (a + n + 127) // 128)], writes=PK(v))
            kb.emit("act", lambda e, a=a, n=n: e.activation(qaT[:, a:a + n], bank(0, n, 0, 64), AF.Copy), reads=PK(0), writes=["qaT"])
            kb.emit("dve", lambda e, a=a, n=n: e.tensor_copy(kaT[:, a:a + n], bank(1, n, 0, 64)), reads=PK(1), writes=["kaT"])
        kb.emit("pool", lambda e: e.tensor_copy(Vb[:, :, vo:vo + 64], va_all[:, :, h * 64:(h + 1) * 64]),
                reads=[("va_all", i) for i in range(22)], writes=["Vb"])
        for t in range(22):
            kb.emit("pool", lambda e, t=t: e.tensor_copy(Vb[:, t, on:on + 64], ones_b), reads=["ones"], writes=["Vb"])
        kb.emit("pool", lambda e: e.tensor_copy(Vbc[:, :, vo:vo + 64], cav_sb[:, :, h * 64:(h + 1) * 64]), reads=["cav_sb"], writes=["Vbc"])
        for t in range(4):
            kb.emit("pool", lambda e, t=t: e.tensor_copy(Vbc[:, t, on:on + 64], ones_b), reads=["ones"], writes=["Vbc"])
        for m in range(17):
            kts = []
            for r in (1, 2, 3, 4, 0):
                t = m + r
                d = dict(K=kaT[:, t * 128:(t + 1) * 128], V=Vb[:, t, :], nk=128, rb=r,
                         bias=(bb0 if t < 5 else None), kkeys=["kaT"], vkeys=["Vb"])
                if r == 0:
                    d["half2"] = (64, 128, 64, 128)
                if r == 4:
                    d["half2"] = (0, 64, 0, 64)
                kts.append(d)
            fin = attend(pcb, qaT[:, 512 + 128 * m:640 + 128 * m], 64, 128, kts, A_SCALE,
                         yaT[vo:vo + 64, h // 2, 128 * m:128 * m + 128], par, PTb, recb, [("yaT", h, m)], ["qaT"], relb_t=relb_t, defer=True)
            if prev[0] is not None:
                prev[0]()
            prev[0] = fin
        kts = [dict(K=kaTc[:, h, t * 128:(t + 1) * 128], V=Vbc[:, t, :], nk=128, rb=t, bias=None, kkeys=["kaTc"], vkeys=["Vbc"])
               for t in range(4)]
        kts.append(dict(K=kaT[:, 2688:2752], V=Vb[0:64, 21, :], nk=64, rb=4, bias=None, kkeys=["kaT"], vkeys=["Vb"]))
        fin = attend(pcb, qaT[:, 2688:2752], 64, 64, kts, A_SCALE, yaT[vo:vo + 64, h // 2, 2176:2240], par, PTb, recb,
                     [("yaT", h, 17)], ["qaT"], relb_t=relb_t, defer=True)
        prev[0]()
        prev[0] = fin
        convert_ffn(range(3 * h, min(NFC, 3 * h + 3)))
    prev = [None]
    for h_ in range(8):
        band_head(h_)
    prev[0]()
    kb.barrier()
    pd.close()
    if stop == "D":
        kb.finalize()
        return nc
    load_nrm(0, 1)

    pe_ = ExitStack()
    wa = sb("wa", [128, 4, 1024], BF16, pe_)
    wb = sb("wb", [128, 4, 1024], BF16, pe_)
    wo = sb("wo", [128, 8, 1024], BF16, pe_)
    wg = sb("wg", [128, 8, 2048], BF16, pe_)
    mT = sb("mT", [128, 8, 512], BF16, pe_)
    sgr = Ring([sb("sg%d" % i, [128, 512], F32, pe_) for i in range(4)], "sg")
    xr2 = Ring([sb("x2r%d" % i, [128, 1024], F32, pe_) for i in range(2)], "x2r")
    mr = Ring([sb("mr%d" % i, [128, 1024], F32, pe_) for i in range(2)], "mr")
    jr2 = Ring([sb("j2r%d" % i, [128, 1024], BF16, pe_) for i in range(2)], "j2r")
    sr2 = Ring([sb("s2r%d" % i, [128, 4], F32, pe_) for i in range(4)], "s2r")
    kb.dma("pool", wa, w_ba.rearrange("(c p) n -> p c n", p=128), writes=["wa"])
    kb.dma("pool", wb, w_bb.rearrange("(c p) n -> p c n", p=128), writes=["wb"])
    kb.dma("pool", wo, w_out.rearrange("(c p) n -> p c n", p=128), writes=["wo"])
    kb.dma("pool", wg, w_g.rearrange("(c p) n -> p c n", p=128), writes=["wg"])
    qblocks = [(448 * i, 448) for i in range(5)]
    for (q0, n) in qblocks:
        for fc in range(8):
            for p in range(4):
                kb.emit("pe", lambda e, p=p, fc=fc, q0=q0, n=n: e.matmul(bank(0, n), wa[:, p, fc * 128:(fc + 1) * 128], yaT[:, p, q0:q0 + n],
                                                                         start=(p == 0), stop=(p == 3)), reads=["wa"], writes=PK(0))
            for p in range(4):
                kb.emit("pe", lambda e, p=p, fc=fc, q0=q0, n=n: e.matmul(bank(1, n), wb[:, p, fc * 128:(fc + 1) * 128], ybT[:, p, q0:q0 + n],
                                                                         start=(p == 0), stop=(p == 3)), reads=["wb"], writes=PK(1))
            for g in range(2):
                for kc in range(8):
                    kb.emit("pe", lambda e, g=g, kc=kc, fc=fc, q0=q0, n=n: e.matmul(
                        bank(2 + g, n), wg[:, kc, g * 1024 + fc * 128:g * 1024 + (fc + 1) * 128], xnT[:, kc, 512 + q0:512 + q0 + n],
                        start=(kc == 0), stop=(kc == 7)), reads=["wg"], writes=PK(2 + g))
            sa, ksa = sgr.next()
            sbb, ksb = sgr.next()
            kb.emit("act", lambda e, sa=sa, n=n: e.activation(sa[:, 0:n], bank(2, n), AF.Sigmoid), reads=PK(2), writes=[ksa])
            kb.emit("act", lambda e, sbb=sbb, n=n: e.activation(sbb[:, 0:n], bank(3, n), AF.Sigmoid), reads=PK(3), writes=[ksb])
            kb.emit("dve", lambda e, sa=sa, n=n: e.tensor_tensor(sa[:, 0:n], sa[:, 0:n], bank(0, n), ALU.mult), reads=PK(0) + [ksa], writes=[ksa])
            kb.emit("dve", lambda e, sbb=sbb, n=n: e.tensor_tensor(sbb[:, 0:n], sbb[:, 0:n], bank(1, n), ALU.mult), reads=PK(1) + [ksb], writes=[ksb])
            kb.emit("dve", lambda e, sa=sa, sbb=sbb, fc=fc, n=n: e.tensor_tensor(mT[:, fc, 0:n], sa[:, 0:n], sbb[:, 0:n], ALU.add),
                    reads=[ksa, ksb], writes=[("mT", fc)])
        for tt_ in range((n + 127) // 128):
            nn = min(128, n - tt_ * 128)
            for hf in range(2):
                for fc in range(8):
                    kb.emit("pe", lambda e, hf=hf, fc=fc, tt_=tt_, nn=nn: e.matmul(bank(4 + hf, 512, 0, nn), mT[:, fc, tt_ * 128:tt_ * 128 + nn],
                                                                                 wo[:, fc, hf * 512:(hf + 1) * 512], start=(fc == 0), stop=(fc == 7)),
                            reads=["wo"] + [("mT", f_) for f_ in range(8)], writes=PK(4 + hf))
            xt, kx = xr2.next()
            mx, kmx = mr.next()
            junk, kj = jr2.next()
            st, kst = sr2.next()
            r0 = 512 + q0 + tt_ * 128
            kb.dma("sp", xt[0:nn], x_all[r0:r0 + nn, :], writes=[kx])
            kb.emit("act", lambda e, mx=mx, nn=nn: e.activation(mx[0:nn], ps[0:nn, 2048:3072], AF.Copy), reads=PK(4, 5), writes=[kmx])
            kb.emit("act", lambda e, mx=mx, junk=junk, st=st, nn=nn: e.activation(junk[0:nn], mx[0:nn], AF.Square, accum_out=st[0:nn, 0:1]),
                    reads=[kmx], writes=[kj, kst])
            rstd_from_ss(st, nn, 1.0 / 1024, [kst])
            kb.emit("dve", lambda e, mx=mx, st=st, nn=nn: e.scalar_tensor_tensor(mx[0:nn], mx[0:nn], st[0:nn, 2:3], nrm_rep[0:nn, NSLOT[1], :], ALU.mult, ALU.mult),
                    reads=[kmx, kst, "nrm_rep"], writes=[kmx])
            kb.emit("dve", lambda e, mx=mx, xt=xt, nn=nn: e.tensor_tensor(mx[0:nn], mx[0:nn], xt[0:nn], ALU.add), reads=[kmx, kx], writes=[kmx])
            kb.dma("sp", x1_d[q0 + tt_ * 128:q0 + tt_ * 128 + nn, :], mx[0:nn], reads=[kmx], writes=[("x1d", (q0 + tt_ * 128) // 64)])
    kb.barrier()
    pe_.close()
    yas.close()
    xs.close()
    if stop == "E":
        kb.finalize()
        return nc
    load_nrm(0, 2)
    load_nrm(1, 3)

    pf = ExitStack()
    NB = len(qblocks)
    x1ra = Ring([sb("x1ra%d" % i, [128, 1024], F32, pf) for i in range(3)], "x1ra")
    x1rb = Ring([sb("x1rb%d" % i, [128, 1024], F32, pf) for i in range(2)], "x1rb")
    xn2Ts = [sb("xn2T%d" % k, [128, 8, 512], BF16, pf) for k in range(2)]
    hTs = [sb("hT%d" % k, [128, NFC, 512], BF16, pf) for k in range(2)]
    mxs_ = [[sb("mx%d_%d" % (k, i), [128, 1024], F32, pf) for i in range(4)] for k in range(2)]
    gtail = sb("gtail", [128, 2, NFC], F32, pf)
    stail = sb("stail", [128, 2, NFC], F32, pf)
    cw = sb("cw", [128, 4, NFC], F32, pf)
    wgr = Ring([sb("wfg%d" % i, [128, 8, 2, 128], BF16, pf) for i in range(4)], "wfg")
    wdr = Ring([sb("wfd%d" % i, [128, 512], BF16, pf) for i in range(8)], "wfd")
    gbr = Ring([sb("gb%d" % i, [128, 516], F32, pf) for i in range(6)], "gb")
    cbr = Ring([sb("cb%d" % i, [128, 512], F32, pf) for i in range(4)], "cb")
    jr3 = Ring([sb("j3r%d" % i, [128, 1024], BF16, pf) for i in range(2)], "j3r")
    sr3 = Ring([sb("s3r%d" % i, [128, 4], F32, pf) for i in range(4)], "s3r")
    sr3b = Ring([sb("s3rb%d" % i, [128, 4], F32, pf) for i in range(4)], "s3rb")
    prr = Ring([sb("prr%d" % i, [128, 1024], F32, pf) for i in range(2)], "prr")
    kb.dma("sp", cw, convw, writes=["cw"])
    kb.emit("pool", lambda e: e.memset(gtail, 0.0), writes=["gtail"])
    kb.dma("sp", stail, sconv, writes=["stail"])

    def tiles_of(b):
        q0, n = qblocks[b]
        return [(tt_, q0 + tt_ * 128, min(128, n - tt_ * 128)) for tt_ in range((n + 127) // 128)]

    def pipe_rounds(gens, depth):
        active = []
        it = iter(gens)
        done = False
        while True:
            if not done and len(active) < depth:
                try:
                    active.append(next(it))
                except StopIteration:
                    done = True
            if not active:
                if done:
                    return
                continue
            for g in list(active):
                try:
                    next(g)
                except StopIteration:
                    active.remove(g)
            yield

    def pre_tile(b, tt_, r0, nn):
        k = b % 2
        xt, kx = x1ra.next()
        kb.dma("sp", xt[0:nn], x1_d[r0:r0 + nn, :], writes=[kx])
        junk, kj = jr3.next()
        st, kst = sr3.next()
        mx, kmx = prr.next()
        kb.emit("act", lambda e: e.activation(junk[0:nn], xt[0:nn], AF.Square, accum_out=st[0:nn, 0:1]),
                reads=[kx], writes=[kj, kst])
        yield
        yield from rstd_gen(st, nn, 1.0 / 1024, [kst])
        kb.emit("dve", lambda e: e.scalar_tensor_tensor(mx[0:nn], xt[0:nn], st[0:nn, 2:3], nrm_rep[0:nn, NSLOT[2], :], ALU.mult, ALU.mult),
                reads=[kx, kst, "nrm_rep"], writes=[kmx])
        yield
        for kc in range(8):
            kb.emit("pe", lambda e, kc=kc: e.transpose(bank(0, nn, off=kc * 128) if kc < 4 else bank(1, nn, off=(kc - 4) * 128),
                                                      mx[0:nn, kc * 128:(kc + 1) * 128], ident[0:nn, 0:nn]),
                    reads=[kmx, "ident"], writes=PK(0, 1))
        yield
        src = ps[:, 0:1024].rearrange("p (k t) -> p k t", k=8)[:, :, 0:nn]
        kb.emit("act", lambda e: e.activation(xn2Ts[k][:, :, tt_ * 128:tt_ * 128 + nn], src, AF.Copy),
                reads=PK(0, 1), writes=[("xn2T", k, tt_)])
        yield

    def ffn_pre(b):
        return pipe_rounds([pre_tile(b, tt_, r0, nn) for (tt_, r0, nn) in tiles_of(b)], 2)

    cbs = {}

    def stage_a(b, fc):
        q0, n = qblocks[b]
        k = b % 2
        xn2T = xn2Ts[k]
        xk = [("xn2T", k, t_[0]) for t_ in tiles_of(b)]
        segs = [(0, n, gtail, "gtail")] if b < NB - 1 else [(0, n - 64, gtail, "gtail"), (n - 64, 64, stail, "stail")]
        wf, kwf = wgr.next()
        kb.dma("sp", wf.rearrange("p a b c -> p (a b c)"), wgu_bf[fc], writes=[kwf])
        pb = 2 + 2 * (fc % 2)
        for v in range(2):
            for kc in range(8):
                kb.emit("pe", lambda e, v=v, kc=kc: e.matmul(bank(pb + v, n), wf[:, kc, v, :], xn2T[:, kc, 0:n],
                                                             start=(kc == 0), stop=(kc == 7)),
                        reads=[kwf] + xk, writes=PK(pb + v))
        cb_, kcb = cbr.next()
        cbs[(b, fc)] = (cb_, kcb, pb)
        for (c0, ns, tl, tlk) in segs:
            gb_, kgb = gbr.next()
            kb.emit("act", lambda e, gb_=gb_, c0=c0, ns=ns: e.activation(gb_[:, 2:2 + ns], bank(pb, ns, off=c0), AF.Copy), reads=PK(pb), writes=[kgb])
            kb.emit("dve", lambda e, gb_=gb_, tl=tl: e.tensor_copy(gb_[:, 0:2], tl[:, :, fc]), reads=[tlk], writes=[(kgb, "t")])
            kb.emit("pool", lambda e, gb_=gb_, c0=c0, ns=ns: e.tensor_scalar(cb_[:, c0:c0 + ns], gb_[:, 2:2 + ns], cw[:, 2, fc:fc + 1], cw[:, 3, fc:fc + 1], ALU.mult, ALU.add),
                    reads=[kgb, "cw"], writes=[kcb])
            kb.emit("dve", lambda e, gb_=gb_, c0=c0, ns=ns: e.scalar_tensor_tensor(cb_[:, c0:c0 + ns], gb_[:, 1:1 + ns], cw[:, 1, fc:fc + 1], cb_[:, c0:c0 + ns], ALU.mult, ALU.add),
                    reads=[kgb, (kgb, "t"), "cw", kcb], writes=[kcb])
            kb.emit("dve", lambda e, gb_=gb_, c0=c0, ns=ns: e.scalar_tensor_tensor(cb_[:, c0:c0 + ns], gb_[:, 0:ns], cw[:, 0, fc:fc + 1], cb_[:, c0:c0 + ns], ALU.mult, ALU.add),
                    reads=[kgb, (kgb, "t"), "cw", kcb], writes=[kcb])
            kb.emit("dve", lambda e, gb_=gb_, ns=ns, tl=tl: e.tensor_copy(tl[:, :, fc], gb_[:, ns:ns + 2]), reads=[kgb, (kgb, "t")], writes=[tlk])

    def stage_b(b, fc):
        q0, n = qblocks[b]
        hT = hTs[b % 2]
        cb_, kcb, pb = cbs.pop((b, fc))
        kb.emit("act", lambda e: e.activation(cb_[:, 0:n], cb_[:, 0:n], AF.Gelu_apprx_tanh), reads=[kcb], writes=[kcb])
        kb.emit("dve", lambda e: e.tensor_tensor(hT[:, fc, 0:n], cb_[:, 0:n], bank(pb + 1, n), ALU.mult),
                reads=[kcb] + PK(pb + 1), writes=[("hT", b % 2, fc)])

    def down_pieces(b):
        hT = hTs[b % 2]
        tl = tiles_of(b)
        for ps_ in range(4):
            hf, pair = ps_ // 2, ps_ % 2
            tls = [t_ for t_ in tl if t_[0] // 2 == pair]
            if not tls:
                continue
            for fc in range(NFC):
                wd_, kwd = wdr.next()
                kb.dma("sp", wd_, wd_bf[fc][:, hf * 512:(hf + 1) * 512], writes=[kwd])
                for (tt_, r0, nn) in tls:
                    kb.emit("pe", lambda e, fc=fc, tt_=tt_, nn=nn, wd_=wd_: e.matmul(bank(6 + tt_ % 2, 512, 0, nn), hT[:, fc, tt_ * 128:tt_ * 128 + nn],
                                                                                 wd_, start=(fc == 0), stop=(fc == NFC - 1)),
                            reads=[kwd, ("hT", b % 2, fc)], writes=PK(6 + tt_ % 2))
                if fc == NFC - 1:
                    for (tt_, r0, nn) in tls:
                        mx = mxs_[b % 2][tt_]
                        kb.emit("act", lambda e, mx=mx, nn=nn, tt_=tt_, hf=hf: e.activation(mx[0:nn, hf * 512:(hf + 1) * 512], bank(6 + tt_ % 2, 512, 0, nn), AF.Copy),
                                reads=PK(6 + tt_ % 2), writes=[("mx", b % 2, tt_)])
                yield

    def post_tile(b, tt_, r0, nn):
        k = b % 2
        mx, kmx = mxs_[k][tt_], ("mx", k, tt_)
        junk, kj = jr3.next()
        st, kst = sr3b.next()
        xt, kx = x1rb.next()
        kb.dma("sp", xt[0:nn], x1_d[r0:r0 + nn, :], writes=[kx])
        kb.emit("act", lambda e: e.activation(junk[0:nn], mx[0:nn], AF.Square, accum_out=st[0:nn, 0:1]),
                reads=[kmx], writes=[kj, kst])
        yield
        yield from rstd_gen(st, nn, 1.0 / 1024, [kst])
        kb.emit("dve", lambda e: e.scalar_tensor_tensor(mx[0:nn], mx[0:nn], st[0:nn, 2:3], nrm_rep[0:nn, NSLOT[3], :], ALU.mult, ALU.mult),
                reads=[kmx, kst, "nrm_rep"], writes=[kmx])
        yield
        kb.emit("dve", lambda e: e.tensor_tensor(mx[0:nn], mx[0:nn], xt[0:nn], ALU.add), reads=[kmx, kx], writes=[kmx])
        yield
        if 128 <= r0 < 2176:
            kb.dma("sp", y_own[r0 - 128:r0 - 128 + nn, :], mx[0:nn], reads=[kmx], final=True)
        elif r0 >= 2176:
            kb.dma("sp", y_smp[0:nn, :], mx[0:nn], reads=[kmx], final=True)
        yield

    def ffn_post(b):
        return pipe_rounds([post_tile(b, tt_, r0, nn) for (tt_, r0, nn) in tiles_of(b)], 2)

    def drain(g, k):
        for _ in range(k):
            if next(g, "end") == "end":
                return True
        return False

    def flush(g):
        if g is not None:
            for _ in g:
                pass

    flush(ffn_pre(0))
    dgen = None
    pregen = None
    postgen = None
    for b in range(NB):
        stage_a(b, 0)
        for fc in range(NFC):
            if fc + 1 < NFC:
                stage_a(b, fc + 1)
            stage_b(b, fc)
            if fc == 2 and b + 1 < NB:
                pregen = ffn_pre(b + 1)
            if pregen is not None and drain(pregen, 2):
                pregen = None
            if postgen is not None and drain(postgen, 1):
                postgen = None
            if dgen is not None and drain(dgen, 4):
                dgen = None
        flush(pregen)
        pregen = None
        if dgen is not None:
            flush(dgen)
        flush(postgen)
        postgen = ffn_post(b - 1) if b >= 1 else None
        if b == NB - 1:
            kb.dma("sp", o_conv, gtail, reads=["gtail"], final=True)
            kb.dma("sp", o_sconv, stail, reads=["stail"], final=True)
        dgen = down_pieces(b)
    flush(postgen)
    flush(dgen)
    flush(ffn_post(NB - 1))
    pf.close()
    kb.finalize()
    es.close()
    return nc


_NC = None


def kernel(x_prompt, x_sample, cache_a_k, cache_a_v, cache_mla_ckv, cache_mla_krope, state_ffn_conv,
           norm_mix_pre, norm_mix_post, w_in, rel_bias_table, kv_norm, w_uk, w_uv, w_branch_a, w_branch_b,
           w_out, norm_ffn_pre, norm_ffn_post, w_ffn_gate, w_ffn_up, conv_w, conv_b, w_ffn_down):
    global _NC
    f = lambda a: np.ascontiguousarray(np.asarray(a, dtype=np.float32))
    x_prompt, x_sample = f(x_prompt), f(x_sample)
    W = f(w_in)[0]
    qa, ka, va, qn, qr, ck, kr_, ga, gb = 0, 512, 1024, 1536, 2048, 2304, 2560, 2592, 3616
    wq = np.zeros((1024, 8, 96), np.float32)
    wq2 = np.zeros((1024, 8, 96), np.float32)
    for h in range(8):
        wq[:, h, 0:64] = W[:, qn + 64 * h:qn + 64 * (h + 1)]
        wq[:, h, 64:96] = W[:, qr + 32 * h:qr + 32 * (h + 1)]
        wq2[:, h, 0:64] = W[:, qn + 64 * h:qn + 64 * (h + 1)]
        wq2[:, h, 64:80] = W[:, qr + 32 * h + 16:qr + 32 * h + 32]
        wq2[:, h, 80:96] = W[:, qr + 32 * h:qr + 32 * h + 16]
    half = 16
    inv = (10000.0 ** (-np.arange(half, dtype=np.float32) / half)).astype(np.float32)

    def cs_tab(pos):
        ang = pos.astype(np.float32)[:, None] * inv[None, :]
        return np.cos(ang).astype(np.float32), np.sin(ang).astype(np.float32)

    tbl = f(rel_bias_table)[0]
    kt_, ki_, qi_ = np.meshgrid(np.arange(5), np.arange(128), np.arange(128), indexing="ij")
    didx = np.clip(512 + qi_ - 128 * kt_ - ki_, -256, 256) + 256
    relb = np.ascontiguousarray(tbl[:, didx].transpose(0, 2, 1, 3))
    nrm = np.zeros((5, 1024), np.float32)
    nrm[0], nrm[1], nrm[2], nrm[3] = f(norm_mix_pre)[0], f(norm_mix_post)[0], f(norm_ffn_pre)[0], f(norm_ffn_post)[0]
    nrm[4, :256] = f(kv_norm)[0]
    nrm_b = np.ascontiguousarray(np.broadcast_to(nrm[None], (128, 5, 1024)))
    cwl = np.ascontiguousarray(np.concatenate([f(conv_w)[0], f(conv_b)], axis=0).reshape(4, NFC, 128).transpose(2, 0, 1))
    common = dict(
        ident=np.eye(128, dtype=np.float32), ones=np.ones((128, 64), np.float32),
        w_c=np.ascontiguousarray(W[:, ck:ga]),
        w_qq=np.ascontiguousarray(np.stack([wq, wq2], axis=2).reshape(8, 128, 8, 2, 96).transpose(2, 1, 0, 3, 4)),
        w_qk=np.ascontiguousarray(np.stack([W[:, qa:ka].reshape(1024, 8, 64), W[:, ka:va].reshape(1024, 8, 64)], axis=2)
                                  .reshape(8, 128, 8, 2, 64).transpose(2, 1, 0, 3, 4)),
        w_kv=np.ascontiguousarray(W[:, ka:qn]), w_g=np.ascontiguousarray(W[:, ga:]),
        relb=relb, w_uk=f(w_uk)[0].reshape(256, 512), w_uv=f(w_uv)[0].reshape(256, 512),
        w_ba=f(w_branch_a)[0], w_bb=f(w_branch_b)[0], w_out=f(w_out)[0],
        w_fgu=np.ascontiguousarray(np.stack([f(w_ffn_gate)[0].reshape(8, 128, NFC, 128), f(w_ffn_up)[0].reshape(8, 128, NFC, 128)], axis=3)
                                   .transpose(2, 1, 0, 3, 4)),
        w_fd=f(w_ffn_down)[0],
        convw=cwl, nrm=nrm_b,
    )
    in_maps = []
    for c in range(8):
        b, j = c // 4, c % 4
        s0 = 2048 * j
        x_all = np.zeros((NALL, 1024), np.float32)
        lo = s0 - 640
        if lo >= 0:
            x_all[0:640] = x_prompt[b, lo:s0]
        x_all[640:2688] = x_prompt[b, s0:s0 + 2048]
        x_all[2688:] = x_sample[c]
        pos_all = np.concatenate([np.arange(lo, s0 + 2048), 1024 + np.arange(64)])
        cc, ss = cs_tab(pos_all)
        cs_all = np.concatenate([cc, ss], axis=1)
        blocks = [v if v < j else 0 for v in range(3)]
        x_kv = np.concatenate([x_prompt[b, 2048 * v:2048 * (v + 1)] for v in blocks], axis=0)
        pos_kv = np.concatenate([np.arange(2048 * v, 2048 * (v + 1)) for v in blocks])
        ck_, sk_ = cs_tab(pos_kv)
        cs_kv = np.concatenate([ck_, sk_], axis=1)
        posq = pos_all[512:]
        cq, sq = cs_tab(posq)
        qtab = np.zeros((2, 32, NQ), np.float32)
        qtab[0, 0:16], qtab[0, 16:32] = cq.T, cq.T
        qtab[1, 0:16], qtab[1, 16:32] = -sq.T, sq.T
        kvbias = np.zeros((128, 4), np.float32)
        for v in range(3):
            if v >= j:
                kvbias[:, v] = NEG
        bandbias = np.zeros((128, 2), np.float32)
        if j == 0:
            bandbias[:, 0] = NEG
        m = dict(common)
        m.update(x_all=x_all, x_kv=np.ascontiguousarray(x_kv), cs_all=cs_all, cs_kv=cs_kv, qtab=qtab,
                 kvbias=kvbias, bandbias=bandbias,
                 cak=f(cache_a_k)[0, c].reshape(512, 512), cav=f(cache_a_v)[0, c].reshape(512, 512),
                 cckv=f(cache_mla_ckv)[0, c], ckr=f(cache_mla_krope)[0, c], sconv=np.ascontiguousarray(f(state_ffn_conv)[0, c].reshape(2, NFC, 128).transpose(2, 0, 1)))
        in_maps.append(m)
    if _NC is None:
        _NC = build()
    res = run_bass_kernel_spmd(_NC, in_maps, core_ids=list(range(8))).results
    y_p = np.zeros((2, 8192, 1024), np.float32)
    ckv_p = np.zeros((1, 2, 8192, 256), np.float32)
    kr_p = np.zeros((1, 2, 8192, 32), np.float32)
    for c in range(8):
        b, j = c // 4, c % 4
        y_p[b, 2048 * j:2048 * (j + 1)] = res[c]["y_own"]
        ckv_p[0, b, 2048 * j:2048 * (j + 1)] = res[c]["o_ckv"]
        kr_p[0, b, 2048 * j:2048 * (j + 1)] = res[c]["o_kr"]
    y_s = np.stack([res[c]["y_smp"] for c in range(8)])
    ak_p = np.stack([res[4 * b + 3]["o_kav"][:, 0:512].reshape(512, 8, 64) for b in range(2)])[None]
    av_p = np.stack([res[4 * b + 3]["o_kav"][:, 512:1024].reshape(512, 8, 64) for b in range(2)])[None]
    conv_p = np.stack([res[4 * b + 3]["o_conv"].transpose(1, 2, 0).reshape(2, DFF) for b in range(2)])[None]
    sk = np.stack([res[c]["o_sk"].reshape(512, 8, 64) for c in range(8)])[None]
    sv = np.stack([res[c]["o_sv"].reshape(512, 8, 64) for c in range(8)])[None]
    sckv = np.stack([res[c]["o_sckv"] for c in range(8)])[None]
    skr = np.stack([res[c]["o_skr"] for c in range(8)])[None]
    sconv = np.stack([res[c]["o_sconv"].transpose(1, 2, 0).reshape(2, DFF) for c in range(8)])[None]
    return (y_p, y_s, ak_p, av_p, ckv_p, kr_p, conv_p, sk, sv, sckv, skr, sconv)
```

```python
import numpy as np
import concourse.bass as bass
import concourse.mybir as mybir
from concourse.bass_utils import run_bass_kernel_spmd
from contextlib import ExitStack

F32 = mybir.dt.float32
BF16 = mybir.dt.bfloat16
AF = mybir.ActivationFunctionType
ALU = mybir.AluOpType

EPS = 1e-6
A_SCALE = 64 ** -0.5
MLA_SCALE = 96 ** -0.5
NEG = -1.0e30
NALL = 2752
NQ = 2240
NKV = 6144
DFF = 2816
NFC = 22


class _Res:
    __slots__ = ("lw", "rd")

    def __init__(self):
        self.lw = None
        self.rd = []


class KB:
    ENG = ("pe", "act", "dve", "pool", "sp")

    def __init__(self, nc, n_dma_sems=20):
        self.nc = nc
        self.ops = []
        self.res = {}
        self.n_dma_sems = n_dma_sems
        self.out_dmas = []
        self.bar_deps = set()
        self.bar_pending = set()
        self.since_bar = {}

    def _r(self, k):
        r = self.res.get(k)
        if r is None:
            r = self.res[k] = _Res()
        return r

    def barrier(self):
        d = set(self.bar_deps)
        for e, i in self.since_bar.items():
            d.add(i)
        for i, op in enumerate(self.ops):
            if op["is_dma"] and i >= getattr(self, "_bar_pos", 0):
                d.add(i)
        self.bar_deps = set()
        last = {}
        for i in d:
            op = self.ops[i]
            if op["is_dma"]:
                if i >= getattr(self, "_bar_pos", 0):
                    self.bar_deps.add(i)
            else:
                last[op["eng"]] = max(last.get(op["eng"], -1), i)
        self.bar_deps.update(last.values())
        for i in self.bar_deps:
            self.ops[i]["sig"] = True
        self._bar_pos = len(self.ops)
        self.since_bar = {}
        self.res = {}
        self.bar_pending = set(self.ENG)

    def _add(self, eng, fn, reads, writes, is_dma):
        idx = len(self.ops)
        deps = set()
        if eng in self.bar_pending:
            deps.update(self.bar_deps)
            self.bar_pending.discard(eng)
        for k in reads:
            r = self._r(k)
            if r.lw is not None:
                deps.add(r.lw)
        for k in writes:
            r = self._r(k)
            if r.lw is not None:
                deps.add(r.lw)
            deps.update(r.rd)
        if eng == "pe" and not is_dma:
            deps = {d for d in deps if not (self.ops[d]["eng"] == "pe" and not self.ops[d]["is_dma"])}
        self.ops.append(dict(eng=eng, fn=fn, deps=deps, is_dma=is_dma, sig=is_dma))
        for d in deps:
            self.ops[d]["sig"] = True
        for k in reads:
            self._r(k).rd.append(idx)
        for k in writes:
            r = self._r(k)
            r.lw = idx
            r.rd = []
        if not is_dma:
            self.since_bar[eng] = idx
        return idx

    def emit(self, eng, fn, reads=(), writes=()):
        return self._add(eng, fn, list(reads), list(writes), False)

    def dma(self, eng, out, in_, reads=(), writes=(), final=False, **kw):
        def fn(e, out=out, in_=in_, kw=kw):
            return e.dma_start(out=out, in_=in_, **kw)
        i = self._add(eng, fn, list(reads), list(writes), True)
        if final:
            self.out_dmas.append(i)
        return i

    def finalize(self):
        nc = self.nc
        ops = self.ops
        esem = {e: nc.alloc_semaphore(name="s_" + e) for e in self.ENG}
        dsem = {e: [nc.alloc_semaphore(name="d_%s_%d" % (e, i)) for i in range(self.n_dma_sems)]
                for e in ("sp", "act", "pool")}
        ecnt = {e: 0 for e in self.ENG}
        dnext = {e: 0 for e in dsem}
        dval = {e: [0] * self.n_dma_sems for e in dsem}
        tok = {}
        prevtok = {}
        for i, op in enumerate(ops):
            if not op["sig"]:
                continue
            e = op["eng"]
            if op["is_dma"]:
                s = dnext[e] % self.n_dma_sems
                dnext[e] += 1
                if dval[e][s] > 0:
                    prevtok[i] = (("d", e, s), dval[e][s])
                dval[e][s] += 16
                tok[i] = (("d", e, s), dval[e][s])
            else:
                ecnt[e] += 1
                tok[i] = (("e", e), ecnt[e])
        seen = {e: {} for e in self.ENG}

        def semof(key):
            return esem[key[1]] if key[0] == "e" else dsem[key[1]][key[2]]

        streams = {e: [] for e in self.ENG}
        for i, op in enumerate(ops):
            streams[op["eng"]].append(i)

        def run(ename, engine):
            sn = seen[ename]
            for i in streams[ename]:
                op = ops[i]
                need = {}
                for d in op["deps"]:
                    k, v = tok[d]
                    if need.get(k, 0) < v:
                        need[k] = v
                if i in prevtok:
                    k, v = prevtok[i]
                    if need.get(k, 0) < v:
                        need[k] = v
                for k, v in need.items():
                    if sn.get(k, 0) < v:
                        engine.wait_ge(semof(k), v)
                        sn[k] = v
                ins = op["fn"](engine)
                if op["sig"]:
                    k, v = tok[i]
                    ins.then_inc(semof(k), 16 if op["is_dma"] else 1)
            if ename == "sp":
                for i in self.out_dmas:
                    k, v = tok[i]
                    if sn.get(k, 0) < v:
                        engine.wait_ge(semof(k), v)
                        sn[k] = v

        with nc.Block() as block:
            @block.tensor
            def _(e):
                run("pe", e)

            @block.scalar
            def _(e):
                run("act", e)

            @block.vector
            def _(e):
                run("dve", e)

            @block.gpsimd
            def _(e):
                run("pool", e)

            @block.sync
            def _(e):
                run("sp", e)


class Ring:
    def __init__(self, aps, name):
        self.aps = aps
        self.name = name
        self.i = 0

    def next(self):
        j = self.i % len(self.aps)
        self.i += 1
        return self.aps[j], (self.name, j)


def build(stop=None):
    nc = bass.Bass("TRN2", target_bir_lowering=False)
    kb = KB(nc)

    def din(name, shape):
        return nc.dram_tensor(name, shape, F32, kind="ExternalInput").ap()

    def dout(name, shape):
        return nc.dram_tensor(name, shape, F32, kind="ExternalOutput").ap()

    x_all = din("x_all", [NALL, 1024])
    x_kv = din("x_kv", [NKV, 1024])
    cs_all = din("cs_all", [NALL, 32])
    cs_kv = din("cs_kv", [NKV, 32])
    qtab_d = din("qtab", [2, 32, NQ])
    kvbias_d = din("kvbias", [128, 4])
    bandbias_d = din("bandbias", [128, 2])
    ident_d = din("ident", [128, 128])
    ones_d = din("ones", [128, 64])
    cak = din("cak", [512, 512])
    cav = din("cav", [512, 512])
    cckv = din("cckv", [1024, 256])
    ckr = din("ckr", [1024, 32])
    sconv = din("sconv", [128, 2, NFC])
    w_c = din("w_c", [1024, 288])
    w_qq = din("w_qq", [8, 128, 8, 2, 96])
    w_qk = din("w_qk", [8, 128, 8, 2, 64])
    w_kv = din("w_kv", [1024, 1024])
    w_g = din("w_g", [1024, 2048])
    relb = din("relb", [8, 128, 5, 128])
    w_uk = din("w_uk", [256, 512])
    w_uv = din("w_uv", [256, 512])
    w_ba = din("w_ba", [512, 1024])
    w_bb = din("w_bb", [512, 1024])
    w_out = din("w_out", [1024, 1024])
    w_fgu = din("w_fgu", [NFC, 128, 8, 2, 128])
    w_fd = din("w_fd", [DFF, 1024])
    convw = din("convw", [128, 4, NFC])
    nrm = din("nrm", [128, 5, 1024])

    y_own = dout("y_own", [2048, 1024])
    y_smp = dout("y_smp", [64, 1024])
    o_kav = dout("o_kav", [512, 1024])
    o_ckv = dout("o_ckv", [2048, 256])
    o_kr = dout("o_kr", [2048, 32])
    o_conv = dout("o_conv", [128, 2, NFC])
    o_sk = dout("o_sk", [512, 512])
    o_sv = dout("o_sv", [512, 512])
    o_sckv = dout("o_sckv", [64, 256])
    o_skr = dout("o_skr", [64, 32])
    o_sconv = dout("o_sconv", [128, 2, NFC])
    x1_d = nc.dram_tensor("x1_scr", [NQ, 1024], F32).ap()
    wgu_bf = nc.dram_tensor("wgu_bf", [NFC, 128, 2048], BF16).ap()
    wd_bf = nc.dram_tensor("wd_bf", [NFC, 128, 1024], BF16).ap()

    ps = nc.alloc_psum_tensor("ps", [128, 4096], F32).ap()

    def bank(b, n=512, p0=0, p1=128, off=0):
        return ps[p0:p1, 512 * b + off:512 * b + off + n]

    def PK(*bs):
        return [("ps", b) for b in bs]

    es = ExitStack()

    def sb(name, shape, dt, stack=None):
        return (stack or es).enter_context(nc.sbuf_tensor("sb_" + name, shape, dt))[:]

    ident = sb("ident", [128, 128], F32)
    nrm_rep = sb("nrm_rep", [128, 2, 1024], F32)
    NSLOT = {}

    def load_nrm(slot, row):
        NSLOT[row] = slot
        kb.dma("sp", nrm_rep[:, slot, :], nrm[:, row, :], writes=["nrm_rep"])
    kvbias = sb("kvbias", [128, 4], F32)
    z1 = sb("z1", [1, 128], BF16)
    kb.emit("pool", lambda e: e.memset(z1, 0.0), writes=["z1"])
    bandbias = sb("bandbias", [128, 2], F32)
    xs = ExitStack()
    xnT = sb("xnT", [128, 8, NALL], BF16, xs)
    ybT = sb("ybT", [128, 4, NQ], BF16, xs)

    kb.dma("sp", ident, ident_d, writes=["ident"])
    kb.dma("sp", kvbias, kvbias_d, writes=["kvbias"])
    kb.dma("sp", bandbias, bandbias_d, writes=["bandbias"])
    load_nrm(0, 0)
    load_nrm(1, 4)

    def rstd_from_ss(st, n, inv_d, keys):
        kb.emit("dve", lambda e: e.tensor_scalar(st[0:n, 1:2], st[0:n, 0:1], inv_d, EPS, ALU.mult, ALU.add),
                reads=keys, writes=keys)
        kb.emit("act", lambda e: e.activation(st[0:n, 3:4], st[0:n, 1:2], AF.Sqrt), reads=keys, writes=keys)
        kb.emit("dve", lambda e: e.reciprocal(st[0:n, 2:3], st[0:n, 3:4]), reads=keys, writes=keys)

    def rstd_gen(st, n, inv_d, keys):
        kb.emit("dve", lambda e: e.tensor_scalar(st[0:n, 1:2], st[0:n, 0:1], inv_d, EPS, ALU.mult, ALU.add),
                reads=keys, writes=keys)
        yield
        kb.emit("act", lambda e: e.activation(st[0:n, 3:4], st[0:n, 1:2], AF.Sqrt), reads=keys, writes=keys)
        yield
        kb.emit("dve", lambda e: e.reciprocal(st[0:n, 2:3], st[0:n, 3:4]), reads=keys, writes=keys)
        yield

    def norm_transpose(xt, kx, n, widx, dst, dkeys, junk, kj, st, kst, b0=0):
        kb.emit("act", lambda e: e.activation(junk[0:n], xt[0:n], AF.Square, accum_out=st[0:n, 0:1]),
                reads=[kx], writes=[kj, kst])
        yield
        yield from rstd_gen(st, n, 1.0 / 1024, [kst])
        kb.emit("dve", lambda e: e.scalar_tensor_tensor(xt[0:n], xt[0:n], st[0:n, 2:3], nrm_rep[0:n, NSLOT[widx], :],
                                                        ALU.mult, ALU.mult),
                reads=[kx, kst, "nrm_rep"], writes=[kx])
        yield
        for kc in range(8):
            kb.emit("pe", lambda e, kc=kc: e.transpose(bank(b0, n, off=kc * 128) if kc < 4 else bank(b0 + 1, n, off=(kc - 4) * 128),
                                                      xt[0:n, kc * 128:(kc + 1) * 128], ident[0:n, 0:n]),
                    reads=[kx, "ident"], writes=PK(b0, b0 + 1))
        yield
        src = ps[:, 512 * b0:512 * b0 + 1024].rearrange("p (k t) -> p k t", k=8)[:, :, 0:n]
        kb.emit("act", lambda e: e.activation(dst, src, AF.Copy), reads=PK(b0, b0 + 1), writes=dkeys)
        yield

    def run_pipeline(gens, depth):
        active = []
        it = iter(gens)
        done = False
        while True:
            if not done and len(active) < depth:
                try:
                    active.append(next(it))
                except StopIteration:
                    done = True
            if not active:
                if done:
                    break
                continue
            for g in list(active):
                try:
                    next(g)
                except StopIteration:
                    active.remove(g)

    lat = ExitStack()
    ckvT_kv = sb("ckvT_kv", [128, 2, NKV], BF16, lat)
    ckvT_own = sb("ckvT_own", [128, 2, NQ], BF16, lat)
    ckvT_c = sb("ckvT_c", [128, 2, 1024], BF16, lat)
    Kp = sb("Kp", [96, NKV + 2048], BF16, lat)
    Ks = sb("Ks", [96, 1088], BF16, lat)
    pa = ExitStack()
    xring = Ring([sb("xr%d" % i, [128, 1024], F32, pa) for i in range(7)], "xr")
    jring = Ring([sb("jr%d" % i, [128, 1024], BF16, pa) for i in range(5)], "jr")
    sring = Ring([sb("sr%d" % i, [128, 4], F32, pa) for i in range(14)], "sr")
    ltring = Ring([sb("lt%d" % i, [128, 320], F32, pa) for i in range(6)], "lt")
    csring = Ring([sb("cs%d" % i, [128, 32], F32, pa) for i in range(6)], "cs")
    tring = Ring([sb("tt%d" % i, [128, 64], F32, pa) for i in range(6)], "tt")
    xtmpT = Ring([sb("xtT%d" % i, [128, 8, 128], BF16, pa) for i in range(6)], "xtT")
    wc = sb("wc", [128, 8, 288], BF16, pa)
    kb.dma("pool", wc, w_c.rearrange("(kc p) n -> p kc n", p=128), writes=["wc"])
    tiles_all = [(t * 128, 128) for t in range(21)] + [(2688, 64)]

    def latent(xT_fn, xkeys, n, cs_rows, dst_ckvT, dst_krT, dkeys, out_ckv=None, out_kr=None, par=0):
        mmb, trb = (2, 3) if par == 0 else (6, 7)
        for kc in range(8):
            kb.emit("pe", lambda e, kc=kc: e.matmul(bank(mmb, 288, 0, n), xT_fn(kc), wc[:, kc, :],
                                                    start=(kc == 0), stop=(kc == 7)),
                    reads=xkeys + ["wc"], writes=PK(mmb))
        lt, kl = ltring.next()
        st, kst = sring.next()
        junk, kj = jring.next()
        cs, kcs = csring.next()
        tt, ktt = tring.next()
        kb.dma("sp", cs[0:n], cs_rows, writes=[kcs])
        yield
        kb.emit("act", lambda e: e.activation(lt[0:n, 0:288], bank(mmb, 288, 0, n), AF.Copy), reads=PK(mmb), writes=[kl])
        yield
        kb.emit("act", lambda e: e.activation(junk[0:n, 0:256], lt[0:n, 0:256], AF.Square, accum_out=st[0:n, 0:1]),
                reads=[kl], writes=[kj, kst])
        x1, x2 = lt[0:n, 256:272], lt[0:n, 272:288]
        c, s = cs[0:n, 0:16], cs[0:n, 16:32]
        kb.emit("dve", lambda e: e.tensor_tensor(tt[0:n, 0:16], x1, c, ALU.mult), reads=[kl, kcs], writes=[ktt])
        kb.emit("dve", lambda e: e.tensor_tensor(tt[0:n, 16:32], x2, s, ALU.mult), reads=[kl, kcs], writes=[ktt])
        kb.emit("dve", lambda e: e.tensor_tensor(tt[0:n, 32:48], x1, s, ALU.mult), reads=[kl, kcs], writes=[ktt])
        kb.emit("dve", lambda e: e.tensor_tensor(tt[0:n, 48:64], x2, c, ALU.mult), reads=[kl, kcs], writes=[ktt])
        yield
        kb.emit("dve", lambda e: e.tensor_tensor(lt[0:n, 288:304], tt[0:n, 0:16], tt[0:n, 16:32], ALU.subtract),
                reads=[ktt], writes=[(kl, "kr")])
        kb.emit("dve", lambda e: e.tensor_tensor(lt[0:n, 304:320], tt[0:n, 32:48], tt[0:n, 48:64], ALU.add),
                reads=[ktt], writes=[(kl, "kr")])
        yield from rstd_gen(st, n, 1.0 / 256, [kst])
        kb.emit("dve", lambda e: e.scalar_tensor_tensor(lt[0:n, 0:256], lt[0:n, 0:256], st[0:n, 2:3],
                                                        nrm_rep[0:n, NSLOT[4], 0:256], ALU.mult, ALU.mult),
                reads=[kl, kst, "nrm_rep"], writes=[kl])
        yield
        if out_ckv is not None:
            kb.dma("sp", out_ckv, lt[0:n, 0:256], reads=[kl], final=True)
            kb.dma("sp", out_kr, lt[0:n, 288:320], reads=[kl, (kl, "kr")], final=True)
        for c2 in range(2):
            kb.emit("pe", lambda e, c2=c2: e.transpose(bank(trb, n, off=c2 * 128), lt[0:n, c2 * 128:(c2 + 1) * 128],
                                                      ident[0:n, 0:n]),
                    reads=[kl, "ident"], writes=PK(trb))
        kb.emit("pe", lambda e: e.transpose(bank(trb, n, 0, 96, off=256), lt[0:n, 224:320], ident[0:n, 0:n]),
                reads=[kl, (kl, "kr"), "ident"], writes=PK(trb))
        yield
        src = bank(trb, 256).rearrange("p (k t) -> p k t", k=2)[:, :, 0:n]
        kb.emit("act", lambda e: e.activation(dst_ckvT, src, AF.Copy), reads=PK(trb), writes=dkeys)
        kb.emit("dve", lambda e: e.tensor_copy(dst_krT, bank(trb, n, 64, 96, off=256)), reads=PK(trb), writes=dkeys)
        yield

    def own_tile(ti, t0, n):
        xt, kx = xring.next()
        kb.dma("sp", xt[0:n], x_all[t0:t0 + n, :], writes=[kx])
        junk, kj = jring.next()
        st, kst = sring.next()
        yield
        yield from norm_transpose(xt, kx, n, 0, xnT[:, :, t0:t0 + n], [("xnT", t0 // 128)], junk, kj, st, kst, b0=4 * (ti % 2))
        if t0 < 512:
            return
        q0 = t0 - 512
        is_own = 640 <= t0 < 2688
        is_smp = t0 == 2688
        oc = ok = None
        if is_own:
            oc, ok = o_ckv[t0 - 640:t0 - 640 + n, :], o_kr[t0 - 640:t0 - 640 + n, :]
        if is_smp:
            oc, ok = o_sckv[:, :], o_skr[:, :]
        if is_own:
            krdst = Kp[64:96, NKV + t0 - 640:NKV + t0 - 640 + n]
        elif is_smp:
            krdst = Ks[64:96, 1024:1088]
        else:
            krdst = Ks[64:96, 0:n]
        yield from latent(lambda kc: xnT[:, kc, t0:t0 + n], [("xnT", t0 // 128)], n, cs_all[t0:t0 + n, :],
                          ckvT_own[:, :, q0:q0 + n], krdst, [("lat_own", t0 // 128), "Ks_dump"], oc, ok, par=ti % 2)

    def kv_tile(t):
        xt, kx = xring.next()
        kb.dma("sp", xt, x_kv[t * 128:(t + 1) * 128, :], writes=[kx])
        junk, kj = jring.next()
        st, kst = sring.next()
        xT, kxT = xtmpT.next()
        yield
        yield from norm_transpose(xt, kx, 128, 0, xT, [kxT], junk, kj, st, kst, b0=4 * (t % 2))
        yield from latent(lambda kc: xT[:, kc, :], [kxT], 128, cs_kv[t * 128:(t + 1) * 128, :],
                          ckvT_kv[:, :, t * 128:(t + 1) * 128], Kp[64:96, t * 128:(t + 1) * 128], [("lat_kv", t)], par=t % 2)

    gens = [own_tile(ti, t0, n) for ti, (t0, n) in enumerate(tiles_all)] + [kv_tile(t) for t in range(NKV // 128)]
    run_pipeline(gens, 5)
    for t in range(8):
        xt, kx = xring.next()
        kb.dma("sp", xt[:, 0:256], cckv[t * 128:(t + 1) * 128, :], writes=[kx])
        kb.emit("pool", lambda e, xt=xt: e.memset(xt[:, 256:320], 0.0), writes=[kx])
        kb.dma("sp", xt[:, 320:352], ckr[t * 128:(t + 1) * 128, :], writes=[kx])
        for c2 in range(2):
            kb.emit("pe", lambda e, c2=c2, xt=xt: e.transpose(bank(3, 128, off=c2 * 128), xt[:, c2 * 128:(c2 + 1) * 128], ident),
                    reads=[kx, "ident"], writes=PK(3))
        kb.emit("pe", lambda e, xt=xt: e.transpose(bank(3, 128, 0, 96, off=256), xt[:, 256:352], ident),
                reads=[kx, "ident"], writes=PK(3))
        src = bank(3, 256).rearrange("p (k t) -> p k t", k=2)
        kb.emit("act", lambda e, t=t, src=src: e.activation(ckvT_c[:, :, t * 128:(t + 1) * 128], src, AF.Copy),
                reads=PK(3), writes=[("lat_c", t)])
        kb.emit("dve", lambda e, t=t: e.tensor_copy(Ks[64:96, t * 128:(t + 1) * 128], bank(3, 128, 64, 96, off=256)),
                reads=PK(3), writes=[("lat_c", t), "Ks_dump"])
    kb.barrier()
    pa.close()
    if stop == "B":
        kb.finalize()
        return nc

    ACCC = [0]

    def attend(pc, q_ap, qrows, ncol, ktiles, scale, out_dst, par, PTring, recring, okeys, qkeys, relb_t=None, hook=None, defer=False):
        nt = len(ktiles)
        if ncol <= 512:
            nbuf, bstep = 4, 1
            ACCC[0] += 1
            accb = 6 + ACCC[0] % 2
            akeys = PK(accb)
        else:
            nbuf, bstep = 2, 2
            accb = 6
            akeys = PK(6, 7)
        LA = nbuf - 1
        G = max(1, 512 // ncol) if ncol <= 128 else 1
        batches = []
        for ti, kt in enumerate(ktiles):
            ok = False
            if batches and len(batches[-1]) < G and G > 1:
                p = ktiles[batches[-1][-1]]
                ok = (p["nk"] == 128 and kt["nk"] == 128 and p.get("c0", 0) == 0 and kt.get("c0", 0) == 0
                      and (p.get("bias") is kt.get("bias"))
                      and (relb_t is None or kt["rb"] == p["rb"] + 1))
            if ok:
                batches[-1].append(ti)
            else:
                batches.append([ti])
        nbt = len(batches)
        st8 = {}

        def segs_of(c0):
            segs = []
            a = c0
            while a < ncol:
                b = min(ncol, (a // 512 + 1) * 512)
                segs.append((a, b))
                a = b
            return segs

        def qk(bi):
            sb_ = 2 + bstep * (bi % nbuf)
            bks = PK(*range(sb_, sb_ + bstep))
            for s_i, ti in enumerate(batches[bi]):
                kt = ktiles[ti]
                nk = kt["nk"]
                c0 = kt.get("c0", 0)
                o_ = 512 * sb_ + s_i * ncol
                for (a, b) in segs_of(c0):
                    kb.emit("pe", lambda e, a=a, b=b, kt=kt, nk=nk, o_=o_: e.matmul(
                        ps[0:nk, o_ + a:o_ + b], kt["K"], q_ap[:, a:b], start=True, stop=True),
                        reads=kt["kkeys"] + qkeys, writes=bks)

        def ex(bi):
            bt = batches[bi]
            nb = len(bt)
            kt = ktiles[bt[0]]
            sb_ = 2 + bstep * (bi % nbuf)
            nk = kt["nk"]
            c0 = kt.get("c0", 0)
            PT, kpt = PTring.next()
            st8[bi] = (PT, kpt)
            w = (nb - 1) * ncol + ncol
            src = ps[0:nk, 512 * sb_ + c0:512 * sb_ + w]
            rkeys = PK(*range(sb_, sb_ + bstep))
            if relb_t is not None:
                tmp, ktmp = pc["tmpring"].next()
                rb = kt["rb"]
                if nb == 1:
                    i0, i1, o0_ = src, relb_t[0:nk, rb, c0:ncol], tmp[0:nk, c0:ncol]
                else:
                    i0 = src.rearrange("p (b c) -> p b c", b=nb)
                    i1 = relb_t[0:nk, rb:rb + nb, 0:ncol]
                    o0_ = tmp[0:nk, 0:w].rearrange("p (b c) -> p b c", b=nb)
                kb.emit("dve", lambda e, i0=i0, i1=i1, o0_=o0_: e.scalar_tensor_tensor(o0_, i0, scale, i1, ALU.mult, ALU.add),
                        reads=rkeys + ["relb"], writes=[ktmp])
                src2, rk2, sc = tmp[0:nk, c0:w], [ktmp], 1.0
            else:
                src2, rk2, sc = src, rkeys, scale
            bias = kt.get("bias")
            if bias is not None:
                kb.emit("act", lambda e, PT=PT, src2=src2, bias=bias, nk=nk, c0=c0, sc=sc, w=w: e.activation(
                    PT[0:nk, c0:w], src2, AF.Exp, bias=bias, scale=sc),
                    reads=rk2 + ["kvbias", "bandbias"], writes=[kpt])
            else:
                kb.emit("act", lambda e, PT=PT, src2=src2, nk=nk, c0=c0, sc=sc, w=w: e.activation(
                    PT[0:nk, c0:w], src2, AF.Exp, scale=sc), reads=rk2, writes=[kpt])

        def pv(bi):
            PT, kpt = st8.pop(bi)
            for s_i, ti in enumerate(batches[bi]):
                kt = ktiles[ti]
                nk = kt["nk"]
                c0 = kt.get("c0", 0)
                po = s_i * ncol
                segs = segs_of(c0)
                pieces = []
                half = kt.get("half")
                if half is not None:
                    lo, hi = (0, 64) if half == "lo" else (64, 128)
                    pieces.append((c0, c0 + 64, lo, hi))
                    for (a, b) in segs_of(c0 + 64):
                        pieces.append((a, b, 0, nk))
                else:
                    for (a, b) in segs:
                        pieces.append((a, b, 0, nk))
                half2 = kt.get("half2")
                if half2 is not None:
                    newp = []
                    h2a, h2b, lo2, hi2 = half2
                    for (a, b, lo, hi) in pieces:
                        if b <= h2a or a >= h2b:
                            newp.append((a, b, lo, hi))
                        else:
                            if a < h2a:
                                newp.append((a, h2a, lo, hi))
                            newp.append((max(a, h2a), min(b, h2b), lo2, hi2))
                            if b > h2b:
                                newp.append((h2b, b, lo, hi))
                    pieces = newp
                st_flag = (ti == 0)
                if ti == 0:
                    assert half is None and half2 is None and c0 == 0
                for pi, (a, b, lo, hi) in enumerate(pieces):
                    last_in_bank = False
                    kb.emit("pe", lambda e, a=a, b=b, lo=lo, hi=hi, kt=kt, PT=PT, st_flag=st_flag, lb=last_in_bank, po=po: e.matmul(
                        ps[:, 512 * accb + a:512 * accb + b], kt["V"][lo:hi, :], PT[lo:hi, po + a:po + b], start=st_flag, stop=lb),
                        reads=[kpt] + kt["vkeys"], writes=akeys)

        for bi in range(min(LA, nbt)):
            qk(bi)
        for bi in range(nbt):
            if bi + LA < nbt:
                qk(bi + LA)
            ex(bi)
            pv(bi)
            if hook is not None:
                hook(batches[bi][-1])
        for (a, b) in segs_of(0):
            kb.emit("pe", lambda e, a=a, b=b: e.matmul(ps[:, 512 * accb + a:512 * accb + b], z1[0:1, 0:128], q_ap[0:1, a:b],
                                                       start=False, stop=True),
                    reads=qkeys + ["z1"], writes=akeys)

        def fin():
            rec, krec = recring.next()
            (o0, o1), (d0, d1) = ((0, 64), (64, 128)) if par == 0 else ((64, 128), (0, 64))
            A0 = 512 * accb
            kb.emit("dve", lambda e: e.tensor_scalar(rec[d0:d1, 0:ncol], ps[d0:d1, A0:A0 + ncol], 1e-30, None, ALU.max),
                    reads=akeys, writes=[krec])
            kb.emit("dve", lambda e: e.reciprocal(rec[d0:d1, 0:ncol], rec[d0:d1, 0:ncol]), reads=[krec], writes=[krec])
            kb.emit("dve", lambda e: e.tensor_tensor(out_dst, ps[o0:o1, A0:A0 + ncol], rec[d0:d1, 0:ncol], ALU.mult),
                    reads=akeys + [krec], writes=okeys)
        if defer:
            return fin
        fin()

    pc_ = ExitStack()
    wuk = sb("wuk", [128, 2, 512], BF16, pc_)
    wuv = sb("wuv", [128, 2, 512], BF16, pc_)
    qtr = Ring([sb("qtab%d" % i, [96, 2, 512], F32, pc_) for i in range(2)], "qtab")
    Vp = sb("Vp", [128, 64, 128], BF16, pc_)
    Vs = sb("Vs", [128, 9, 128], BF16, pc_)
    qTs = [sb("qT%d" % i, [96, NQ], BF16, pc_) for i in range(2)]
    wqr = Ring([sb("wq%d" % i, [128, 8, 2, 96], BF16, pc_) for i in range(2)], "wq")
    PTr = Ring([sb("PT%d" % i, [128, 1024], BF16, pc_) for i in range(4)], "PT")
    recr = Ring([sb("rec%d" % i, [128, 1024], F32, pc_) for i in range(1)], "rec")
    rt = Ring([sb("rt%d" % i, [96, 512], F32, pc_) for i in range(4)], "rt")
    ones_sb = sb("ones_sb", [128, 64], BF16, pc_)
    kb.dma("pool", wuk, w_uk.rearrange("(c p) n -> p c n", p=128), writes=["wuk"])
    kb.dma("pool", wuv, w_uv.rearrange("(c p) n -> p c n", p=128), writes=["wuv"])
    kb.dma("pool", ones_sb, ones_d, writes=["ones"])
    pc = {}

    qgroups = [(640 - 512, 1024), (640 - 512 + 1024, 1024), (0, 128)]
    kvb = [kvbias[:, v:v + 1] for v in range(3)]

    def mla_prep(h):
        par = h % 2
        vo = 0 if par == 0 else 64
        on = 64 if par == 0 else 0
        qT = qTs[h % 2]
        wq, kwq = wqr.next()
        th = [(-1, lambda: kb.dma("pool", wq, w_qq[h], writes=[kwq]))]

        def kexp(dst, srcT, ncols, dkey):
            for c in range(2):
                kb.emit("pe", lambda e, c=c: e.matmul(bank(0, ncols, 0, 64), wuk[:, c, h * 64:(h + 1) * 64], srcT(c),
                                                      start=(c == 0), stop=(c == 1)),
                        reads=["wuk"], writes=PK(0))
            kb.emit("dve", lambda e: e.tensor_copy(dst, bank(0, ncols, 0, 64)), reads=PK(0), writes=[dkey])

        def vexp(dst3, srcT, nt_, nk, dkey, ones_dst):
            for j in range(nt_):
                for c in range(2):
                    kb.emit("pe", lambda e, j=j, c=c: e.matmul(bank(1, 64, 0, nk, off=j * 64), srcT(c, j),
                                                              wuv[:, c, h * 64:(h + 1) * 64], start=(c == 0), stop=(c == 1)),
                            reads=["wuv"], writes=PK(1))
            src = bank(1, nt_ * 64, 0, nk).rearrange("p (t d) -> p t d", d=64)
            kb.emit("dve", lambda e: e.tensor_copy(dst3, src), reads=PK(1), writes=[dkey])

        def ones_fill(V3, t0_, nt_, dkey):
            for t in range(t0_, t0_ + nt_):
                kb.emit("pool", lambda e, t=t: e.tensor_copy(V3[:, t, on:on + 64], ones_sb), reads=["ones"], writes=[dkey])

        for g in range(12):
            th.append((4 * g + 3, lambda g=g: kexp(Kp[0:64, g * 512:(g + 1) * 512],
                                                   lambda c: ckvT_kv[:, c, g * 512:(g + 1) * 512], 512, ("Kp", g))))
        for g in range(4):
            th.append((48 + 4 * g + 3, lambda g=g: kexp(Kp[0:64, NKV + g * 512:NKV + (g + 1) * 512],
                                                        lambda c: ckvT_own[:, c, 128 + g * 512:128 + (g + 1) * 512], 512, ("Kp", 12 + g))))
        for g in range(2):
            th.append((-1, lambda g=g: kexp(Ks[0:64, g * 512:(g + 1) * 512], lambda c: ckvT_c[:, c, g * 512:(g + 1) * 512], 512, ("Ks", g))))
        th.append((-1, lambda: kexp(Ks[0:64, 1024:1088], lambda c: ckvT_own[:, c, 2176:2240], 64, ("Ks", 2))))
        for g in range(6):
            def vth(g=g):
                ones_fill(Vp, g * 8, 8, ("Vp", g))
                vexp(Vp[:, g * 8:(g + 1) * 8, vo:vo + 64], lambda c, j: ckvT_kv[:, c, (g * 8 + j) * 128:(g * 8 + j + 1) * 128], 8, 128, ("Vp", g), None)
            th.append((8 * g + 7, vth))
        for g in range(2):
            def vth2(g=g):
                ones_fill(Vp, 48 + g * 8, 8, ("Vp", 6 + g))
                vexp(Vp[:, 48 + g * 8:48 + (g + 1) * 8, vo:vo + 64],
                     lambda c, j: ckvT_own[:, c, 128 + (g * 8 + j) * 128:128 + (g * 8 + j + 1) * 128], 8, 128, ("Vp", 6 + g), None)
            th.append((48 + 8 * g + 7, vth2))

        def vs_th():
            ones_fill(Vs, 0, 9, "Vs")
            vexp(Vs[:, 0:8, vo:vo + 64], lambda c, j: ckvT_c[:, c, j * 128:(j + 1) * 128], 8, 128, "Vs", None)
            vexp(Vs[0:64, 8:9, vo:vo + 64], lambda c, j: ckvT_own[:, c, 2176:2240], 1, 64, "Vs", None)
        th.append((-1, vs_th))

        def qproj(a, n):
            for v in range(2):
                for kc in range(8):
                    kb.emit("pe", lambda e, v=v, kc=kc: e.matmul(bank(v, n, 0, 96), wq[:, kc, v, :],
                                                                 xnT[:, kc, 512 + a:512 + a + n],
                                                                 start=(kc == 0), stop=(kc == 7)),
                            reads=[kwq] + [("xnT", tt_) for tt_ in range((512 + a) // 128, (512 + a + n + 127) // 128)],
                            writes=PK(v))
            kb.emit("dve", lambda e: e.tensor_copy(qT[0:64, a:a + n], bank(0, n, 0, 64)),
                    reads=PK(0), writes=[("qT", h % 2, a // 512)])
            r1, k1 = rt.next()
            r2, k2 = rt.next()
            qtab, kqt = qtr.next()
            kb.dma("sp", qtab[64:96, 0, 0:n], qtab_d[0][:, a:a + n], writes=[kqt])
            kb.dma("sp", qtab[64:96, 1, 0:n], qtab_d[1][:, a:a + n], writes=[kqt])
            kb.emit("dve", lambda e: e.tensor_tensor(r1[64:96, 0:n], bank(0, n, 64, 96), qtab[64:96, 0, 0:n], ALU.mult),
                    reads=PK(0) + [kqt], writes=[k1])
            kb.emit("dve", lambda e: e.tensor_tensor(r2[64:96, 0:n], bank(1, n, 64, 96), qtab[64:96, 1, 0:n], ALU.mult),
                    reads=PK(1) + [kqt], writes=[k2])
            kb.emit("dve", lambda e: e.tensor_tensor(qT[64:96, a:a + n], r1[64:96, 0:n], r2[64:96, 0:n], ALU.add),
                    reads=[k1, k2], writes=[("qT", h % 2, a // 512)])
        for (a, n) in [(0, 512), (512, 512), (1024, 512), (1536, 512), (2048, 192)]:
            th.append((-1, lambda a=a, n=n: qproj(a, n)))
        return th

    def mla_attn(h, nxt):
        par = h % 2
        vo = 0 if par == 0 else 64
        qT = qTs[h % 2]
        qk = [("qT", h % 2, i) for i in range(5)]
        kvt = [dict(K=Kp[0:96, t * 128:(t + 1) * 128], V=Vp[:, t, :], nk=128, bias=kvb[t // 16],
                    kkeys=[("Kp", t // 4)], vkeys=[("Vp", t // 8)]) for t in range(48)]

        def own_t(t, c0, half):
            return dict(K=Kp[0:96, NKV + t * 128:NKV + (t + 1) * 128], V=Vp[:, 48 + t, :], nk=128, bias=None, c0=c0, half=half,
                        kkeys=[("Kp", 12 + t // 4)], vkeys=[("Vp", 6 + t // 8)])
        attend(pc, qT[0:96, 0:128], 96, 128, kvt, MLA_SCALE, ybT[vo:vo + 64, h // 2, 0:128], par, PTr, recr,
               [("ybT", h, 2)], qk)
        kts = [dict(K=Ks[0:96, t * 128:(t + 1) * 128], V=Vs[:, t, :], nk=128, bias=None,
                    kkeys=[("Ks", t // 4)], vkeys=["Vs"]) for t in range(8)]
        kts.append(dict(K=Ks[0:96, 1024:1088], V=Vs[0:64, 8, :], nk=64, bias=None, kkeys=[("Ks", 2)], vkeys=["Vs"]))
        attend(pc, qT[0:96, 2176:2240], 96, 64, kts, MLA_SCALE, ybT[vo:vo + 64, h // 2, 2176:2240], par, PTr, recr,
               [("ybT", h, 3)], qk)
        kt0 = kvt + [own_t(t, 128 * t, "lo") for t in range(8)]
        attend(pc, qT[0:96, 128:1152], 96, 1024, kt0, MLA_SCALE, ybT[vo:vo + 64, h // 2, 128:1152], par, PTr, recr,
               [("ybT", h, 0)], qk)
        kt1 = kvt + [own_t(t, 0, None) for t in range(8)] + [own_t(8 + t, 128 * t, "lo") for t in range(8)]
        pend = sorted(nxt, key=lambda x: x[0])

        def hook(ti):
            k = 0
            while pend and pend[0][0] <= ti and k < 2:
                pend.pop(0)[1]()
                k += 1
        attend(pc, qT[0:96, 1152:2176], 96, 1024, kt1, MLA_SCALE, ybT[vo:vo + 64, h // 2, 1152:2176], par, PTr, recr,
               [("ybT", h, 1)], qk, hook=hook)
        while pend:
            pend.pop(0)[1]()

    for (_, fn_) in mla_prep(0):
        fn_()
    for h_ in range(8):
        mla_attn(h_, mla_prep(h_ + 1) if h_ < 7 else [])
    kb.barrier()
    pc_.close()
    lat.close()
    if stop == "C":
        kb.finalize()
        return nc
    yas = ExitStack()
    yaT = sb("yaT", [128, 4, NQ], BF16, yas)

    pd = ExitStack()
    wab = Ring([sb("wab%d" % i, [128, 8, 2, 64], BF16, pd) for i in range(2)], "wab")
    wkv = sb("wkv", [128, 8, 1024], BF16, pd)
    va_all = sb("va_all", [128, 22, 512], BF16, pd)
    cav_sb = sb("cav_sb", [128, 4, 512], BF16, pd)
    kaTc = sb("kaTc", [64, 8, 512], BF16, pd)
    qaT = sb("qaT", [64, NALL], BF16, pd)
    kaT = sb("kaT", [64, NALL], BF16, pd)
    Vb = sb("Vb", [128, 22, 128], BF16, pd)
    Vbc = sb("Vbc", [128, 4, 128], BF16, pd)
    relb_t = sb("relb_t", [128, 5, 128], F32, pd)
    PTb = Ring([sb("PTb%d" % i, [128, 512], BF16, pd) for i in range(6)], "PTb")
    recb = Ring([sb("recb%d" % i, [128, 128], F32, pd) for i in range(3)], "recb")
    tmpr = Ring([sb("tmpb%d" % i, [128, 512], F32, pd) for i in range(6)], "tmpb")
    stg = Ring([sb("stg%d" % i, [128, 1024], F32, pd) for i in range(2)], "stg")
    ones_b = sb("ones_b", [128, 64], BF16, pd)
    kb.dma("pool", ones_b, ones_d, writes=["ones"])
    cvr = Ring([sb("cv%d" % i, [128, 2048], BF16, pd) for i in range(3)], "cv")

    def convert_ffn(fcs):
        for fc in fcs:
            cv, kcv = cvr.next()
            kb.dma("pool", cv, w_fgu[fc].rearrange("p a b c -> p (a b c)"), writes=[kcv])
            kb.dma("pool", wgu_bf[fc], cv, reads=[kcv])
            cv, kcv = cvr.next()
            kb.dma("pool", cv[:, 0:1024], w_fd[fc * 128:(fc + 1) * 128, :], writes=[kcv])
            kb.dma("pool", wd_bf[fc], cv[:, 0:1024], reads=[kcv])

    kb.dma("pool", wkv, w_kv.rearrange("(kc p) n -> p kc n", p=128), writes=["wkv"])
    kb.dma("pool", cav_sb, cav.rearrange("(t p) n -> p t n", p=128), writes=["cav_sb"])
    pcb = {"tmpring": tmpr}
    kb.emit("pool", lambda e: e.memset(va_all[:, 21, :], 0.0), writes=[("va_all", 21)])
    for ti, (t0, n) in enumerate(tiles_all):
        for hf in range(2):
            for kc in range(8):
                kb.emit("pe", lambda e, hf=hf, kc=kc, t0=t0, n=n: e.matmul(bank(hf, 512, 0, n), xnT[:, kc, t0:t0 + n],
                                                                          wkv[:, kc, hf * 512:(hf + 1) * 512],
                                                                          start=(kc == 0), stop=(kc == 7)),
                        reads=["wkv", ("xnT", ti)], writes=PK(hf))
        kb.emit("act", lambda e, ti=ti, n=n: e.activation(va_all[0:n, ti, :], bank(1, 512, 0, n), AF.Copy),
                reads=PK(1), writes=[("va_all", ti)])
        if 2176 <= t0 < 2688 or t0 == 2688:
            s_, ks_ = stg.next()
            kb.emit("dve", lambda e, s_=s_, n=n: e.tensor_copy(s_[0:n, :], ps[0:n, 0:1024]), reads=PK(0, 1), writes=[ks_])
            if t0 < 2688:
                kb.dma("sp", o_kav[t0 - 2176:t0 - 2176 + n, :], s_[0:n, :], reads=[ks_], final=True)
            else:
                kb.dma("sp", o_sk[448:512, :], s_[0:64, 0:512], reads=[ks_], final=True)
                kb.dma("sp", o_sv[448:512, :], s_[0:64, 512:1024], reads=[ks_], final=True)
    kb.dma("sp", o_sk[0:448, :], cak[64:512, :], final=True)
    kb.dma("sp", o_sv[0:448, :], cav[64:512, :], final=True)
    for t in range(4):
        s_, ks_ = stg.next()
        kb.dma("sp", s_[:, 0:512], cak[t * 128:(t + 1) * 128, :], writes=[ks_])
        for hh in range(8):
            kb.emit("pe", lambda e, hh=hh, s_=s_: e.transpose(ps[0:64, hh * 128:(hh + 1) * 128], s_[:, hh * 64:(hh + 1) * 64], ident),
                    reads=[ks_, "ident"], writes=PK(0, 1))
        src = ps[0:64, 0:1024].rearrange("p (k t) -> p k t", k=8)
        kb.emit("act", lambda e, t=t, src=src: e.activation(kaTc[:, :, t * 128:(t + 1) * 128], src, AF.Copy),
                reads=PK(0, 1), writes=["kaTc"])

    bb0 = bandbias[:, 0:1]

    def band_head(h):
        par = h % 2
        vo = 0 if par == 0 else 64
        on = 64 if par == 0 else 0
        w2, kw2 = wab.next()
        kb.dma("pool", w2, w_qk[h], writes=[kw2])
        kb.dma("sp", relb_t, relb[h], writes=["relb"])
        for (a, n) in [(0, 512), (512, 512), (1024, 512), (1536, 512), (2048, 512), (2560, 192)]:
            for v in range(2):
                for kc in range(8):
                    kb.emit("pe", lambda e, v=v, kc=kc, a=a, n=n: e.matmul(bank(v, n, 0, 64), w2[:, kc, v, :], xnT[:, kc, a:a + n],
                                                                           start=(kc == 0), stop=(kc == 7)),
                            reads=[kw2] + [("xnT", tt_) for tt_ in range(a // 128, (a + n + 127) // 128)], writes=PK(v))
            kb.emit("act", lambda e, a=a, n=n: e.activation(qaT[:, a:a + n], bank(0, n, 0, 64), AF.Copy), reads=PK(0), writes=["qaT"])
            kb.emit("dve", lambda e, a=a, n=n: e.tensor_copy(kaT[:, a:a + n], bank(1, n, 0, 64)), reads=PK(1), writes=["kaT"])
        kb.emit("pool", lambda e: e.tensor_copy(Vb[:, :, vo:vo + 64], va_all[:, :, h * 64:(h + 1) * 64]),
                reads=[("va_all", i) for i in range(22)], writes=["Vb"])
        for t in range(22):
            kb.emit("pool", lambda e, t=t: e.tensor_copy(Vb[:, t, on:on + 64], ones_b), reads=["ones"], writes=["Vb"])
        kb.emit("pool", lambda e: e.tensor_copy(Vbc[:, :, vo:vo + 64], cav_sb[:, :, h * 64:(h + 1) * 64]), reads=["cav_sb"], writes=["Vbc"])
        for t in range(4):
            kb.emit("pool", lambda e, t=t: e.tensor_copy(Vbc[:, t, on:on + 64], ones_b), reads=["ones"], writes=["Vbc"])
        for m in range(17):
            kts = []
            for r in (1, 2, 3, 4, 0):
                t = m + r
                d = dict(K=kaT[:, t * 128:(t + 1) * 128], V=Vb[:, t, :], nk=128, rb=r,
                         bias=(bb0 if t < 5 else None), kkeys=["kaT"], vkeys=["Vb"])
                if r == 0:
                    d["half2"] = (64, 128, 64, 128)
                if r == 4:
                    d["half2"] = (0, 64, 0, 64)
                kts.append(d)
            fin = attend(pcb, qaT[:, 512 + 128 * m:640 + 128 * m], 64, 128, kts, A_SCALE,
                         yaT[vo:vo + 64, h // 2, 128 * m:128 * m + 128], par, PTb, recb, [("yaT", h, m)], ["qaT"], relb_t=relb_t, defer=True)
            if prev[0] is not None:
                prev[0]()
            prev[0] = fin
        kts = [dict(K=kaTc[:, h, t * 128:(t + 1) * 128], V=Vbc[:, t, :], nk=128, rb=t, bias=None, kkeys=["kaTc"], vkeys=["Vbc"])
               for t in range(4)]
        kts.append(dict(K=kaT[:, 2688:2752], V=Vb[0:64, 21, :], nk=64, rb=4, bias=None, kkeys=["kaT"], vkeys=["Vb"]))
        fin = attend(pcb, qaT[:, 2688:2752], 64, 64, kts, A_SCALE, yaT[vo:vo + 64, h // 2, 2176:2240], par, PTb, recb,
                     [("yaT", h, 17)], ["qaT"], relb_t=relb_t, defer=True)
        prev[0]()
        prev[0] = fin
        convert_ffn(range(3 * h, min(NFC, 3 * h + 3)))
    prev = [None]
    for h_ in range(8):
        band_head(h_)
    prev[0]()
    kb.barrier()
    pd.close()
    if stop == "D":
        kb.finalize()
        return nc
    load_nrm(0, 1)

    pe_ = ExitStack()
    wa = sb("wa", [128, 4, 1024], BF16, pe_)
    wb = sb("wb", [128, 4, 1024], BF16, pe_)
    wo = sb("wo", [128, 8, 1024], BF16, pe_)
    wg = sb("wg", [128, 8, 2048], BF16, pe_)
    mT = sb("mT", [128, 8, 512], BF16, pe_)
    sgr = Ring([sb("sg%d" % i, [128, 512], F32, pe_) for i in range(4)], "sg")
    xr2 = Ring([sb("x2r%d" % i, [128, 1024], F32, pe_) for i in range(2)], "x2r")
    mr = Ring([sb("mr%d" % i, [128, 1024], F32, pe_) for i in range(2)], "mr")
    jr2 = Ring([sb("j2r%d" % i, [128, 1024], BF16, pe_) for i in range(2)], "j2r")
    sr2 = Ring([sb("s2r%d" % i, [128, 4], F32, pe_) for i in range(4)], "s2r")
    kb.dma("pool", wa, w_ba.rearrange("(c p) n -> p c n", p=128), writes=["wa"])
    kb.dma("pool", wb, w_bb.rearrange("(c p) n -> p c n", p=128), writes=["wb"])
    kb.dma("pool", wo, w_out.rearrange("(c p) n -> p c n", p=128), writes=["wo"])
    kb.dma("pool", wg, w_g.rearrange("(c p) n -> p c n", p=128), writes=["wg"])
    qblocks = [(0, 512), (512, 512), (1024, 512), (1536, 512), (2048, 192)]
    for (q0, n) in qblocks:
        for fc in range(8):
            for p in range(4):
                kb.emit("pe", lambda e, p=p, fc=fc, q0=q0, n=n: e.matmul(bank(0, n), wa[:, p, fc * 128:(fc + 1) * 128], yaT[:, p, q0:q0 + n],
                                                                         start=(p == 0), stop=(p == 3)), reads=["wa"], writes=PK(0))
            for p in range(4):
                kb.emit("pe", lambda e, p=p, fc=fc, q0=q0, n=n: e.matmul(bank(1, n), wb[:, p, fc * 128:(fc + 1) * 128], ybT[:, p, q0:q0 + n],
                                                                         start=(p == 0), stop=(p == 3)), reads=["wb"], writes=PK(1))
            for g in range(2):
                for kc in range(8):
                    kb.emit("pe", lambda e, g=g, kc=kc, fc=fc, q0=q0, n=n: e.matmul(
                        bank(2 + g, n), wg[:, kc, g * 1024 + fc * 128:g * 1024 + (fc + 1) * 128], xnT[:, kc, 512 + q0:512 + q0 + n],
                        start=(kc == 0), stop=(kc == 7)), reads=["wg"], writes=PK(2 + g))
            sa, ksa = sgr.next()
            sbb, ksb = sgr.next()
            kb.emit("act", lambda e, sa=sa, n=n: e.activation(sa[:, 0:n], bank(2, n), AF.Sigmoid), reads=PK(2), writes=[ksa])
            kb.emit("act", lambda e, sbb=sbb, n=n: e.activation(sbb[:, 0:n], bank(3, n), AF.Sigmoid), reads=PK(3), writes=[ksb])
            kb.emit("dve", lambda e, sa=sa, n=n: e.tensor_tensor(sa[:, 0:n], sa[:, 0:n], bank(0, n), ALU.mult), reads=PK(0) + [ksa], writes=[ksa])
            kb.emit("dve", lambda e, sbb=sbb, n=n: e.tensor_tensor(sbb[:, 0:n], sbb[:, 0:n], bank(1, n), ALU.mult), reads=PK(1) + [ksb], writes=[ksb])
            kb.emit("dve", lambda e, sa=sa, sbb=sbb, fc=fc, n=n: e.tensor_tensor(mT[:, fc, 0:n], sa[:, 0:n], sbb[:, 0:n], ALU.add),
                    reads=[ksa, ksb], writes=[("mT", fc)])
        for tt_ in range((n + 127) // 128):
            nn = min(128, n - tt_ * 128)
            for hf in range(2):
                for fc in range(8):
                    kb.emit("pe", lambda e, hf=hf, fc=fc, tt_=tt_, nn=nn: e.matmul(bank(4 + hf, 512, 0, nn), mT[:, fc, tt_ * 128:tt_ * 128 + nn],
                                                                                 wo[:, fc, hf * 512:(hf + 1) * 512], start=(fc == 0), stop=(fc == 7)),
                            reads=["wo"] + [("mT", f_) for f_ in range(8)], writes=PK(4 + hf))
            xt, kx = xr2.next()
            mx, kmx = mr.next()
            junk, kj = jr2.next()
            st, kst = sr2.next()
            r0 = 512 + q0 + tt_ * 128
            kb.dma("sp", xt[0:nn], x_all[r0:r0 + nn, :], writes=[kx])
            kb.emit("act", lambda e, mx=mx, nn=nn: e.activation(mx[0:nn], ps[0:nn, 2048:3072], AF.Copy), reads=PK(4, 5), writes=[kmx])
            kb.emit("act", lambda e, mx=mx, junk=junk, st=st, nn=nn: e.activation(junk[0:nn], mx[0:nn], AF.Square, accum_out=st[0:nn, 0:1]),
                    reads=[kmx], writes=[kj, kst])
            rstd_from_ss(st, nn, 1.0 / 1024, [kst])
            kb.emit("dve", lambda e, mx=mx, st=st, nn=nn: e.scalar_tensor_tensor(mx[0:nn], mx[0:nn], st[0:nn, 2:3], nrm_rep[0:nn, NSLOT[1], :], ALU.mult, ALU.mult),
                    reads=[kmx, kst, "nrm_rep"], writes=[kmx])
            kb.emit("dve", lambda e, mx=mx, xt=xt, nn=nn: e.tensor_tensor(mx[0:nn], mx[0:nn], xt[0:nn], ALU.add), reads=[kmx, kx], writes=[kmx])
            kb.dma("sp", x1_d[q0 + tt_ * 128:q0 + tt_ * 128 + nn, :], mx[0:nn], reads=[kmx], writes=[("x1d", (q0 + tt_ * 128) // 64)])
    kb.barrier()
    pe_.close()
    yas.close()
    xs.close()
    if stop == "E":
        kb.finalize()
        return nc
    load_nrm(0, 2)
    load_nrm(1, 3)

    pf = ExitStack()
    NB = len(qblocks)
    x1ra = Ring([sb("x1ra%d" % i, [128, 1024], F32, pf) for i in range(3)], "x1ra")
    x1rb = Ring([sb("x1rb%d" % i, [128, 1024], F32, pf) for i in range(2)], "x1rb")
    xn2Ts = [sb("xn2T%d" % k, [128, 8, 512], BF16, pf) for k in range(2)]
    hTs = [sb("hT%d" % k, [128, NFC, 512], BF16, pf) for k in range(2)]
    mxs_ = [[sb("mx%d_%d" % (k, i), [128, 1024], F32, pf) for i in range(4)] for k in range(2)]
    gtail = sb("gtail", [128, 2, NFC], F32, pf)
    stail = sb("stail", [128, 2, NFC], F32, pf)
    cw = sb("cw", [128, 4, NFC], F32, pf)
    wgr = Ring([sb("wfg%d" % i, [128, 8, 2, 128], BF16, pf) for i in range(4)], "wfg")
    wdr = Ring([sb("wfd%d" % i, [128, 512], BF16, pf) for i in range(8)], "wfd")
    gbr = Ring([sb("gb%d" % i, [128, 516], F32, pf) for i in range(6)], "gb")
    cbr = Ring([sb("cb%d" % i, [128, 512], F32, pf) for i in range(4)], "cb")
    jr3 = Ring([sb("j3r%d" % i, [128, 1024], BF16, pf) for i in range(2)], "j3r")
    sr3 = Ring([sb("s3r%d" % i, [128, 4], F32, pf) for i in range(4)], "s3r")
    sr3b = Ring([sb("s3rb%d" % i, [128, 4], F32, pf) for i in range(4)], "s3rb")
    prr = Ring([sb("prr%d" % i, [128, 1024], F32, pf) for i in range(2)], "prr")
    kb.dma("sp", cw, convw, writes=["cw"])
    kb.emit("pool", lambda e: e.memset(gtail, 0.0), writes=["gtail"])
    kb.dma("sp", stail, sconv, writes=["stail"])

    def tiles_of(b):
        q0, n = qblocks[b]
        return [(tt_, q0 + tt_ * 128, min(128, n - tt_ * 128)) for tt_ in range((n + 127) // 128)]

    def pipe_rounds(gens, depth):
        active = []
        it = iter(gens)
        done = False
        while True:
            if not done and len(active) < depth:
                try:
                    active.append(next(it))
                except StopIteration:
                    done = True
            if not active:
                if done:
                    return
                continue
            for g in list(active):
                try:
                    next(g)
                except StopIteration:
                    active.remove(g)
            yield

    def pre_tile(b, tt_, r0, nn):
        k = b % 2
        xt, kx = x1ra.next()
        kb.dma("sp", xt[0:nn], x1_d[r0:r0 + nn, :], writes=[kx])
        junk, kj = jr3.next()
        st, kst = sr3.next()
        mx, kmx = prr.next()
        kb.emit("act", lambda e: e.activation(junk[0:nn], xt[0:nn], AF.Square, accum_out=st[0:nn, 0:1]),
                reads=[kx], writes=[kj, kst])
        yield
        yield from rstd_gen(st, nn, 1.0 / 1024, [kst])
        kb.emit("dve", lambda e: e.scalar_tensor_tensor(mx[0:nn], xt[0:nn], st[0:nn, 2:3], nrm_rep[0:nn, NSLOT[2], :], ALU.mult, ALU.mult),
                reads=[kx, kst, "nrm_rep"], writes=[kmx])
        yield
        for kc in range(8):
            kb.emit("pe", lambda e, kc=kc: e.transpose(bank(0, nn, off=kc * 128) if kc < 4 else bank(1, nn, off=(kc - 4) * 128),
                                                      mx[0:nn, kc * 128:(kc + 1) * 128], ident[0:nn, 0:nn]),
                    reads=[kmx, "ident"], writes=PK(0, 1))
        yield
        src = ps[:, 0:1024].rearrange("p (k t) -> p k t", k=8)[:, :, 0:nn]
        kb.emit("act", lambda e: e.activation(xn2Ts[k][:, :, tt_ * 128:tt_ * 128 + nn], src, AF.Copy),
                reads=PK(0, 1), writes=[("xn2T", k, tt_)])
        yield

    def ffn_pre(b):
        return pipe_rounds([pre_tile(b, tt_, r0, nn) for (tt_, r0, nn) in tiles_of(b)], 2)

    cbs = {}

    def stage_a(b, fc):
        q0, n = qblocks[b]
        k = b % 2
        xn2T = xn2Ts[k]
        xk = [("xn2T", k, t_[0]) for t_ in tiles_of(b)]
        segs = [(0, n, gtail, "gtail")] if b < NB - 1 else [(0, 128, gtail, "gtail"), (128, 64, stail, "stail")]
        wf, kwf = wgr.next()
        kb.dma("sp", wf.rearrange("p a b c -> p (a b c)"), wgu_bf[fc], writes=[kwf])
        pb = 2 + 2 * (fc % 2)
        for v in range(2):
            for kc in range(8):
                kb.emit("pe", lambda e, v=v, kc=kc: e.matmul(bank(pb + v, n), wf[:, kc, v, :], xn2T[:, kc, 0:n],
                                                             start=(kc == 0), stop=(kc == 7)),
                        reads=[kwf] + xk, writes=PK(pb + v))
        cb_, kcb = cbr.next()
        cbs[(b, fc)] = (cb_, kcb, pb)
        for (c0, ns, tl, tlk) in segs:
            gb_, kgb = gbr.next()
            kb.emit("act", lambda e, gb_=gb_, c0=c0, ns=ns: e.activation(gb_[:, 2:2 + ns], bank(pb, ns, off=c0), AF.Copy), reads=PK(pb), writes=[kgb])
            kb.emit("dve", lambda e, gb_=gb_, tl=tl: e.tensor_copy(gb_[:, 0:2], tl[:, :, fc]), reads=[tlk], writes=[(kgb, "t")])
            kb.emit("pool", lambda e, gb_=gb_, c0=c0, ns=ns: e.tensor_scalar(cb_[:, c0:c0 + ns], gb_[:, 2:2 + ns], cw[:, 2, fc:fc + 1], cw[:, 3, fc:fc + 1], ALU.mult, ALU.add),
                    reads=[kgb, "cw"], writes=[kcb])
            kb.emit("dve", lambda e, gb_=gb_, c0=c0, ns=ns: e.scalar_tensor_tensor(cb_[:, c0:c0 + ns], gb_[:, 1:1 + ns], cw[:, 1, fc:fc + 1], cb_[:, c0:c0 + ns], ALU.mult, ALU.add),
                    reads=[kgb, (kgb, "t"), "cw", kcb], writes=[kcb])
            kb.emit("dve", lambda e, gb_=gb_, c0=c0, ns=ns: e.scalar_tensor_tensor(cb_[:, c0:c0 + ns], gb_[:, 0:ns], cw[:, 0, fc:fc + 1], cb_[:, c0:c0 + ns], ALU.mult, ALU.add),
                    reads=[kgb, (kgb, "t"), "cw", kcb], writes=[kcb])
            kb.emit("dve", lambda e, gb_=gb_, ns=ns, tl=tl: e.tensor_copy(tl[:, :, fc], gb_[:, ns:ns + 2]), reads=[kgb, (kgb, "t")], writes=[tlk])

    def stage_b(b, fc):
        q0, n = qblocks[b]
        hT = hTs[b % 2]
        cb_, kcb, pb = cbs.pop((b, fc))
        kb.emit("act", lambda e: e.activation(cb_[:, 0:n], cb_[:, 0:n], AF.Gelu_apprx_tanh), reads=[kcb], writes=[kcb])
        kb.emit("dve", lambda e: e.tensor_tensor(hT[:, fc, 0:n], cb_[:, 0:n], bank(pb + 1, n), ALU.mult),
                reads=[kcb] + PK(pb + 1), writes=[("hT", b % 2, fc)])

    def down_pieces(b):
        hT = hTs[b % 2]
        tl = tiles_of(b)
        for ps_ in range(4):
            hf, pair = ps_ // 2, ps_ % 2
            tls = [t_ for t_ in tl if t_[0] // 2 == pair]
            if not tls:
                continue
            for fc in range(NFC):
                wd_, kwd = wdr.next()
                kb.dma("sp", wd_, wd_bf[fc][:, hf * 512:(hf + 1) * 512], writes=[kwd])
                for (tt_, r0, nn) in tls:
                    kb.emit("pe", lambda e, fc=fc, tt_=tt_, nn=nn, wd_=wd_: e.matmul(bank(6 + tt_ % 2, 512, 0, nn), hT[:, fc, tt_ * 128:tt_ * 128 + nn],
                                                                                 wd_, start=(fc == 0), stop=(fc == NFC - 1)),
                            reads=[kwd, ("hT", b % 2, fc)], writes=PK(6 + tt_ % 2))
                if fc == NFC - 1:
                    for (tt_, r0, nn) in tls:
                        mx = mxs_[b % 2][tt_]
                        kb.emit("act", lambda e, mx=mx, nn=nn, tt_=tt_, hf=hf: e.activation(mx[0:nn, hf * 512:(hf + 1) * 512], bank(6 + tt_ % 2, 512, 0, nn), AF.Copy),
                                reads=PK(6 + tt_ % 2), writes=[("mx", b % 2, tt_)])
                yield

    def post_tile(b, tt_, r0, nn):
        k = b % 2
        mx, kmx = mxs_[k][tt_], ("mx", k, tt_)
        junk, kj = jr3.next()
        st, kst = sr3b.next()
        xt, kx = x1rb.next()
        kb.dma("sp", xt[0:nn], x1_d[r0:r0 + nn, :], writes=[kx])
        kb.emit("act", lambda e: e.activation(junk[0:nn], mx[0:nn], AF.Square, accum_out=st[0:nn, 0:1]),
                reads=[kmx], writes=[kj, kst])
        yield
        yield from rstd_gen(st, nn, 1.0 / 1024, [kst])
        kb.emit("dve", lambda e: e.scalar_tensor_tensor(mx[0:nn], mx[0:nn], st[0:nn, 2:3], nrm_rep[0:nn, NSLOT[3], :], ALU.mult, ALU.mult),
                reads=[kmx, kst, "nrm_rep"], writes=[kmx])
        yield
        kb.emit("dve", lambda e: e.tensor_tensor(mx[0:nn], mx[0:nn], xt[0:nn], ALU.add), reads=[kmx, kx], writes=[kmx])
        yield
        if 128 <= r0 < 2176:
            kb.dma("sp", y_own[r0 - 128:r0 - 128 + nn, :], mx[0:nn], reads=[kmx], final=True)
        elif r0 >= 2176:
            kb.dma("sp", y_smp[0:nn, :], mx[0:nn], reads=[kmx], final=True)
        yield

    def ffn_post(b):
        return pipe_rounds([post_tile(b, tt_, r0, nn) for (tt_, r0, nn) in tiles_of(b)], 2)

    def drain(g, k):
        for _ in range(k):
            if next(g, "end") == "end":
                return True
        return False

    def flush(g):
        if g is not None:
            for _ in g:
                pass

    flush(ffn_pre(0))
    dgen = None
    pregen = None
    postgen = None
    for b in range(NB):
        stage_a(b, 0)
        for fc in range(NFC):
            if fc + 1 < NFC:
                stage_a(b, fc + 1)
            stage_b(b, fc)
            if fc == 2 and b + 1 < NB:
                pregen = ffn_pre(b + 1)
            if pregen is not None and drain(pregen, 2):
                pregen = None
            if postgen is not None and drain(postgen, 1):
                postgen = None
            if dgen is not None and drain(dgen, 4):
                dgen = None
        flush(pregen)
        pregen = None
        if dgen is not None:
            flush(dgen)
        flush(postgen)
        postgen = ffn_post(b - 1) if b >= 1 else None
        if b == NB - 1:
            kb.dma("sp", o_conv, gtail, reads=["gtail"], final=True)
            kb.dma("sp", o_sconv, stail, reads=["stail"], final=True)
        dgen = down_pieces(b)
    flush(postgen)
    flush(dgen)
    flush(ffn_post(NB - 1))
    pf.close()
    kb.finalize()
    es.close()
    return nc


_NC = None


def kernel(x_prompt, x_sample, cache_a_k, cache_a_v, cache_mla_ckv, cache_mla_krope, state_ffn_conv,
           norm_mix_pre, norm_mix_post, w_in, rel_bias_table, kv_norm, w_uk, w_uv, w_branch_a, w_branch_b,
           w_out, norm_ffn_pre, norm_ffn_post, w_ffn_gate, w_ffn_up, conv_w, conv_b, w_ffn_down):
    global _NC
    f = lambda a: np.ascontiguousarray(np.asarray(a, dtype=np.float32))
    x_prompt, x_sample = f(x_prompt), f(x_sample)
    W = f(w_in)[0]
    qa, ka, va, qn, qr, ck, kr_, ga, gb = 0, 512, 1024, 1536, 2048, 2304, 2560, 2592, 3616
    wq = np.zeros((1024, 8, 96), np.float32)
    wq2 = np.zeros((1024, 8, 96), np.float32)
    for h in range(8):
        wq[:, h, 0:64] = W[:, qn + 64 * h:qn + 64 * (h + 1)]
        wq[:, h, 64:96] = W[:, qr + 32 * h:qr + 32 * (h + 1)]
        wq2[:, h, 0:64] = W[:, qn + 64 * h:qn + 64 * (h + 1)]
        wq2[:, h, 64:80] = W[:, qr + 32 * h + 16:qr + 32 * h + 32]
        wq2[:, h, 80:96] = W[:, qr + 32 * h:qr + 32 * h + 16]
    half = 16
    inv = (10000.0 ** (-np.arange(half, dtype=np.float32) / half)).astype(np.float32)

    def cs_tab(pos):
        ang = pos.astype(np.float32)[:, None] * inv[None, :]
        return np.cos(ang).astype(np.float32), np.sin(ang).astype(np.float32)

    tbl = f(rel_bias_table)[0]
    kt_, ki_, qi_ = np.meshgrid(np.arange(5), np.arange(128), np.arange(128), indexing="ij")
    didx = np.clip(512 + qi_ - 128 * kt_ - ki_, -256, 256) + 256
    relb = np.ascontiguousarray(tbl[:, didx].transpose(0, 2, 1, 3))
    nrm = np.zeros((5, 1024), np.float32)
    nrm[0], nrm[1], nrm[2], nrm[3] = f(norm_mix_pre)[0], f(norm_mix_post)[0], f(norm_ffn_pre)[0], f(norm_ffn_post)[0]
    nrm[4, :256] = f(kv_norm)[0]
    nrm_b = np.ascontiguousarray(np.broadcast_to(nrm[None], (128, 5, 1024)))
    cwl = np.ascontiguousarray(np.concatenate([f(conv_w)[0], f(conv_b)], axis=0).reshape(4, NFC, 128).transpose(2, 0, 1))
    common = dict(
        ident=np.eye(128, dtype=np.float32), ones=np.ones((128, 64), np.float32),
        w_c=np.ascontiguousarray(W[:, ck:ga]),
        w_qq=np.ascontiguousarray(np.stack([wq, wq2], axis=2).reshape(8, 128, 8, 2, 96).transpose(2, 1, 0, 3, 4)),
        w_qk=np.ascontiguousarray(np.stack([W[:, qa:ka].reshape(1024, 8, 64), W[:, ka:va].reshape(1024, 8, 64)], axis=2)
                                  .reshape(8, 128, 8, 2, 64).transpose(2, 1, 0, 3, 4)),
        w_kv=np.ascontiguousarray(W[:, ka:qn]), w_g=np.ascontiguousarray(W[:, ga:]),
        relb=relb, w_uk=f(w_uk)[0].reshape(256, 512), w_uv=f(w_uv)[0].reshape(256, 512),
        w_ba=f(w_branch_a)[0], w_bb=f(w_branch_b)[0], w_out=f(w_out)[0],
        w_fgu=np.ascontiguousarray(np.stack([f(w_ffn_gate)[0].reshape(8, 128, NFC, 128), f(w_ffn_up)[0].reshape(8, 128, NFC, 128)], axis=3)
                                   .transpose(2, 1, 0, 3, 4)),
        w_fd=f(w_ffn_down)[0],
        convw=cwl, nrm=nrm_b,
    )
    in_maps = []
    for c in range(8):
        b, j = c // 4, c % 4
        s0 = 2048 * j
        x_all = np.zeros((NALL, 1024), np.float32)
        lo = s0 - 640
        if lo >= 0:
            x_all[0:640] = x_prompt[b, lo:s0]
        x_all[640:2688] = x_prompt[b, s0:s0 + 2048]
        x_all[2688:] = x_sample[c]
        pos_all = np.concatenate([np.arange(lo, s0 + 2048), 1024 + np.arange(64)])
        cc, ss = cs_tab(pos_all)
        cs_all = np.concatenate([cc, ss], axis=1)
        blocks = [v if v < j else 0 for v in range(3)]
        x_kv = np.concatenate([x_prompt[b, 2048 * v:2048 * (v + 1)] for v in blocks], axis=0)
        pos_kv = np.concatenate([np.arange(2048 * v, 2048 * (v + 1)) for v in blocks])
        ck_, sk_ = cs_tab(pos_kv)
        cs_kv = np.concatenate([ck_, sk_], axis=1)
        posq = pos_all[512:]
        cq, sq = cs_tab(posq)
        qtab = np.zeros((2, 32, NQ), np.float32)
        qtab[0, 0:16], qtab[0, 16:32] = cq.T, cq.T
        qtab[1, 0:16], qtab[1, 16:32] = -sq.T, sq.T
        kvbias = np.zeros((128, 4), np.float32)
        for v in range(3):
            if v >= j:
                kvbias[:, v] = NEG
        bandbias = np.zeros((128, 2), np.float32)
        if j == 0:
            bandbias[:, 0] = NEG
        m = dict(common)
        m.update(x_all=x_all, x_kv=np.ascontiguousarray(x_kv), cs_all=cs_all, cs_kv=cs_kv, qtab=qtab,
                 kvbias=kvbias, bandbias=bandbias,
                 cak=f(cache_a_k)[0, c].reshape(512, 512), cav=f(cache_a_v)[0, c].reshape(512, 512),
                 cckv=f(cache_mla_ckv)[0, c], ckr=f(cache_mla_krope)[0, c], sconv=np.ascontiguousarray(f(state_ffn_conv)[0, c].reshape(2, NFC, 128).transpose(2, 0, 1)))
        in_maps.append(m)
    if _NC is None:
        _NC = build()
    res = run_bass_kernel_spmd(_NC, in_maps, core_ids=list(range(8))).results
    y_p = np.zeros((2, 8192, 1024), np.float32)
    ckv_p = np.zeros((1, 2, 8192, 256), np.float32)
    kr_p = np.zeros((1, 2, 8192, 32), np.float32)
    for c in range(8):
        b, j = c // 4, c % 4
        y_p[b, 2048 * j:2048 * (j + 1)] = res[c]["y_own"]
        ckv_p[0, b, 2048 * j:2048 * (j + 1)] = res[c]["o_ckv"]
        kr_p[0, b, 2048 * j:2048 * (j + 1)] = res[c]["o_kr"]
    y_s = np.stack([res[c]["y_smp"] for c in range(8)])
    ak_p = np.stack([res[4 * b + 3]["o_kav"][:, 0:512].reshape(512, 8, 64) for b in range(2)])[None]
    av_p = np.stack([res[4 * b + 3]["o_kav"][:, 512:1024].reshape(512, 8, 64) for b in range(2)])[None]
    conv_p = np.stack([res[4 * b + 3]["o_conv"].transpose(1, 2, 0).reshape(2, DFF) for b in range(2)])[None]
    sk = np.stack([res[c]["o_sk"].reshape(512, 8, 64) for c in range(8)])[None]
    sv = np.stack([res[c]["o_sv"].reshape(512, 8, 64) for c in range(8)])[None]
    sckv = np.stack([res[c]["o_sckv"] for c in range(8)])[None]
    skr = np.stack([res[c]["o_skr"] for c in range(8)])[None]
    sconv = np.stack([res[c]["o_sconv"].transpose(1, 2, 0).reshape(2, DFF) for c in range(8)])[None]
    return (y_p, y_s, ak_p, av_p, ckv_p, kr_p, conv_p, sk, sv, sckv, skr, sconv)
```

```python
import numpy as np
import concourse.bass as bass
import concourse.mybir as mybir
from concourse.bass_utils import run_bass_kernel_spmd
from contextlib import ExitStack

F32 = mybir.dt.float32
BF16 = mybir.dt.bfloat16
AF = mybir.ActivationFunctionType
ALU = mybir.AluOpType

EPS = 1e-6
A_SCALE = 64 ** -0.5
MLA_SCALE = 96 ** -0.5
NEG = -1.0e30
NALL = 2752
NQ = 2240
NKV = 6144
DFF = 2816
NFC = 22


class _Res:
    __slots__ = ("lw", "rd")

    def __init__(self):
        self.lw = None
        self.rd = []


class KB:
    ENG = ("pe", "act", "dve", "pool", "sp")

    def __init__(self, nc, n_dma_sems=20):
        self.nc = nc
        self.ops = []
        self.res = {}
        self.n_dma_sems = n_dma_sems
        self.out_dmas = []
        self.bar_deps = set()
        self.bar_pending = set()
        self.since_bar = {}

    def _r(self, k):
        r = self.res.get(k)
        if r is None:
            r = self.res[k] = _Res()
        return r

    def barrier(self):
        d = set(self.bar_deps)
        for e, i in self.since_bar.items():
            d.add(i)
        for i, op in enumerate(self.ops):
            if op["is_dma"] and i >= getattr(self, "_bar_pos", 0):
                d.add(i)
        self.bar_deps = set()
        last = {}
        for i in d:
            op = self.ops[i]
            if op["is_dma"]:
                if i >= getattr(self, "_bar_pos", 0):
                    self.bar_deps.add(i)
            else:
                last[op["eng"]] = max(last.get(op["eng"], -1), i)
        self.bar_deps.update(last.values())
        for i in self.bar_deps:
            self.ops[i]["sig"] = True
        self._bar_pos = len(self.ops)
        self.since_bar = {}
        self.res = {}
        self.bar_pending = set(self.ENG)

    def _add(self, eng, fn, reads, writes, is_dma):
        idx = len(self.ops)
        deps = set()
        if eng in self.bar_pending:
            deps.update(self.bar_deps)
            self.bar_pending.discard(eng)
        for k in reads:
            r = self._r(k)
            if r.lw is not None:
                deps.add(r.lw)
        for k in writes:
            r = self._r(k)
            if r.lw is not None:
                deps.add(r.lw)
            deps.update(r.rd)
        if eng == "pe" and not is_dma:
            deps = {d for d in deps if not (self.ops[d]["eng"] == "pe" and not self.ops[d]["is_dma"])}
        self.ops.append(dict(eng=eng, fn=fn, deps=deps, is_dma=is_dma, sig=is_dma))
        for d in deps:
            self.ops[d]["sig"] = True
        for k in reads:
            self._r(k).rd.append(idx)
        for k in writes:
            r = self._r(k)
            r.lw = idx
            r.rd = []
        if not is_dma:
            self.since_bar[eng] = idx
        return idx

    def emit(self, eng, fn, reads=(), writes=()):
        return self._add(eng, fn, list(reads), list(writes), False)

    def dma(self, eng, out, in_, reads=(), writes=(), final=False, **kw):
        def fn(e, out=out, in_=in_, kw=kw):
            return e.dma_start(out=out, in_=in_, **kw)
        i = self._add(eng, fn, list(reads), list(writes), True)
        if final:
            self.out_dmas.append(i)
        return i

    def finalize(self):
        nc = self.nc
        ops = self.ops
        esem = {e: nc.alloc_semaphore(name="s_" + e) for e in self.ENG}
        dsem = {e: [nc.alloc_semaphore(name="d_%s_%d" % (e, i)) for i in range(self.n_dma_sems)]
                for e in ("sp", "act", "pool")}
        ecnt = {e: 0 for e in self.ENG}
        dnext = {e: 0 for e in dsem}
        dval = {e: [0] * self.n_dma_sems for e in dsem}
        tok = {}
        prevtok = {}
        for i, op in enumerate(ops):
            if not op["sig"]:
                continue
            e = op["eng"]
            if op["is_dma"]:
                s = dnext[e] % self.n_dma_sems
                dnext[e] += 1
                if dval[e][s] > 0:
                    prevtok[i] = (("d", e, s), dval[e][s])
                dval[e][s] += 16
                tok[i] = (("d", e, s), dval[e][s])
            else:
                ecnt[e] += 1
                tok[i] = (("e", e), ecnt[e])
        seen = {e: {} for e in self.ENG}

        def semof(key):
            return esem[key[1]] if key[0] == "e" else dsem[key[1]][key[2]]

        streams = {e: [] for e in self.ENG}
        for i, op in enumerate(ops):
            streams[op["eng"]].append(i)

        def run(ename, engine):
            sn = seen[ename]
            for i in streams[ename]:
                op = ops[i]
                need = {}
                for d in op["deps"]:
                    k, v = tok[d]
                    if need.get(k, 0) < v:
                        need[k] = v
                if i in prevtok:
                    k, v = prevtok[i]
                    if need.get(k, 0) < v:
                        need[k] = v
                for k, v in need.items():
                    if sn.get(k, 0) < v:
                        engine.wait_ge(semof(k), v)
                        sn[k] = v
                ins = op["fn"](engine)
                if op["sig"]:
                    k, v = tok[i]
                    ins.then_inc(semof(k), 16 if op["is_dma"] else 1)
            if ename == "sp":
                for i in self.out_dmas:
                    k, v = tok[i]
                    if sn.get(k, 0) < v:
                        engine.wait_ge(semof(k), v)
                        sn[k] = v

        with nc.Block() as block:
            @block.tensor
            def _(e):
                run("pe", e)

            @block.scalar
            def _(e):
                run("act", e)

            @block.vector
            def _(e):
                run("dve", e)

            @block.gpsimd
            def _(e):
                run("pool", e)

            @block.sync
            def _(e):
                run("sp", e)


class Ring:
    def __init__(self, aps, name):
        self.aps = aps
        self.name = name
        self.i = 0

    def next(self):
        j = self.i % len(self.aps)
        self.i += 1
        return self.aps[j], (self.name, j)


def build(stop=None):
    nc = bass.Bass("TRN2", target_bir_lowering=False)
    kb = KB(nc)

    def din(name, shape):
        return nc.dram_tensor(name, shape, F32, kind="ExternalInput").ap()

    def dout(name, shape):
        return nc.dram_tensor(name, shape, F32, kind="ExternalOutput").ap()

    x_all = din("x_all", [NALL, 1024])
    x_kv = din("x_kv", [NKV, 1024])
    cs_all = din("cs_all", [NALL, 32])
    cs_kv = din("cs_kv", [NKV, 32])
    qtab_d = din("qtab", [2, 32, NQ])
    kvbias_d = din("kvbias", [128, 4])
    bandbias_d = din("bandbias", [128, 2])
    ident_d = din("ident", [128, 128])
    ones_d = din("ones", [128, 64])
    cak = din("cak", [512, 512])
    cav = din("cav", [512, 512])
    cckv = din("cckv", [1024, 256])
    ckr = din("ckr", [1024, 32])
    sconv = din("sconv", [128, 2, NFC])
    w_c = din("w_c", [1024, 288])
    w_qq = din("w_qq", [8, 128, 8, 2, 96])
    w_qk = din("w_qk", [8, 128, 8, 2, 64])
    w_kv = din("w_kv", [1024, 1024])
    w_g = din("w_g", [1024, 2048])
    relb = din("relb", [8, 128, 5, 128])
    w_uk = din("w_uk", [256, 512])
    w_uv = din("w_uv", [256, 512])
    w_ba = din("w_ba", [512, 1024])
    w_bb = din("w_bb", [512, 1024])
    w_out = din("w_out", [1024, 1024])
    w_fgu = din("w_fgu", [NFC, 128, 8, 2, 128])
    w_fd = din("w_fd", [DFF, 1024])
    convw = din("convw", [128, 4, NFC])
    nrm = din("nrm", [128, 5, 1024])

    y_own = dout("y_own", [2048, 1024])
    y_smp = dout("y_smp", [64, 1024])
    o_kav = dout("o_kav", [512, 1024])
    o_ckv = dout("o_ckv", [2048, 256])
    o_kr = dout("o_kr", [2048, 32])
    o_conv = dout("o_conv", [128, 2, NFC])
    o_sk = dout("o_sk", [512, 512])
    o_sv = dout("o_sv", [512, 512])
    o_sckv = dout("o_sckv", [64, 256])
    o_skr = dout("o_skr", [64, 32])
    o_sconv = dout("o_sconv", [128, 2, NFC])
    x1_d = nc.dram_tensor("x1_scr", [NQ, 1024], F32).ap()
    wgu_bf = nc.dram_tensor("wgu_bf", [NFC, 128, 2048], BF16).ap()
    wd_bf = nc.dram_tensor("wd_bf", [NFC, 128, 1024], BF16).ap()

    ps = nc.alloc_psum_tensor("ps", [128, 4096], F32).ap()

    def bank(b, n=512, p0=0, p1=128, off=0):
        return ps[p0:p1, 512 * b + off:512 * b + off + n]

    def PK(*bs):
        return [("ps", b) for b in bs]

    es = ExitStack()

    def sb(name, shape, dt, stack=None):
        return (stack or es).enter_context(nc.sbuf_tensor("sb_" + name, shape, dt))[:]

    ident = sb("ident", [128, 128], F32)
    nrm_rep = sb("nrm_rep", [128, 2, 1024], F32)
    NSLOT = {}

    def load_nrm(slot, row):
        NSLOT[row] = slot
        kb.dma("sp", nrm_rep[:, slot, :], nrm[:, row, :], writes=["nrm_rep"])
    kvbias = sb("kvbias", [128, 4], F32)
    z1 = sb("z1", [1, 128], BF16)
    kb.emit("pool", lambda e: e.memset(z1, 0.0), writes=["z1"])
    bandbias = sb("bandbias", [128, 2], F32)
    xs = ExitStack()
    xnT = sb("xnT", [128, 8, NALL], BF16, xs)
    ybT = sb("ybT", [128, 4, NQ], BF16, xs)

    kb.dma("sp", ident, ident_d, writes=["ident"])
    kb.dma("sp", kvbias, kvbias_d, writes=["kvbias"])
    kb.dma("sp", bandbias, bandbias_d, writes=["bandbias"])
    load_nrm(0, 0)
    load_nrm(1, 4)

    def rstd_from_ss(st, n, inv_d, keys):
        kb.emit("dve", lambda e: e.tensor_scalar(st[0:n, 1:2], st[0:n, 0:1], inv_d, EPS, ALU.mult, ALU.add),
                reads=keys, writes=keys)
        kb.emit("act", lambda e: e.activation(st[0:n, 3:4], st[0:n, 1:2], AF.Sqrt), reads=keys, writes=keys)
        kb.emit("dve", lambda e: e.reciprocal(st[0:n, 2:3], st[0:n, 3:4]), reads=keys, writes=keys)

    def rstd_gen(st, n, inv_d, keys):
        kb.emit("dve", lambda e: e.tensor_scalar(st[0:n, 1:2], st[0:n, 0:1], inv_d, EPS, ALU.mult, ALU.add),
                reads=keys, writes=keys)
        yield
        kb.emit("act", lambda e: e.activation(st[0:n, 3:4], st[0:n, 1:2], AF.Sqrt), reads=keys, writes=keys)
        yield
        kb.emit("dve", lambda e: e.reciprocal(st[0:n, 2:3], st[0:n, 3:4]), reads=keys, writes=keys)
        yield

    def norm_transpose(xt, kx, n, widx, dst, dkeys, junk, kj, st, kst, b0=0):
        kb.emit("act", lambda e: e.activation(junk[0:n], xt[0:n], AF.Square, accum_out=st[0:n, 0:1]),
                reads=[kx], writes=[kj, kst])
        yield
        yield from rstd_gen(st, n, 1.0 / 1024, [kst])
        kb.emit("dve", lambda e: e.scalar_tensor_tensor(xt[0:n], xt[0:n], st[0:n, 2:3], nrm_rep[0:n, NSLOT[widx], :],
                                                        ALU.mult, ALU.mult),
                reads=[kx, kst, "nrm_rep"], writes=[kx])
        yield
        for kc in range(8):
            kb.emit("pe", lambda e, kc=kc: e.transpose(bank(b0, n, off=kc * 128) if kc < 4 else bank(b0 + 1, n, off=(kc - 4) * 128),
                                                      xt[0:n, kc * 128:(kc + 1) * 128], ident[0:n, 0:n]),
                    reads=[kx, "ident"], writes=PK(b0, b0 + 1))
        yield
        src = ps[:, 512 * b0:512 * b0 + 1024].rearrange("p (k t) -> p k t", k=8)[:, :, 0:n]
        kb.emit("act", lambda e: e.activation(dst, src, AF.Copy), reads=PK(b0, b0 + 1), writes=dkeys)
        yield

    def run_pipeline(gens, depth):
        active = []
        it = iter(gens)
        done = False
        while True:
            if not done and len(active) < depth:
                try:
                    active.append(next(it))
                except StopIteration:
                    done = True
            if not active:
                if done:
                    break
                continue
            for g in list(active):
                try:
                    next(g)
                except StopIteration:
                    active.remove(g)

    lat = ExitStack()
    ckvT_kv = sb("ckvT_kv", [128, 2, NKV], BF16, lat)
    ckvT_own = sb("ckvT_own", [128, 2, NQ], BF16, lat)
    ckvT_c = sb("ckvT_c", [128, 2, 1024], BF16, lat)
    Kp = sb("Kp", [96, NKV + 2048], BF16, lat)
    Ks = sb("Ks", [96, 1088], BF16, lat)
    pa = ExitStack()
    xring = Ring([sb("xr%d" % i, [128, 1024], F32, pa) for i in range(7)], "xr")
    jring = Ring([sb("jr%d" % i, [128, 1024], BF16, pa) for i in range(5)], "jr")
    sring = Ring([sb("sr%d" % i, [128, 4], F32, pa) for i in range(14)], "sr")
    ltring = Ring([sb("lt%d" % i, [128, 320], F32, pa) for i in range(6)], "lt")
    csring = Ring([sb("cs%d" % i, [128, 32], F32, pa) for i in range(6)], "cs")
    tring = Ring([sb("tt%d" % i, [128, 64], F32, pa) for i in range(6)], "tt")
    xtmpT = Ring([sb("xtT%d" % i, [128, 8, 128], BF16, pa) for i in range(6)], "xtT")
    wc = sb("wc", [128, 8, 288], BF16, pa)
    kb.dma("pool", wc, w_c.rearrange("(kc p) n -> p kc n", p=128), writes=["wc"])
    tiles_all = [(t * 128, 128) for t in range(21)] + [(2688, 64)]

    def latent(xT_fn, xkeys, n, cs_rows, dst_ckvT, dst_krT, dkeys, out_ckv=None, out_kr=None, par=0):
        mmb, trb = (2, 3) if par == 0 else (6, 7)
        for kc in range(8):
            kb.emit("pe", lambda e, kc=kc: e.matmul(bank(mmb, 288, 0, n), xT_fn(kc), wc[:, kc, :],
                                                    start=(kc == 0), stop=(kc == 7)),
                    reads=xkeys + ["wc"], writes=PK(mmb))
        lt, kl = ltring.next()
        st, kst = sring.next()
        junk, kj = jring.next()
        cs, kcs = csring.next()
        tt, ktt = tring.next()
        kb.dma("sp", cs[0:n], cs_rows, writes=[kcs])
        yield
        kb.emit("act", lambda e: e.activation(lt[0:n, 0:288], bank(mmb, 288, 0, n), AF.Copy), reads=PK(mmb), writes=[kl])
        yield
        kb.emit("act", lambda e: e.activation(junk[0:n, 0:256], lt[0:n, 0:256], AF.Square, accum_out=st[0:n, 0:1]),
                reads=[kl], writes=[kj, kst])
        x1, x2 = lt[0:n, 256:272], lt[0:n, 272:288]
        c, s = cs[0:n, 0:16], cs[0:n, 16:32]
        kb.emit("dve", lambda e: e.tensor_tensor(tt[0:n, 0:16], x1, c, ALU.mult), reads=[kl, kcs], writes=[ktt])
        kb.emit("dve", lambda e: e.tensor_tensor(tt[0:n, 16:32], x2, s, ALU.mult), reads=[kl, kcs], writes=[ktt])
        kb.emit("dve", lambda e: e.tensor_tensor(tt[0:n, 32:48], x1, s, ALU.mult), reads=[kl, kcs], writes=[ktt])
        kb.emit("dve", lambda e: e.tensor_tensor(tt[0:n, 48:64], x2, c, ALU.mult), reads=[kl, kcs], writes=[ktt])
        yield
        kb.emit("dve", lambda e: e.tensor_tensor(lt[0:n, 288:304], tt[0:n, 0:16], tt[0:n, 16:32], ALU.subtract),
                reads=[ktt], writes=[(kl, "kr")])
        kb.emit("dve", lambda e: e.tensor_tensor(lt[0:n, 304:320], tt[0:n, 32:48], tt[0:n, 48:64], ALU.add),
                reads=[ktt], writes=[(kl, "kr")])
        yield from rstd_gen(st, n, 1.0 / 256, [kst])
        kb.emit("dve", lambda e: e.scalar_tensor_tensor(lt[0:n, 0:256], lt[0:n, 0:256], st[0:n, 2:3],
                                                        nrm_rep[0:n, NSLOT[4], 0:256], ALU.mult, ALU.mult),
                reads=[kl, kst, "nrm_rep"], writes=[kl])
        yield
        if out_ckv is not None:
            kb.dma("sp", out_ckv, lt[0:n, 0:256], reads=[kl], final=True)
            kb.dma("sp", out_kr, lt[0:n, 288:320], reads=[kl, (kl, "kr")], final=True)
        for c2 in range(2):
            kb.emit("pe", lambda e, c2=c2: e.transpose(bank(trb, n, off=c2 * 128), lt[0:n, c2 * 128:(c2 + 1) * 128],
                                                      ident[0:n, 0:n]),
                    reads=[kl, "ident"], writes=PK(trb))
        kb.emit("pe", lambda e: e.transpose(bank(trb, n, 0, 96, off=256), lt[0:n, 224:320], ident[0:n, 0:n]),
                reads=[kl, (kl, "kr"), "ident"], writes=PK(trb))
        yield
        src = bank(trb, 256).rearrange("p (k t) -> p k t", k=2)[:, :, 0:n]
        kb.emit("act", lambda e: e.activation(dst_ckvT, src, AF.Copy), reads=PK(trb), writes=dkeys)
        kb.emit("dve", lambda e: e.tensor_copy(dst_krT, bank(trb, n, 64, 96, off=256)), reads=PK(trb), writes=dkeys)
        yield

    def own_tile(ti, t0, n):
        xt, kx = xring.next()
        kb.dma("sp", xt[0:n], x_all[t0:t0 + n, :], writes=[kx])
        junk, kj = jring.next()
        st, kst = sring.next()
        yield
        yield from norm_transpose(xt, kx, n, 0, xnT[:, :, t0:t0 + n], [("xnT", t0 // 128)], junk, kj, st, kst, b0=4 * (ti % 2))
        if t0 < 512:
            return
        q0 = t0 - 512
        is_own = 640 <= t0 < 2688
        is_smp = t0 == 2688
        oc = ok = None
        if is_own:
            oc, ok = o_ckv[t0 - 640:t0 - 640 + n, :], o_kr[t0 - 640:t0 - 640 + n, :]
        if is_smp:
            oc, ok = o_sckv[:, :], o_skr[:, :]
        if is_own:
            krdst = Kp[64:96, NKV + t0 - 640:NKV + t0 - 640 + n]
        elif is_smp:
            krdst = Ks[64:96, 1024:1088]
        else:
            krdst = Ks[64:96, 0:n]
        yield from latent(lambda kc: xnT[:, kc, t0:t0 + n], [("xnT", t0 // 128)], n, cs_all[t0:t0 + n, :],
                          ckvT_own[:, :, q0:q0 + n], krdst, [("lat_own", t0 // 128), "Ks_dump"], oc, ok, par=ti % 2)

    def kv_tile(t):
        xt, kx = xring.next()
        kb.dma("sp", xt, x_kv[t * 128:(t + 1) * 128, :], writes=[kx])
        junk, kj = jring.next()
        st, kst = sring.next()
        xT, kxT = xtmpT.next()
        yield
        yield from norm_transpose(xt, kx, 128, 0, xT, [kxT], junk, kj, st, kst, b0=4 * (t % 2))
        yield from latent(lambda kc: xT[:, kc, :], [kxT], 128, cs_kv[t * 128:(t + 1) * 128, :],
                          ckvT_kv[:, :, t * 128:(t + 1) * 128], Kp[64:96, t * 128:(t + 1) * 128], [("lat_kv", t)], par=t % 2)

    gens = [own_tile(ti, t0, n) for ti, (t0, n) in enumerate(tiles_all)] + [kv_tile(t) for t in range(NKV // 128)]
    run_pipeline(gens, 5)
    for t in range(8):
        xt, kx = xring.next()
        kb.dma("sp", xt[:, 0:256], cckv[t * 128:(t + 1) * 128, :], writes=[kx])
        kb.emit("pool", lambda e, xt=xt: e.memset(xt[:, 256:320], 0.0), writes=[kx])
        kb.dma("sp", xt[:, 320:352], ckr[t * 128:(t + 1) * 128, :], writes=[kx])
        for c2 in range(2):
            kb.emit("pe", lambda e, c2=c2, xt=xt: e.transpose(bank(3, 128, off=c2 * 128), xt[:, c2 * 128:(c2 + 1) * 128], ident),
                    reads=[kx, "ident"], writes=PK(3))
        kb.emit("pe", lambda e, xt=xt: e.transpose(bank(3, 128, 0, 96, off=256), xt[:, 256:352], ident),
                reads=[kx, "ident"], writes=PK(3))
        src = bank(3, 256).rearrange("p (k t) -> p k t", k=2)
        kb.emit("act", lambda e, t=t, src=src: e.activation(ckvT_c[:, :, t * 128:(t + 1) * 128], src, AF.Copy),
                reads=PK(3), writes=[("lat_c", t)])
        kb.emit("dve", lambda e, t=t: e.tensor_copy(Ks[64:96, t * 128:(t + 1) * 128], bank(3, 128, 64, 96, off=256)),
                reads=PK(3), writes=[("lat_c", t), "Ks_dump"])
    kb.barrier()
    pa.close()
    if stop == "B":
        kb.finalize()
        return nc

    ACCC = [0]

    def attend(pc, q_ap, qrows, ncol, ktiles, scale, out_dst, par, PTring, recring, okeys, qkeys, relb_t=None, hook=None, defer=False):
        nt = len(ktiles)
        if ncol <= 512:
            nbuf, bstep = 4, 1
            ACCC[0] += 1
            accb = 6 + ACCC[0] % 2
            akeys = PK(accb)
        else:
            nbuf, bstep = 2, 2
            accb = 6
            akeys = PK(6, 7)
        LA = nbuf - 1
        G = max(1, 512 // ncol) if ncol <= 128 else 1
        batches = []
        for ti, kt in enumerate(ktiles):
            ok = False
            if batches and len(batches[-1]) < G and G > 1:
                p = ktiles[batches[-1][-1]]
                ok = (p["nk"] == 128 and kt["nk"] == 128 and p.get("c0", 0) == 0 and kt.get("c0", 0) == 0
                      and (p.get("bias") is kt.get("bias"))
                      and (relb_t is None or kt["rb"] == p["rb"] + 1))
            if ok:
                batches[-1].append(ti)
            else:
                batches.append([ti])
        nbt = len(batches)
        st8 = {}

        def segs_of(c0):
            segs = []
            a = c0
            while a < ncol:
                b = min(ncol, (a // 512 + 1) * 512)
                segs.append((a, b))
                a = b
            return segs

        def qk(bi):
            sb_ = 2 + bstep * (bi % nbuf)
            bks = PK(*range(sb_, sb_ + bstep))
            for s_i, ti in enumerate(batches[bi]):
                kt = ktiles[ti]
                nk = kt["nk"]
                c0 = kt.get("c0", 0)
                o_ = 512 * sb_ + s_i * ncol
                for (a, b) in segs_of(c0):
                    kb.emit("pe", lambda e, a=a, b=b, kt=kt, nk=nk, o_=o_: e.matmul(
                        ps[0:nk, o_ + a:o_ + b], kt["K"], q_ap[:, a:b], start=True, stop=True),
                        reads=kt["kkeys"] + qkeys, writes=bks)

        def ex(bi):
            bt = batches[bi]
            nb = len(bt)
            kt = ktiles[bt[0]]
            sb_ = 2 + bstep * (bi % nbuf)
            nk = kt["nk"]
            c0 = kt.get("c0", 0)
            PT, kpt = PTring.next()
            st8[bi] = (PT, kpt)
            w = (nb - 1) * ncol + ncol
            src = ps[0:nk, 512 * sb_ + c0:512 * sb_ + w]
            rkeys = PK(*range(sb_, sb_ + bstep))
            if relb_t is not None:
                tmp, ktmp = pc["tmpring"].next()
                rb = kt["rb"]
                if nb == 1:
                    i0, i1, o0_ = src, relb_t[0:nk, rb, c0:ncol], tmp[0:nk, c0:ncol]
                else:
                    i0 = src.rearrange("p (b c) -> p b c", b=nb)
                    i1 = relb_t[0:nk, rb:rb + nb, 0:ncol]
                    o0_ = tmp[0:nk, 0:w].rearrange("p (b c) -> p b c", b=nb)
                kb.emit("dve", lambda e, i0=i0, i1=i1, o0_=o0_: e.scalar_tensor_tensor(o0_, i0, scale, i1, ALU.mult, ALU.add),
                        reads=rkeys + ["relb"], writes=[ktmp])
                src2, rk2, sc = tmp[0:nk, c0:w], [ktmp], 1.0
            else:
                src2, rk2, sc = src, rkeys, scale
            bias = kt.get("bias")
            if bias is not None:
                kb.emit("act", lambda e, PT=PT, src2=src2, bias=bias, nk=nk, c0=c0, sc=sc, w=w: e.activation(
                    PT[0:nk, c0:w], src2, AF.Exp, bias=bias, scale=sc),
                    reads=rk2 + ["kvbias", "bandbias"], writes=[kpt])
            else:
                kb.emit("act", lambda e, PT=PT, src2=src2, nk=nk, c0=c0, sc=sc, w=w: e.activation(
                    PT[0:nk, c0:w], src2, AF.Exp, scale=sc), reads=rk2, writes=[kpt])

        def pv(bi):
            PT, kpt = st8.pop(bi)
            for s_i, ti in enumerate(batches[bi]):
                kt = ktiles[ti]
                nk = kt["nk"]
                c0 = kt.get("c0", 0)
                po = s_i * ncol
                segs = segs_of(c0)
                pieces = []
                half = kt.get("half")
                if half is not None:
                    lo, hi = (0, 64) if half == "lo" else (64, 128)
                    pieces.append((c0, c0 + 64, lo, hi))
                    for (a, b) in segs_of(c0 + 64):
                        pieces.append((a, b, 0, nk))
                else:
                    for (a, b) in segs:
                        pieces.append((a, b, 0, nk))
                half2 = kt.get("half2")
                if half2 is not None:
                    newp = []
                    h2a, h2b, lo2, hi2 = half2
                    for (a, b, lo, hi) in pieces:
                        if b <= h2a or a >= h2b:
                            newp.append((a, b, lo, hi))
                        else:
                            if a < h2a:
                                newp.append((a, h2a, lo, hi))
                            newp.append((max(a, h2a), min(b, h2b), lo2, hi2))
                            if b > h2b:
                                newp.append((h2b, b, lo, hi))
                    pieces = newp
                st_flag = (ti == 0)
                if ti == 0:
                    assert half is None and half2 is None and c0 == 0
                for pi, (a, b, lo, hi) in enumerate(pieces):
                    last_in_bank = False
                    kb.emit("pe", lambda e, a=a, b=b, lo=lo, hi=hi, kt=kt, PT=PT, st_flag=st_flag, lb=last_in_bank, po=po: e.matmul(
                        ps[:, 512 * accb + a:512 * accb + b], kt["V"][lo:hi, :], PT[lo:hi, po + a:po + b], start=st_flag, stop=lb),
                        reads=[kpt] + kt["vkeys"], writes=akeys)

        for bi in range(min(LA, nbt)):
            qk(bi)
        for bi in range(nbt):
            if bi + LA < nbt:
                qk(bi + LA)
            ex(bi)
            pv(bi)
            if hook is not None:
                hook(batches[bi][-1])
        for (a, b) in segs_of(0):
            kb.emit("pe", lambda e, a=a, b=b: e.matmul(ps[:, 512 * accb + a:512 * accb + b], z1[0:1, 0:128], q_ap[0:1, a:b],
                                                       start=False, stop=True),
                    reads=qkeys + ["z1"], writes=akeys)

        def fin():
            rec, krec = recring.next()
            (o0, o1), (d0, d1) = ((0, 64), (64, 128)) if par == 0 else ((64, 128), (0, 64))
            A0 = 512 * accb
            kb.emit("dve", lambda e: e.tensor_scalar(rec[d0:d1, 0:ncol], ps[d0:d1, A0:A0 + ncol], 1e-30, None, ALU.max),
                    reads=akeys, writes=[krec])
            kb.emit("dve", lambda e: e.reciprocal(rec[d0:d1, 0:ncol], rec[d0:d1, 0:ncol]), reads=[krec], writes=[krec])
            kb.emit("dve", lambda e: e.tensor_tensor(out_dst, ps[o0:o1, A0:A0 + ncol], rec[d0:d1, 0:ncol], ALU.mult),
                    reads=akeys + [krec], writes=okeys)
        if defer:
            return fin
        fin()

    pc_ = ExitStack()
    wuk = sb("wuk", [128, 2, 512], BF16, pc_)
    wuv = sb("wuv", [128, 2, 512], BF16, pc_)
    qtr = Ring([sb("qtab%d" % i, [96, 2, 512], F32, pc_) for i in range(2)], "qtab")
    Vp = sb("Vp", [128, 64, 128], BF16, pc_)
    Vs = sb("Vs", [128, 9, 128], BF16, pc_)
    qTs = [sb("qT%d" % i, [96, NQ], BF16, pc_) for i in range(2)]
    wqr = Ring([sb("wq%d" % i, [128, 8, 2, 96], BF16, pc_) for i in range(2)], "wq")
    PTr = Ring([sb("PT%d" % i, [128, 1024], BF16, pc_) for i in range(4)], "PT")
    recr = Ring([sb("rec%d" % i, [128, 1024], F32, pc_) for i in range(1)], "rec")
    rt = Ring([sb("rt%d" % i, [96, 512], F32, pc_) for i in range(4)], "rt")
    ones_sb = sb("ones_sb", [128, 64], BF16, pc_)
    kb.dma("pool", wuk, w_uk.rearrange("(c p) n -> p c n", p=128), writes=["wuk"])
    kb.dma("pool", wuv, w_uv.rearrange("(c p) n -> p c n", p=128), writes=["wuv"])
    kb.dma("pool", ones_sb, ones_d, writes=["ones"])
    pc = {}

    qgroups = [(640 - 512, 1024), (640 - 512 + 1024, 1024), (0, 128)]
    kvb = [kvbias[:, v:v + 1] for v in range(3)]

    def mla_prep(h):
        par = h % 2
        vo = 0 if par == 0 else 64
        on = 64 if par == 0 else 0
        qT = qTs[h % 2]
        wq, kwq = wqr.next()
        th = [(-1, lambda: kb.dma("pool", wq, w_qq[h], writes=[kwq]))]

        def kexp(dst, srcT, ncols, dkey):
            for c in range(2):
                kb.emit("pe", lambda e, c=c: e.matmul(bank(0, ncols, 0, 64), wuk[:, c, h * 64:(h + 1) * 64], srcT(c),
                                                      start=(c == 0), stop=(c == 1)),
                        reads=["wuk"], writes=PK(0))
            kb.emit("dve", lambda e: e.tensor_copy(dst, bank(0, ncols, 0, 64)), reads=PK(0), writes=[dkey])

        def vexp(dst3, srcT, nt_, nk, dkey, ones_dst):
            for j in range(nt_):
                for c in range(2):
                    kb.emit("pe", lambda e, j=j, c=c: e.matmul(bank(1, 64, 0, nk, off=j * 64), srcT(c, j),
                                                              wuv[:, c, h * 64:(h + 1) * 64], start=(c == 0), stop=(c == 1)),
                            reads=["wuv"], writes=PK(1))
            src = bank(1, nt_ * 64, 0, nk).rearrange("p (t d) -> p t d", d=64)
            kb.emit("dve", lambda e: e.tensor_copy(dst3, src), reads=PK(1), writes=[dkey])

        def ones_fill(V3, t0_, nt_, dkey):
            for t in range(t0_, t0_ + nt_):
                kb.emit("pool", lambda e, t=t: e.tensor_copy(V3[:, t, on:on + 64], ones_sb), reads=["ones"], writes=[dkey])

        for g in range(12):
            th.append((4 * g + 3, lambda g=g: kexp(Kp[0:64, g * 512:(g + 1) * 512],
                                                   lambda c: ckvT_kv[:, c, g * 512:(g + 1) * 512], 512, ("Kp", g))))
        for g in range(4):
            th.append((48 + 4 * g + 3, lambda g=g: kexp(Kp[0:64, NKV + g * 512:NKV + (g + 1) * 512],
                                                        lambda c: ckvT_own[:, c, 128 + g * 512:128 + (g + 1) * 512], 512, ("Kp", 12 + g))))
        for g in range(2):
            th.append((-1, lambda g=g: kexp(Ks[0:64, g * 512:(g + 1) * 512], lambda c: ckvT_c[:, c, g * 512:(g + 1) * 512], 512, ("Ks", g))))
        th.append((-1, lambda: kexp(Ks[0:64, 1024:1088], lambda c: ckvT_own[:, c, 2176:2240], 64, ("Ks", 2))))
        for g in range(6):
            def vth(g=g):
                ones_fill(Vp, g * 8, 8, ("Vp", g))
                vexp(Vp[:, g * 8:(g + 1) * 8, vo:vo + 64], lambda c, j: ckvT_kv[:, c, (g * 8 + j) * 128:(g * 8 + j + 1) * 128], 8, 128, ("Vp", g), None)
            th.append((8 * g + 7, vth))
        for g in range(2):
            def vth2(g=g):
                ones_fill(Vp, 48 + g * 8, 8, ("Vp", 6 + g))
                vexp(Vp[:, 48 + g * 8:48 + (g + 1) * 8, vo:vo + 64],
                     lambda c, j: ckvT_own[:, c, 128 + (g * 8 + j) * 128:128 + (g * 8 + j + 1) * 128], 8, 128, ("Vp", 6 + g), None)
            th.append((48 + 8 * g + 7, vth2))

        def vs_th():
            ones_fill(Vs, 0, 9, "Vs")
            vexp(Vs[:, 0:8, vo:vo + 64], lambda c, j: ckvT_c[:, c, j * 128:(j + 1) * 128], 8, 128, "Vs", None)
            vexp(Vs[0:64, 8:9, vo:vo + 64], lambda c, j: ckvT_own[:, c, 2176:2240], 1, 64, "Vs", None)
        th.append((-1, vs_th))

        def qproj(a, n):
            for v in range(2):
                for kc in range(8):
                    kb.emit("pe", lambda e, v=v, kc=kc: e.matmul(bank(v, n, 0, 96), wq[:, kc, v, :],
                                                                 xnT[:, kc, 512 + a:512 + a + n],
                                                                 start=(kc == 0), stop=(kc == 7)),
                            reads=[kwq] + [("xnT", tt_) for tt_ in range((512 + a) // 128, (512 + a + n + 127) // 128)],
                            writes=PK(v))
            kb.emit("dve", lambda e: e.tensor_copy(qT[0:64, a:a + n], bank(0, n, 0, 64)),
                    reads=PK(0), writes=[("qT", h % 2, a // 512)])
            r1, k1 = rt.next()
            r2, k2 = rt.next()
            qtab, kqt = qtr.next()
            kb.dma("sp", qtab[64:96, 0, 0:n], qtab_d[0][:, a:a + n], writes=[kqt])
            kb.dma("sp", qtab[64:96, 1, 0:n], qtab_d[1][:, a:a + n], writes=[kqt])
            kb.emit("dve", lambda e: e.tensor_tensor(r1[64:96, 0:n], bank(0, n, 64, 96), qtab[64:96, 0, 0:n], ALU.mult),
                    reads=PK(0) + [kqt], writes=[k1])
            kb.emit("dve", lambda e: e.tensor_tensor(r2[64:96, 0:n], bank(1, n, 64, 96), qtab[64:96, 1, 0:n], ALU.mult),
                    reads=PK(1) + [kqt], writes=[k2])
            kb.emit("dve", lambda e: e.tensor_tensor(qT[64:96, a:a + n], r1[64:96, 0:n], r2[64:96, 0:n], ALU.add),
                    reads=[k1, k2], writes=[("qT", h % 2, a // 512)])
        for (a, n) in [(0, 512), (512, 512), (1024, 512), (1536, 512), (2048, 192)]:
            th.append((-1, lambda a=a, n=n: qproj(a, n)))
        return th

    def mla_attn(h, nxt):
        par = h % 2
        vo = 0 if par == 0 else 64
        qT = qTs[h % 2]
        qk = [("qT", h % 2, i) for i in range(5)]
        kvt = [dict(K=Kp[0:96, t * 128:(t + 1) * 128], V=Vp[:, t, :], nk=128, bias=kvb[t // 16],
                    kkeys=[("Kp", t // 4)], vkeys=[("Vp", t // 8)]) for t in range(48)]

        def own_t(t, c0, half):
            return dict(K=Kp[0:96, NKV + t * 128:NKV + (t + 1) * 128], V=Vp[:, 48 + t, :], nk=128, bias=None, c0=c0, half=half,
                        kkeys=[("Kp", 12 + t // 4)], vkeys=[("Vp", 6 + t // 8)])
        attend(pc, qT[0:96, 0:128], 96, 128, kvt, MLA_SCALE, ybT[vo:vo + 64, h // 2, 0:128], par, PTr, recr,
               [("ybT", h, 2)], qk)
        kts = [dict(K=Ks[0:96, t * 128:(t + 1) * 128], V=Vs[:, t, :], nk=128, bias=None,
                    kkeys=[("Ks", t // 4)], vkeys=["Vs"]) for t in range(8)]
        kts.append(dict(K=Ks[0:96, 1024:1088], V=Vs[0:64, 8, :], nk=64, bias=None, kkeys=[("Ks", 2)], vkeys=["Vs"]))
        attend(pc, qT[0:96, 2176:2240], 96, 64, kts, MLA_SCALE, ybT[vo:vo + 64, h // 2, 2176:2240], par, PTr, recr,
               [("ybT", h, 3)], qk)
        kt0 = kvt + [own_t(t, 128 * t, "lo") for t in range(8)]
        attend(pc, qT[0:96, 128:1152], 96, 1024, kt0, MLA_SCALE, ybT[vo:vo + 64, h // 2, 128:1152], par, PTr, recr,
               [("ybT", h, 0)], qk)
        kt1 = kvt + [own_t(t, 0, None) for t in range(8)] + [own_t(8 + t, 128 * t, "lo") for t in range(8)]
        pend = sorted(nxt, key=lambda x: x[0])

        def hook(ti):
            k = 0
            while pend and pend[0][0] <= ti and k < 1:
                pend.pop(0)[1]()
                k += 1
        attend(pc, qT[0:96, 1152:2176], 96, 1024, kt1, MLA_SCALE, ybT[vo:vo + 64, h // 2, 1152:2176], par, PTr, recr,
               [("ybT", h, 1)], qk, hook=hook)
        while pend:
            pend.pop(0)[1]()

    for (_, fn_) in mla_prep(0):
        fn_()
    for h_ in range(8):
        mla_attn(h_, mla_prep(h_ + 1) if h_ < 7 else [])
    kb.barrier()
    pc_.close()
    lat.close()
    if stop == "C":
        kb.finalize()
        return nc
    yas = ExitStack()
    yaT = sb("yaT", [128, 4, NQ], BF16, yas)

    pd = ExitStack()
    wab = Ring([sb("wab%d" % i, [128, 8, 2, 64], BF16, pd) for i in range(2)], "wab")
    wkv = sb("wkv", [128, 8, 1024], BF16, pd)
    va_all = sb("va_all", [128, 22, 512], BF16, pd)
    cav_sb = sb("cav_sb", [128, 4, 512], BF16, pd)
    kaTc = sb("kaTc", [64, 8, 512], BF16, pd)
    qaT = sb("qaT", [64, NALL], BF16, pd)
    kaT = sb("kaT", [64, NALL], BF16, pd)
    Vb = sb("Vb", [128, 22, 128], BF16, pd)
    Vbc = sb("Vbc", [128, 4, 128], BF16, pd)
    relb_t = sb("relb_t", [128, 5, 128], F32, pd)
    PTb = Ring([sb("PTb%d" % i, [128, 512], BF16, pd) for i in range(6)], "PTb")
    recb = Ring([sb("recb%d" % i, [128, 128], F32, pd) for i in range(3)], "recb")
    tmpr = Ring([sb("tmpb%d" % i, [128, 512], F32, pd) for i in range(6)], "tmpb")
    stg = Ring([sb("stg%d" % i, [128, 1024], F32, pd) for i in range(2)], "stg")
    ones_b = sb("ones_b", [128, 64], BF16, pd)
    kb.dma("pool", ones_b, ones_d, writes=["ones"])
    cvr = Ring([sb("cv%d" % i, [128, 2048], BF16, pd) for i in range(3)], "cv")

    def convert_ffn(fcs):
        for fc in fcs:
            cv, kcv = cvr.next()
            kb.dma("pool", cv, w_fgu[fc].rearrange("p a b c -> p (a b c)"), writes=[kcv])
            kb.dma("pool", wgu_bf[fc], cv, reads=[kcv])
            cv, kcv = cvr.next()
            kb.dma("pool", cv[:, 0:1024], w_fd[fc * 128:(fc + 1) * 128, :], writes=[kcv])
            kb.dma("pool", wd_bf[fc], cv[:, 0:1024], reads=[kcv])

    kb.dma("pool", wkv, w_kv.rearrange("(kc p) n -> p kc n", p=128), writes=["wkv"])
    kb.dma("pool", cav_sb, cav.rearrange("(t p) n -> p t n", p=128), writes=["cav_sb"])
    pcb = {"tmpring": tmpr}
    kb.emit("pool", lambda e: e.memset(va_all[:, 21, :], 0.0), writes=[("va_all", 21)])
    for ti, (t0, n) in enumerate(tiles_all):
        for hf in range(2):
            for kc in range(8):
                kb.emit("pe", lambda e, hf=hf, kc=kc, t0=t0, n=n: e.matmul(bank(hf, 512, 0, n), xnT[:, kc, t0:t0 + n],
                                                                          wkv[:, kc, hf * 512:(hf + 1) * 512],
                                                                          start=(kc == 0), stop=(kc == 7)),
                        reads=["wkv", ("xnT", ti)], writes=PK(hf))
        kb.emit("act", lambda e, ti=ti, n=n: e.activation(va_all[0:n, ti, :], bank(1, 512, 0, n), AF.Copy),
                reads=PK(1), writes=[("va_all", ti)])
        if 2176 <= t0 < 2688 or t0 == 2688:
            s_, ks_ = stg.next()
            kb.emit("dve", lambda e, s_=s_, n=n: e.tensor_copy(s_[0:n, :], ps[0:n, 0:1024]), reads=PK(0, 1), writes=[ks_])
            if t0 < 2688:
                kb.dma("sp", o_kav[t0 - 2176:t0 - 2176 + n, :], s_[0:n, :], reads=[ks_], final=True)
            else:
                kb.dma("sp", o_sk[448:512, :], s_[0:64, 0:512], reads=[ks_], final=True)
                kb.dma("sp", o_sv[448:512, :], s_[0:64, 512:1024], reads=[ks_], final=True)
    kb.dma("sp", o_sk[0:448, :], cak[64:512, :], final=True)
    kb.dma("sp", o_sv[0:448, :], cav[64:512, :], final=True)
    for t in range(4):
        s_, ks_ = stg.next()
        kb.dma("sp", s_[:, 0:512], cak[t * 128:(t + 1) * 128, :], writes=[ks_])
        for hh in range(8):
            kb.emit("pe", lambda e, hh=hh, s_=s_: e.transpose(ps[0:64, hh * 128:(hh + 1) * 128], s_[:, hh * 64:(hh + 1) * 64], ident),
                    reads=[ks_, "ident"], writes=PK(0, 1))
        src = ps[0:64, 0:1024].rearrange("p (k t) -> p k t", k=8)
        kb.emit("act", lambda e, t=t, src=src: e.activation(kaTc[:, :, t * 128:(t + 1) * 128], src, AF.Copy),
                reads=PK(0, 1), writes=["kaTc"])

    bb0 = bandbias[:, 0:1]

    def band_head(h):
        par = h % 2
        vo = 0 if par == 0 else 64
        on = 64 if par == 0 else 0
        w2, kw2 = wab.next()
        kb.dma("pool", w2, w_qk[h], writes=[kw2])
        kb.dma("sp", relb_t, relb[h], writes=["relb"])
        for (a, n) in [(0, 512), (512, 512), (1024, 512), (1536, 512), (2048, 512), (2560, 192)]:
            for v in range(2):
                for kc in range(8):
                    kb.emit("pe", lambda e, v=v, kc=kc, a=a, n=n: e.matmul(bank(v, n, 0, 64), w2[:, kc, v, :], xnT[:, kc, a:a + n],
                                                                           start=(kc == 0), stop=(kc == 7)),
                            reads=[kw2] + [("xnT", tt_) for tt_ in range(a // 128, (a + n + 127) // 128)], writes=PK(v))
            kb.emit("act", lambda e, a=a, n=n: e.activation(qaT[:, a:a + n], bank(0, n, 0, 64), AF.Copy), reads=PK(0), writes=["qaT"])
            kb.emit("dve", lambda e, a=a, n=n: e.tensor_copy(kaT[:, a:a + n], bank(1, n, 0, 64)), reads=PK(1), writes=["kaT"])
        kb.emit("pool", lambda e: e.tensor_copy(Vb[:, :, vo:vo + 64], va_all[:, :, h * 64:(h + 1) * 64]),
                reads=[("va_all", i) for i in range(22)], writes=["Vb"])
        for t in range(22):
            kb.emit("pool", lambda e, t=t: e.tensor_copy(Vb[:, t, on:on + 64], ones_b), reads=["ones"], writes=["Vb"])
        kb.emit("pool", lambda e: e.tensor_copy(Vbc[:, :, vo:vo + 64], cav_sb[:, :, h * 64:(h + 1) * 64]), reads=["cav_sb"], writes=["Vbc"])
        for t in range(4):
            kb.emit("pool", lambda e, t=t: e.tensor_copy(Vbc[:, t, on:on + 64], ones_b), reads=["ones"], writes=["Vbc"])
        for m in range(17):
            kts = []
            for r in (1, 2, 3, 4, 0):
                t = m + r
                d = dict(K=kaT[:, t * 128:(t + 1) * 128], V=Vb[:, t, :], nk=128, rb=r,
                         bias=(bb0 if t < 5 else None), kkeys=["kaT"], vkeys=["Vb"])
                if r == 0:
                    d["half2"] = (64, 128, 64, 128)
                if r == 4:
                    d["half2"] = (0, 64, 0, 64)
                kts.append(d)
            fin = attend(pcb, qaT[:, 512 + 128 * m:640 + 128 * m], 64, 128, kts, A_SCALE,
                         yaT[vo:vo + 64, h // 2, 128 * m:128 * m + 128], par, PTb, recb, [("yaT", h, m)], ["qaT"], relb_t=relb_t, defer=True)
            if prev[0] is not None:
                prev[0]()
            prev[0] = fin
        kts = [dict(K=kaTc[:, h, t * 128:(t + 1) * 128], V=Vbc[:, t, :], nk=128, rb=t, bias=None, kkeys=["kaTc"], vkeys=["Vbc"])
               for t in range(4)]
        kts.append(dict(K=kaT[:, 2688:2752], V=Vb[0:64, 21, :], nk=64, rb=4, bias=None, kkeys=["kaT"], vkeys=["Vb"]))
        fin = attend(pcb, qaT[:, 2688:2752], 64, 64, kts, A_SCALE, yaT[vo:vo + 64, h // 2, 2176:2240], par, PTb, recb,
                     [("yaT", h, 17)], ["qaT"], relb_t=relb_t, defer=True)
        prev[0]()
        prev[0] = fin
        convert_ffn(range(3 * h, min(NFC, 3 * h + 3)))
    prev = [None]
    for h_ in range(8):
        band_head(h_)
    prev[0]()
    kb.barrier()
    pd.close()
    if stop == "D":
        kb.finalize()
        return nc
    load_nrm(0, 1)

    pe_ = ExitStack()
    wa = sb("wa", [128, 4, 1024], BF16, pe_)
    wb = sb("wb", [128, 4, 1024], BF16, pe_)
    wo = sb("wo", [128, 8, 1024], BF16, pe_)
    wg = sb("wg", [128, 8, 2048], BF16, pe_)
    mT = sb("mT", [128, 8, 512], BF16, pe_)
    sgr = Ring([sb("sg%d" % i, [128, 512], F32, pe_) for i in range(4)], "sg")
    xr2 = Ring([sb("x2r%d" % i, [128, 1024], F32, pe_) for i in range(2)], "x2r")
    mr = Ring([sb("mr%d" % i, [128, 1024], F32, pe_) for i in range(2)], "mr")
    jr2 = Ring([sb("j2r%d" % i, [128, 1024], BF16, pe_) for i in range(2)], "j2r")
    sr2 = Ring([sb("s2r%d" % i, [128, 4], F32, pe_) for i in range(4)], "s2r")
    kb.dma("pool", wa, w_ba.rearrange("(c p) n -> p c n", p=128), writes=["wa"])
    kb.dma("pool", wb, w_bb.rearrange("(c p) n -> p c n", p=128), writes=["wb"])
    kb.dma("pool", wo, w_out.rearrange("(c p) n -> p c n", p=128), writes=["wo"])
    kb.dma("pool", wg, w_g.rearrange("(c p) n -> p c n", p=128), writes=["wg"])
    qblocks = [(0, 512), (512, 512), (1024, 512), (1536, 512), (2048, 192)]
    for (q0, n) in qblocks:
        for fc in range(8):
            for p in range(4):
                kb.emit("pe", lambda e, p=p, fc=fc, q0=q0, n=n: e.matmul(bank(0, n), wa[:, p, fc * 128:(fc + 1) * 128], yaT[:, p, q0:q0 + n],
                                                                         start=(p == 0), stop=(p == 3)), reads=["wa"], writes=PK(0))
            for p in range(4):
                kb.emit("pe", lambda e, p=p, fc=fc, q0=q0, n=n: e.matmul(bank(1, n), wb[:, p, fc * 128:(fc + 1) * 128], ybT[:, p, q0:q0 + n],
                                                                         start=(p == 0), stop=(p == 3)), reads=["wb"], writes=PK(1))
            for g in range(2):
                for kc in range(8):
                    kb.emit("pe", lambda e, g=g, kc=kc, fc=fc, q0=q0, n=n: e.matmul(
                        bank(2 + g, n), wg[:, kc, g * 1024 + fc * 128:g * 1024 + (fc + 1) * 128], xnT[:, kc, 512 + q0:512 + q0 + n],
                        start=(kc == 0), stop=(kc == 7)), reads=["wg"], writes=PK(2 + g))
            sa, ksa = sgr.next()
            sbb, ksb = sgr.next()
            kb.emit("act", lambda e, sa=sa, n=n: e.activation(sa[:, 0:n], bank(2, n), AF.Sigmoid), reads=PK(2), writes=[ksa])
            kb.emit("act", lambda e, sbb=sbb, n=n: e.activation(sbb[:, 0:n], bank(3, n), AF.Sigmoid), reads=PK(3), writes=[ksb])
            kb.emit("dve", lambda e, sa=sa, n=n: e.tensor_tensor(sa[:, 0:n], sa[:, 0:n], bank(0, n), ALU.mult), reads=PK(0) + [ksa], writes=[ksa])
            kb.emit("dve", lambda e, sbb=sbb, n=n: e.tensor_tensor(sbb[:, 0:n], sbb[:, 0:n], bank(1, n), ALU.mult), reads=PK(1) + [ksb], writes=[ksb])
            kb.emit("dve", lambda e, sa=sa, sbb=sbb, fc=fc, n=n: e.tensor_tensor(mT[:, fc, 0:n], sa[:, 0:n], sbb[:, 0:n], ALU.add),
                    reads=[ksa, ksb], writes=[("mT", fc)])
        for tt_ in range((n + 127) // 128):
            nn = min(128, n - tt_ * 128)
            for hf in range(2):
                for fc in range(8):
                    kb.emit("pe", lambda e, hf=hf, fc=fc, tt_=tt_, nn=nn: e.matmul(bank(4 + hf, 512, 0, nn), mT[:, fc, tt_ * 128:tt_ * 128 + nn],
                                                                                 wo[:, fc, hf * 512:(hf + 1) * 512], start=(fc == 0), stop=(fc == 7)),
                            reads=["wo"] + [("mT", f_) for f_ in range(8)], writes=PK(4 + hf))
            xt, kx = xr2.next()
            mx, kmx = mr.next()
            junk, kj = jr2.next()
            st, kst = sr2.next()
            r0 = 512 + q0 + tt_ * 128
            kb.dma("sp", xt[0:nn], x_all[r0:r0 + nn, :], writes=[kx])
            kb.emit("act", lambda e, mx=mx, nn=nn: e.activation(mx[0:nn], ps[0:nn, 2048:3072], AF.Copy), reads=PK(4, 5), writes=[kmx])
            kb.emit("act", lambda e, mx=mx, junk=junk, st=st, nn=nn: e.activation(junk[0:nn], mx[0:nn], AF.Square, accum_out=st[0:nn, 0:1]),
                    reads=[kmx], writes=[kj, kst])
            rstd_from_ss(st, nn, 1.0 / 1024, [kst])
            kb.emit("dve", lambda e, mx=mx, st=st, nn=nn: e.scalar_tensor_tensor(mx[0:nn], mx[0:nn], st[0:nn, 2:3], nrm_rep[0:nn, NSLOT[1], :], ALU.mult, ALU.mult),
                    reads=[kmx, kst, "nrm_rep"], writes=[kmx])
            kb.emit("dve", lambda e, mx=mx, xt=xt, nn=nn: e.tensor_tensor(mx[0:nn], mx[0:nn], xt[0:nn], ALU.add), reads=[kmx, kx], writes=[kmx])
            kb.dma("sp", x1_d[q0 + tt_ * 128:q0 + tt_ * 128 + nn, :], mx[0:nn], reads=[kmx], writes=[("x1d", (q0 + tt_ * 128) // 64)])
    kb.barrier()
    pe_.close()
    yas.close()
    xs.close()
    if stop == "E":
        kb.finalize()
        return nc
    load_nrm(0, 2)
    load_nrm(1, 3)

    pf = ExitStack()
    NB = len(qblocks)
    x1ra = Ring([sb("x1ra%d" % i, [128, 1024], F32, pf) for i in range(3)], "x1ra")
    x1rb = Ring([sb("x1rb%d" % i, [128, 1024], F32, pf) for i in range(2)], "x1rb")
    xn2Ts = [sb("xn2T%d" % k, [128, 8, 512], BF16, pf) for k in range(2)]
    hTs = [sb("hT%d" % k, [128, NFC, 512], BF16, pf) for k in range(2)]
    mxs_ = [[sb("mx%d_%d" % (k, i), [128, 1024], F32, pf) for i in range(4)] for k in range(2)]
    gtail = sb("gtail", [128, 2, NFC], F32, pf)
    stail = sb("stail", [128, 2, NFC], F32, pf)
    cw = sb("cw", [128, 4, NFC], F32, pf)
    wgr = Ring([sb("wfg%d" % i, [128, 8, 2, 128], BF16, pf) for i in range(4)], "wfg")
    wdr = Ring([sb("wfd%d" % i, [128, 512], BF16, pf) for i in range(8)], "wfd")
    gbr = Ring([sb("gb%d" % i, [128, 516], F32, pf) for i in range(6)], "gb")
    cbr = Ring([sb("cb%d" % i, [128, 512], F32, pf) for i in range(4)], "cb")
    jr3 = Ring([sb("j3r%d" % i, [128, 1024], BF16, pf) for i in range(2)], "j3r")
    sr3 = Ring([sb("s3r%d" % i, [128, 4], F32, pf) for i in range(4)], "s3r")
    sr3b = Ring([sb("s3rb%d" % i, [128, 4], F32, pf) for i in range(4)], "s3rb")
    prr = Ring([sb("prr%d" % i, [128, 1024], F32, pf) for i in range(2)], "prr")
    kb.dma("sp", cw, convw, writes=["cw"])
    kb.emit("pool", lambda e: e.memset(gtail, 0.0), writes=["gtail"])
    kb.dma("sp", stail, sconv, writes=["stail"])

    def tiles_of(b):
        q0, n = qblocks[b]
        return [(tt_, q0 + tt_ * 128, min(128, n - tt_ * 128)) for tt_ in range((n + 127) // 128)]

    def pipe_rounds(gens, depth):
        active = []
        it = iter(gens)
        done = False
        while True:
            if not done and len(active) < depth:
                try:
                    active.append(next(it))
                except StopIteration:
                    done = True
            if not active:
                if done:
                    return
                continue
            for g in list(active):
                try:
                    next(g)
                except StopIteration:
                    active.remove(g)
            yield

    def pre_tile(b, tt_, r0, nn):
        k = b % 2
        xt, kx = x1ra.next()
        kb.dma("sp", xt[0:nn], x1_d[r0:r0 + nn, :], writes=[kx])
        junk, kj = jr3.next()
        st, kst = sr3.next()
        mx, kmx = prr.next()
        kb.emit("act", lambda e: e.activation(junk[0:nn], xt[0:nn], AF.Square, accum_out=st[0:nn, 0:1]),
                reads=[kx], writes=[kj, kst])
        yield
        yield from rstd_gen(st, nn, 1.0 / 1024, [kst])
        kb.emit("dve", lambda e: e.scalar_tensor_tensor(mx[0:nn], xt[0:nn], st[0:nn, 2:3], nrm_rep[0:nn, NSLOT[2], :], ALU.mult, ALU.mult),
                reads=[kx, kst, "nrm_rep"], writes=[kmx])
        yield
        for kc in range(8):
            kb.emit("pe", lambda e, kc=kc: e.transpose(bank(0, nn, off=kc * 128) if kc < 4 else bank(1, nn, off=(kc - 4) * 128),
                                                      mx[0:nn, kc * 128:(kc + 1) * 128], ident[0:nn, 0:nn]),
                    reads=[kmx, "ident"], writes=PK(0, 1))
        yield
        src = ps[:, 0:1024].rearrange("p (k t) -> p k t", k=8)[:, :, 0:nn]
        kb.emit("act", lambda e: e.activation(xn2Ts[k][:, :, tt_ * 128:tt_ * 128 + nn], src, AF.Copy),
                reads=PK(0, 1), writes=[("xn2T", k, tt_)])
        yield

    def ffn_pre(b):
        return pipe_rounds([pre_tile(b, tt_, r0, nn) for (tt_, r0, nn) in tiles_of(b)], 2)

    cbs = {}

    def stage_a(b, fc):
        q0, n = qblocks[b]
        k = b % 2
        xn2T = xn2Ts[k]
        xk = [("xn2T", k, t_[0]) for t_ in tiles_of(b)]
        segs = [(0, n, gtail, "gtail")] if b < NB - 1 else [(0, 128, gtail, "gtail"), (128, 64, stail, "stail")]
        wf, kwf = wgr.next()
        kb.dma("sp", wf.rearrange("p a b c -> p (a b c)"), wgu_bf[fc], writes=[kwf])
        pb = 2 + 2 * (fc % 2)
        for v in range(2):
            for kc in range(8):
                kb.emit("pe", lambda e, v=v, kc=kc: e.matmul(bank(pb + v, n), wf[:, kc, v, :], xn2T[:, kc, 0:n],
                                                             start=(kc == 0), stop=(kc == 7)),
                        reads=[kwf] + xk, writes=PK(pb + v))
        cb_, kcb = cbr.next()
        cbs[(b, fc)] = (cb_, kcb, pb)
        for (c0, ns, tl, tlk) in segs:
            gb_, kgb = gbr.next()
            kb.emit("act", lambda e, gb_=gb_, c0=c0, ns=ns: e.activation(gb_[:, 2:2 + ns], bank(pb, ns, off=c0), AF.Copy), reads=PK(pb), writes=[kgb])
            kb.emit("dve", lambda e, gb_=gb_, tl=tl: e.tensor_copy(gb_[:, 0:2], tl[:, :, fc]), reads=[tlk], writes=[(kgb, "t")])
            kb.emit("pool", lambda e, gb_=gb_, c0=c0, ns=ns: e.tensor_scalar(cb_[:, c0:c0 + ns], gb_[:, 2:2 + ns], cw[:, 2, fc:fc + 1], cw[:, 3, fc:fc + 1], ALU.mult, ALU.add),
                    reads=[kgb, "cw"], writes=[kcb])
            kb.emit("dve", lambda e, gb_=gb_, c0=c0, ns=ns: e.scalar_tensor_tensor(cb_[:, c0:c0 + ns], gb_[:, 1:1 + ns], cw[:, 1, fc:fc + 1], cb_[:, c0:c0 + ns], ALU.mult, ALU.add),
                    reads=[kgb, (kgb, "t"), "cw", kcb], writes=[kcb])
            kb.emit("dve", lambda e, gb_=gb_, c0=c0, ns=ns: e.scalar_tensor_tensor(cb_[:, c0:c0 + ns], gb_[:, 0:ns], cw[:, 0, fc:fc + 1], cb_[:, c0:c0 + ns], ALU.mult, ALU.add),
                    reads=[kgb, (kgb, "t"), "cw", kcb], writes=[kcb])
            kb.emit("dve", lambda e, gb_=gb_, ns=ns, tl=tl: e.tensor_copy(tl[:, :, fc], gb_[:, ns:ns + 2]), reads=[kgb, (kgb, "t")], writes=[tlk])

    def stage_b(b, fc):
        q0, n = qblocks[b]
        hT = hTs[b % 2]
        cb_, kcb, pb = cbs.pop((b, fc))
        kb.emit("act", lambda e: e.activation(cb_[:, 0:n], cb_[:, 0:n], AF.Gelu_apprx_tanh), reads=[kcb], writes=[kcb])
        kb.emit("dve", lambda e: e.tensor_tensor(hT[:, fc, 0:n], cb_[:, 0:n], bank(pb + 1, n), ALU.mult),
                reads=[kcb] + PK(pb + 1), writes=[("hT", b % 2, fc)])

    def down_pieces(b):
        hT = hTs[b % 2]
        tl = tiles_of(b)
        for ps_ in range(4):
            hf, pair = ps_ // 2, ps_ % 2
            tls = [t_ for t_ in tl if t_[0] // 2 == pair]
            if not tls:
                continue
            for fc in range(NFC):
                wd_, kwd = wdr.next()
                kb.dma("sp", wd_, wd_bf[fc][:, hf * 512:(hf + 1) * 512], writes=[kwd])
                for (tt_, r0, nn) in tls:
                    kb.emit("pe", lambda e, fc=fc, tt_=tt_, nn=nn, wd_=wd_: e.matmul(bank(6 + tt_ % 2, 512, 0, nn), hT[:, fc, tt_ * 128:tt_ * 128 + nn],
                                                                                 wd_, start=(fc == 0), stop=(fc == NFC - 1)),
                            reads=[kwd, ("hT", b % 2, fc)], writes=PK(6 + tt_ % 2))
                if fc == NFC - 1:
                    for (tt_, r0, nn) in tls:
                        mx = mxs_[b % 2][tt_]
                        kb.emit("act", lambda e, mx=mx, nn=nn, tt_=tt_, hf=hf: e.activation(mx[0:nn, hf * 512:(hf + 1) * 512], bank(6 + tt_ % 2, 512, 0, nn), AF.Copy),
                                reads=PK(6 + tt_ % 2), writes=[("mx", b % 2, tt_)])
                yield

    def post_tile(b, tt_, r0, nn):
        k = b % 2
        mx, kmx = mxs_[k][tt_], ("mx", k, tt_)
        junk, kj = jr3.next()
        st, kst = sr3b.next()
        xt, kx = x1rb.next()
        kb.dma("sp", xt[0:nn], x1_d[r0:r0 + nn, :], writes=[kx])
        kb.emit("act", lambda e: e.activation(junk[0:nn], mx[0:nn], AF.Square, accum_out=st[0:nn, 0:1]),
                reads=[kmx], writes=[kj, kst])
        yield
        yield from rstd_gen(st, nn, 1.0 / 1024, [kst])
        kb.emit("dve", lambda e: e.scalar_tensor_tensor(mx[0:nn], mx[0:nn], st[0:nn, 2:3], nrm_rep[0:nn, NSLOT[3], :], ALU.mult, ALU.mult),
                reads=[kmx, kst, "nrm_rep"], writes=[kmx])
        yield
        kb.emit("dve", lambda e: e.tensor_tensor(mx[0:nn], mx[0:nn], xt[0:nn], ALU.add), reads=[kmx, kx], writes=[kmx])
        yield
        if 128 <= r0 < 2176:
            kb.dma("sp", y_own[r0 - 128:r0 - 128 + nn, :], mx[0:nn], reads=[kmx], final=True)
        elif r0 >= 2176:
            kb.dma("sp", y_smp[0:nn, :], mx[0:nn], reads=[kmx], final=True)
        yield

    def ffn_post(b):
        return pipe_rounds([post_tile(b, tt_, r0, nn) for (tt_, r0, nn) in tiles_of(b)], 2)

    def drain(g, k):
        for _ in range(k):
            if next(g, "end") == "end":
                return True
        return False

    def flush(g):
        if g is not None:
            for _ in g:
                pass

    flush(ffn_pre(0))
    dgen = None
    pregen = None
    postgen = None
    for b in range(NB):
        stage_a(b, 0)
        for fc in range(NFC):
            if fc + 1 < NFC:
                stage_a(b, fc + 1)
            stage_b(b, fc)
            if fc == 2 and b + 1 < NB:
                pregen = ffn_pre(b + 1)
            if pregen is not None and drain(pregen, 2):
                pregen = None
            if postgen is not None and drain(postgen, 1):
                postgen = None
            if dgen is not None and drain(dgen, 4):
                dgen = None
        flush(pregen)
        pregen = None
        if dgen is not None:
            flush(dgen)
        flush(postgen)
        postgen = ffn_post(b - 1) if b >= 1 else None
        if b == NB - 1:
            kb.dma("sp", o_conv, gtail, reads=["gtail"], final=True)
            kb.dma("sp", o_sconv, stail, reads=["stail"], final=True)
        dgen = down_pieces(b)
    flush(postgen)
    flush(dgen)
    flush(ffn_post(NB - 1))
    pf.close()
    kb.finalize()
    es.close()
    return nc


_NC = None


def kernel(x_prompt, x_sample, cache_a_k, cache_a_v, cache_mla_ckv, cache_mla_krope, state_ffn_conv,
           norm_mix_pre, norm_mix_post, w_in, rel_bias_table, kv_norm, w_uk, w_uv, w_branch_a, w_branch_b,
           w_out, norm_ffn_pre, norm_ffn_post, w_ffn_gate, w_ffn_up, conv_w, conv_b, w_ffn_down):
    global _NC
    f = lambda a: np.ascontiguousarray(np.asarray(a, dtype=np.float32))
    x_prompt, x_sample = f(x_prompt), f(x_sample)
    W = f(w_in)[0]
    qa, ka, va, qn, qr, ck, kr_, ga, gb = 0, 512, 1024, 1536, 2048, 2304, 2560, 2592, 3616
    wq = np.zeros((1024, 8, 96), np.float32)
    wq2 = np.zeros((1024, 8, 96), np.float32)
    for h in range(8):
        wq[:, h, 0:64] = W[:, qn + 64 * h:qn + 64 * (h + 1)]
        wq[:, h, 64:96] = W[:, qr + 32 * h:qr + 32 * (h + 1)]
        wq2[:, h, 0:64] = W[:, qn + 64 * h:qn + 64 * (h + 1)]
        wq2[:, h, 64:80] = W[:, qr + 32 * h + 16:qr + 32 * h + 32]
        wq2[:, h, 80:96] = W[:, qr + 32 * h:qr + 32 * h + 16]
    half = 16
    inv = (10000.0 ** (-np.arange(half, dtype=np.float32) / half)).astype(np.float32)

    def cs_tab(pos):
        ang = pos.astype(np.float32)[:, None] * inv[None, :]
        return np.cos(ang).astype(np.float32), np.sin(ang).astype(np.float32)

    tbl = f(rel_bias_table)[0]
    kt_, ki_, qi_ = np.meshgrid(np.arange(5), np.arange(128), np.arange(128), indexing="ij")
    didx = np.clip(512 + qi_ - 128 * kt_ - ki_, -256, 256) + 256
    relb = np.ascontiguousarray(tbl[:, didx].transpose(0, 2, 1, 3))
    nrm = np.zeros((5, 1024), np.float32)
    nrm[0], nrm[1], nrm[2], nrm[3] = f(norm_mix_pre)[0], f(norm_mix_post)[0], f(norm_ffn_pre)[0], f(norm_ffn_post)[0]
    nrm[4, :256] = f(kv_norm)[0]
    nrm_b = np.ascontiguousarray(np.broadcast_to(nrm[None], (128, 5, 1024)))
    cwl = np.ascontiguousarray(np.concatenate([f(conv_w)[0], f(conv_b)], axis=0).reshape(4, NFC, 128).transpose(2, 0, 1))
    common = dict(
        ident=np.eye(128, dtype=np.float32), ones=np.ones((128, 64), np.float32),
        w_c=np.ascontiguousarray(W[:, ck:ga]),
        w_qq=np.ascontiguousarray(np.stack([wq, wq2], axis=2).reshape(8, 128, 8, 2, 96).transpose(2, 1, 0, 3, 4)),
        w_qk=np.ascontiguousarray(np.stack([W[:, qa:ka].reshape(1024, 8, 64), W[:, ka:va].reshape(1024, 8, 64)], axis=2)
                                  .reshape(8, 128, 8, 2, 64).transpose(2, 1, 0, 3, 4)),
        w_kv=np.ascontiguousarray(W[:, ka:qn]), w_g=np.ascontiguousarray(W[:, ga:]),
        relb=relb, w_uk=f(w_uk)[0].reshape(256, 512), w_uv=f(w_uv)[0].reshape(256, 512),
        w_ba=f(w_branch_a)[0], w_bb=f(w_branch_b)[0], w_out=f(w_out)[0],
        w_fgu=np.ascontiguousarray(np.stack([f(w_ffn_gate)[0].reshape(8, 128, NFC, 128), f(w_ffn_up)[0].reshape(8, 128, NFC, 128)], axis=3)
                                   .transpose(2, 1, 0, 3, 4)),
        w_fd=f(w_ffn_down)[0],
        convw=cwl, nrm=nrm_b,
    )
    in_maps = []
    for c in range(8):
        b, j = c // 4, c % 4
        s0 = 2048 * j
        x_all = np.zeros((NALL, 1024), np.float32)
        lo = s0 - 640
        if lo >= 0:
            x_all[0:640] = x_prompt[b, lo:s0]
        x_all[640:2688] = x_prompt[b, s0:s0 + 2048]
        x_all[2688:] = x_sample[c]
        pos_all = np.concatenate([np.arange(lo, s0 + 2048), 1024 + np.arange(64)])
        cc, ss = cs_tab(pos_all)
        cs_all = np.concatenate([cc, ss], axis=1)
        blocks = [v if v < j else 0 for v in range(3)]
        x_kv = np.concatenate([x_prompt[b, 2048 * v:2048 * (v + 1)] for v in blocks], axis=0)
        pos_kv = np.concatenate([np.arange(2048 * v, 2048 * (v + 1)) for v in blocks])
        ck_, sk_ = cs_tab(pos_kv)
        cs_kv = np.concatenate([ck_, sk_], axis=1)
        posq = pos_all[512:]
        cq, sq = cs_tab(posq)
        qtab = np.zeros((2, 32, NQ), np.float32)
        qtab[0, 0:16], qtab[0, 16:32] = cq.T, cq.T
        qtab[1, 0:16], qtab[1, 16:32] = -sq.T, sq.T
        kvbias = np.zeros((128, 4), np.float32)
        for v in range(3):
            if v >= j:
                kvbias[:, v] = NEG
        bandbias = np.zeros((128, 2), np.float32)
        if j == 0:
            bandbias[:, 0] = NEG
        m = dict(common)
        m.update(x_all=x_all, x_kv=np.ascontiguousarray(x_kv), cs_all=cs_all, cs_kv=cs_kv, qtab=qtab,
                 kvbias=kvbias, bandbias=bandbias,
                 cak=f(cache_a_k)[0, c].reshape(512, 512), cav=f(cache_a_v)[0, c].reshape(512, 512),
                 cckv=f(cache_mla_ckv)[0, c], ckr=f(cache_mla_krope)[0, c], sconv=np.ascontiguousarray(f(state_ffn_conv)[0, c].reshape(2, NFC, 128).transpose(2, 0, 1)))
        in_maps.append(m)
    if _NC is None:
        _NC = build()
    res = run_bass_kernel_spmd(_NC, in_maps, core_ids=list(range(8))).results
    y_p = np.zeros((2, 8192, 1024), np.float32)
    ckv_p = np.zeros((1, 2, 8192, 256), np.float32)
    kr_p = np.zeros((1, 2, 8192, 32), np.float32)
    for c in range(8):
        b, j = c // 4, c % 4
        y_p[b, 2048 * j:2048 * (j + 1)] = res[c]["y_own"]
        ckv_p[0, b, 2048 * j:2048 * (j + 1)] = res[c]["o_ckv"]
        kr_p[0, b, 2048 * j:2048 * (j + 1)] = res[c]["o_kr"]
    y_s = np.stack([res[c]["y_smp"] for c in range(8)])
    ak_p = np.stack([res[4 * b + 3]["o_kav"][:, 0:512].reshape(512, 8, 64) for b in range(2)])[None]
    av_p = np.stack([res[4 * b + 3]["o_kav"][:, 512:1024].reshape(512, 8, 64) for b in range(2)])[None]
    conv_p = np.stack([res[4 * b + 3]["o_conv"].transpose(1, 2, 0).reshape(2, DFF) for b in range(2)])[None]
    sk = np.stack([res[c]["o_sk"].reshape(512, 8, 64) for c in range(8)])[None]
    sv = np.stack([res[c]["o_sv"].reshape(512, 8, 64) for c in range(8)])[None]
    sckv = np.stack([res[c]["o_sckv"] for c in range(8)])[None]
    skr = np.stack([res[c]["o_skr"] for c in range(8)])[None]
    sconv = np.stack([res[c]["o_sconv"].transpose(1, 2, 0).reshape(2, DFF) for c in range(8)])[None]
    return (y_p, y_s, ak_p, av_p, ckv_p, kr_p, conv_p, sk, sv, sckv, skr, sconv)
```
